# Optimizing a Trainium2 kernel written in Bass

```python
import math
import jax, jax.numpy as jnp
from jax import lax
import numpy as np

D_MODEL = 1024
BATCH = 4
SEQ = 8192
DEPTH = 2

GRID_W = 64
CTX_LEN = 256
EPS = 1e-6
N_MOD = 9
D_FF = 2816
D_CONV = 512
CONV_W = 3
MLA_HEADS = 8
MLA_NOPE = 64
MLA_ROPE = 32
MLA_V = 64
MLA_Q_RANK = 256
MLA_KV_RANK = 128
MLA_SCALE = (MLA_NOPE + MLA_ROPE) ** -0.5
ROPE_BASE = 10000.0
Q_BLOCK = 128
EVEN_Q_START = 3 * D_CONV
EVEN_KV_START = EVEN_Q_START + MLA_Q_RANK
D_EVEN_IN = EVEN_KV_START + MLA_KV_RANK + MLA_ROPE
D_EVEN_OUT = D_CONV + MLA_HEADS * MLA_V
HGRN_HEADS = 8
HGRN_EXPAND = 128
D_HGRN = HGRN_HEADS * HGRN_EXPAND
HGRN_SCALE = HGRN_EXPAND ** -0.5
HGRN_CHUNK = 64
D_ODD_IN = 5 * D_HGRN
N_EVEN = (DEPTH + 1) // 2
N_ODD = DEPTH // 2

kernel_name = "hybrid_conv_mla_hgrn2_macaron_dit"


def rmsnorm(x, g):
    xf = x.astype(jnp.float32)
    y = xf * lax.rsqrt(jnp.mean(xf * xf, axis=-1, keepdims=True) + EPS)
    return (y * g.astype(jnp.float32)).astype(x.dtype)


def modulate(h, shift, scale):
    return h * (1 + scale) + shift


def swiglu(u, w1, w3, w2):
    return (jax.nn.silu(u @ w1) * (u @ w3)) @ w2


def ada_mod(cond, w, b):
    m = jax.nn.silu(cond) @ w + b
    return m.reshape(m.shape[:-1] + (N_MOD, D_MODEL))


def ffn_half(h, m, j, g, w1, w3, w2):
    u = modulate(rmsnorm(h, g), m[:, :, 3 * j], m[:, :, 3 * j + 1])
    return h + 0.5 * m[:, :, 3 * j + 2] * swiglu(u, w1, w3, w2)


def axial_rope_tables(rows):
    row = jnp.repeat(jnp.arange(rows, dtype=jnp.int32), GRID_W).astype(jnp.float32)
    col = jnp.tile(jnp.arange(GRID_W, dtype=jnp.int32), rows).astype(jnp.float32)
    n_freq = MLA_ROPE // 4
    inv = ROPE_BASE ** (-jnp.arange(n_freq, dtype=jnp.float32) / n_freq)
    ang = jnp.stack([row[:, None] * inv, col[:, None] * inv], axis=1)
    return jnp.cos(ang), jnp.sin(ang)


def apply_axial_rope(x, cos, sin):
    shp = x.shape
    xr = x.reshape(shp[:-1] + (2, 2, MLA_ROPE // 4))
    x1, x2 = xr[..., 0, :], xr[..., 1, :]
    c = cos[:, None].astype(x.dtype)
    s = sin[:, None].astype(x.dtype)
    return jnp.stack([x1 * c - x2 * s, x1 * s + x2 * c], axis=-2).reshape(shp)


def short_conv_centred(u, w):
    n = u.shape[1]
    up = jnp.pad(u, ((0, 0), (1, 1), (0, 0)))
    return up[:, :n] * w[0] + up[:, 1:n + 1] * w[1] + up[:, 2:] * w[2]


def short_conv_mixer(p, conv_w):
    gate_b, gate_c, v = jnp.split(p, 3, axis=-1)
    return gate_b * short_conv_centred(gate_c * v, conv_w)


def mla_queries(c_q, q_norm_g, w_uq):
    bsz, n = c_q.shape[:2]
    q = (rmsnorm(c_q, q_norm_g) @ w_uq).reshape(bsz, n, MLA_HEADS, MLA_NOPE + MLA_ROPE)
    return q[..., :MLA_NOPE], q[..., MLA_NOPE:]


def mla_keys_values(p_kv, kv_norm_g, w_ukv):
    bsz, n = p_kv.shape[:2]
    c_kv, k_r = p_kv[..., :MLA_KV_RANK], p_kv[..., MLA_KV_RANK:]
    kv = (rmsnorm(c_kv, kv_norm_g) @ w_ukv).reshape(bsz, n, MLA_HEADS, MLA_NOPE + MLA_V)
    return kv[..., :MLA_NOPE], k_r[:, :, None, :], kv[..., MLA_NOPE:]


def mla_attend(q_n, q_r, k_n, k_r, v):
    s = jnp.einsum('bqhd,bkhd->bhqk', q_n, k_n) + jnp.einsum('bqhd,bkd->bhqk', q_r, k_r)
    p = jax.nn.softmax(s.astype(jnp.float32) * MLA_SCALE, axis=-1).astype(v.dtype)
    return jnp.einsum('bhqk,bkhd->bqhd', p, v)


def mla_blocked(q_n, q_r, k_n, k_r, v):
    bsz, n, H, _ = q_n.shape
    nb = n // Q_BLOCK

    def to_blocks(a):
        return jnp.moveaxis(a.reshape(bsz, nb, Q_BLOCK, H, a.shape[-1]), 1, 0)

    o = lax.map(lambda qs: mla_attend(qs[0], qs[1], k_n, k_r, v), (to_blocks(q_n), to_blocks(q_r)))
    return jnp.moveaxis(o, 0, 1).reshape(bsz, n, H * v.shape[-1])


def even_mixer(h_x, h_c, w_in, conv_w, q_norm_g, w_uq, kv_norm_g, w_ukv, w_out, cos, sin, need_ctx_out):
    bsz, n_ctx = h_c.shape[:2]
    p_x = h_x @ w_in
    a_x = short_conv_mixer(p_x[..., :EVEN_Q_START], conv_w)
    qn_x, qr_x = mla_queries(p_x[..., EVEN_Q_START:EVEN_KV_START], q_norm_g, w_uq)
    kn_x, kr_x, v_x = mla_keys_values(p_x[..., EVEN_KV_START:], kv_norm_g, w_ukv)
    qr_x = apply_axial_rope(qr_x, cos, sin)
    kr_x = apply_axial_rope(kr_x, cos, sin)
    kn_c, kr_c, v_c = mla_keys_values(h_c @ w_in[:, EVEN_KV_START:], kv_norm_g, w_ukv)
    k_n = jnp.concatenate([kn_c, kn_x], axis=1)
    k_r = jnp.concatenate([kr_c, kr_x], axis=1)[:, :, 0]
    v = jnp.concatenate([v_c, v_x], axis=1)
    b_x = mla_blocked(qn_x, qr_x, k_n, k_r, v)
    y_x = jnp.concatenate([a_x, b_x], axis=-1) @ w_out
    if not need_ctx_out:
        return y_x, None
    p_c = h_c @ w_in[:, :EVEN_KV_START]
    a_c = short_conv_mixer(p_c[..., :EVEN_Q_START], conv_w)
    qn_c, qr_c = mla_queries(p_c[..., EVEN_Q_START:], q_norm_g, w_uq)
    b_c = mla_attend(qn_c, qr_c, kn_c, kr_c[:, :, 0], v_c).reshape(bsz, n_ctx, MLA_HEADS * MLA_V)
    y_c = jnp.concatenate([a_c, b_c], axis=-1) @ w_out
    return y_x, y_c


def gla_chunked(q, k, v, log_f, s0):
    bsz, n, H, _ = q.shape
    nc = n // HGRN_CHUNK

    def chunks(a):
        return a.reshape(bsz, nc, HGRN_CHUNK, H, a.shape[-1])

    q, k, v, log_f = chunks(q), chunks(k), chunks(v), chunks(log_f)
    b = jnp.cumsum(log_f, axis=2)
    b_last = b[:, :, -1:]
    q_t = q * jnp.exp(b)
    k_t = k * jnp.exp(-b)
    k_end = k * jnp.exp(b_last - b)
    mask = jnp.tril(jnp.ones((HGRN_CHUNK, HGRN_CHUNK), dtype=bool))
    att = jnp.where(mask, jnp.einsum('bnchk,bnshk->bnhcs', q_t, k_t), 0.0)
    o_intra = jnp.einsum('bnhcs,bnshv->bnchv', att, v)
    ds = jnp.einsum('bnchk,bnchv->nbhkv', k_end, v)
    decay = jnp.moveaxis(jnp.exp(b_last[:, :, 0]), 1, 0)

    def step(s, inp):
        a, d = inp
        return a[..., None] * s + d, s

    s_final, s_prev = lax.scan(step, s0, (decay, ds))
    o_inter = jnp.einsum('bnchk,nbhkv->bnchv', q_t, s_prev)
    return (o_intra + o_inter).reshape(bsz, n, H, v.shape[-1]), s_final


def context_final_state(k, v, log_f):
    b = jnp.cumsum(log_f, axis=1)
    return jnp.einsum('bthk,bthv->bhkv', k * jnp.exp(b[:, -1:] - b), v)


def odd_mixer(h_x, h_c, w_in, lb, g_norm_g, w_out, need_ctx_out):
    f32 = jnp.float32

    def heads(a):
        return a.reshape(a.shape[:2] + (HGRN_HEADS, HGRN_EXPAND)).astype(f32)

    lb = lb.reshape(2, HGRN_HEADS, HGRN_EXPAND)

    def gates(z, lb_dir):
        f = lb_dir + (1 - lb_dir) * jax.nn.sigmoid(z)
        return jnp.log(f), 1 - f

    def flip(a):
        return jnp.flip(a, axis=1)

    def readout(o, g, dtype):
        y = rmsnorm(o, g_norm_g).astype(dtype) * jax.nn.silu(g.reshape(o.shape).astype(dtype))
        return y.reshape(y.shape[:2] + (D_HGRN,)) @ w_out

    q_x, i_x, zf_x, zb_x, g_x = jnp.split(h_x @ w_in, 5, axis=-1)
    q_x, i_x = heads(q_x) * HGRN_SCALE, heads(i_x)
    lf_xf, k_xf = gates(heads(zf_x), lb[0])
    lf_xb, k_xb = gates(heads(zb_x), lb[1])
    i_c, zf_c, zb_c = jnp.split(h_c @ w_in[:, D_HGRN:4 * D_HGRN], 3, axis=-1)
    i_c = heads(i_c)
    lf_cf, k_cf = gates(heads(zf_c), lb[0])
    lf_cb, k_cb = gates(heads(zb_c), lb[1])
    y_c = None
    if need_ctx_out:
        q_c = heads(h_c @ w_in[:, :D_HGRN]) * HGRN_SCALE
        g_c = h_c @ w_in[:, 4 * D_HGRN:]
        zeros = jnp.zeros((h_c.shape[0], HGRN_HEADS, HGRN_EXPAND, HGRN_EXPAND), f32)
        o_cf, s_cf = gla_chunked(q_c, k_cf, i_c, lf_cf, zeros)
        o_cb, s_cb = gla_chunked(flip(q_c), flip(k_cb), flip(i_c), flip(lf_cb), zeros)
        y_c = readout(o_cf + flip(o_cb), g_c, h_c.dtype)
    else:
        s_cf = context_final_state(k_cf, i_c, lf_cf)
        s_cb = context_final_state(flip(k_cb), flip(i_c), flip(lf_cb))
    o_xf, _ = gla_chunked(q_x, k_xf, i_x, lf_xf, s_cf)
    o_xb, _ = gla_chunked(flip(q_x), flip(k_xb), flip(i_x), flip(lf_xb), s_cb)
    y_x = readout(o_xf + flip(o_xb), g_x, h_x.dtype)
    return y_x, y_c


def setup_inputs(seed: int = 0) -> dict:
    key = jax.random.key(seed)
    ks = jax.random.split(key, 22)

    def nrm(k, shape, scale=1.0):
        return jax.random.normal(k, shape, jnp.float32) * scale

    def w(k, shape, fan_in, scale=1.0):
        return nrm(k, shape, scale * fan_in ** -0.5)

    def gain(k, shape):
        return 1.0 + nrm(k, shape, 0.02)

    return {
        "x": nrm(ks[0], (BATCH, SEQ, D_MODEL)),
        "c": nrm(ks[1], (BATCH, D_MODEL)),
        "ctx": nrm(ks[2], (BATCH, CTX_LEN, D_MODEL)),
        "c_ctx": nrm(ks[3], (D_MODEL,)),
        "ada_w": w(ks[4], (DEPTH, D_MODEL, N_MOD * D_MODEL), D_MODEL, 0.5),
        "ada_b": nrm(ks[5], (DEPTH, N_MOD * D_MODEL), 0.02),
        "norm_g": gain(ks[6], (DEPTH, 3, D_MODEL)),
        "ffn_w1": w(ks[7], (DEPTH, 2, D_MODEL, D_FF), D_MODEL),
        "ffn_w3": w(ks[8], (DEPTH, 2, D_MODEL, D_FF), D_MODEL),
        "ffn_w2": w(ks[9], (DEPTH, 2, D_FF, D_MODEL), D_FF),
        "even_w_in": w(ks[10], (N_EVEN, D_MODEL, D_EVEN_IN), D_MODEL),
        "even_conv_w": w(ks[11], (N_EVEN, CONV_W, D_CONV), CONV_W),
        "mla_q_norm_g": gain(ks[12], (N_EVEN, MLA_Q_RANK)),
        "mla_w_uq": w(ks[13], (N_EVEN, MLA_Q_RANK, MLA_HEADS * (MLA_NOPE + MLA_ROPE)), MLA_Q_RANK),
        "mla_kv_norm_g": gain(ks[14], (N_EVEN, MLA_KV_RANK)),
        "mla_w_ukv": w(ks[15], (N_EVEN, MLA_KV_RANK, MLA_HEADS * (MLA_NOPE + MLA_V)), MLA_KV_RANK),
        "even_w_out": w(ks[16], (N_EVEN, D_EVEN_OUT, D_MODEL), D_EVEN_OUT),
        "odd_w_in": w(ks[17], (N_ODD, D_MODEL, D_ODD_IN), D_MODEL),
        "hgrn_lb_logits": nrm(ks[18], (DEPTH, 2, D_HGRN), 0.1),
        "hgrn_g_norm_g": gain(ks[19], (N_ODD, HGRN_EXPAND)),
        "odd_w_out": w(ks[20], (N_ODD, D_HGRN, D_MODEL), D_HGRN),
        "final_norm_g": gain(ks[21], (D_MODEL,)),
    }


def reference(x, c, ctx, c_ctx, ada_w, ada_b, norm_g, ffn_w1, ffn_w3, ffn_w2, even_w_in, even_conv_w,
              mla_q_norm_g, mla_w_uq, mla_kv_norm_g, mla_w_ukv, even_w_out, odd_w_in, hgrn_lb_logits,
              hgrn_g_norm_g, odd_w_out, final_norm_g):
    rows = x.shape[1] // GRID_W
    cos, sin = axial_rope_tables(rows)
    lb_p = jax.nn.softmax(hgrn_lb_logits.astype(jnp.float32), axis=0)
    lb_table = jnp.cumsum(lb_p, axis=0) - lb_p[0]
    h = ctx
    for l in range(DEPTH):
        need_ctx_out = l < DEPTH - 1
        mx = ada_mod(c, ada_w[l], ada_b[l])[:, None]
        mc = ada_mod(c_ctx, ada_w[l], ada_b[l])[None, None]
        x = ffn_half(x, mx, 0, norm_g[l, 0], ffn_w1[l, 0], ffn_w3[l, 0], ffn_w2[l, 0])
        h = ffn_half(h, mc, 0, norm_g[l, 0], ffn_w1[l, 0], ffn_w3[l, 0], ffn_w2[l, 0])
        ux = modulate(rmsnorm(x, norm_g[l, 1]), mx[:, :, 3], mx[:, :, 4])
        uh = modulate(rmsnorm(h, norm_g[l, 1]), mc[:, :, 3], mc[:, :, 4])
        if l % 2 == 0:
            e = l // 2
            y_x, y_c = even_mixer(ux, uh, even_w_in[e], even_conv_w[e], mla_q_norm_g[e], mla_w_uq[e],
                                  mla_kv_norm_g[e], mla_w_ukv[e], even_w_out[e], cos, sin, need_ctx_out)
        else:
            o = l // 2
            y_x, y_c = odd_mixer(ux, uh, odd_w_in[o], lb_table[l], hgrn_g_norm_g[o], odd_w_out[o], need_ctx_out)
        x = x + mx[:, :, 5] * y_x
        x = ffn_half(x, mx, 2, norm_g[l, 2], ffn_w1[l, 1], ffn_w3[l, 1], ffn_w2[l, 1])
        if need_ctx_out:
            h = h + mc[:, :, 5] * y_c
            h = ffn_half(h, mc, 2, norm_g[l, 2], ffn_w1[l, 1], ffn_w3[l, 1], ffn_w2[l, 1])
    return rmsnorm(x, final_norm_g)
```

```python
import numpy as np
from contextlib import ExitStack
import concourse.bass as bass
import concourse.mybir as mybir
from concourse.bass_utils import run_bass_kernel_spmd

F32 = mybir.dt.float32
BF16 = mybir.dt.bfloat16
AF = mybir.ActivationFunctionType
ALU = mybir.AluOpType

D = 1024
NCTX = 256
SEQ = 8192
SEQ_L = SEQ // 2
NT = NCTX + SEQ_L
NKEY = NCTX + SEQ
PAIRS = [[0, 1], [2, 3], [4, 5], [6, 7]]
DFF = 2816
NJ = DFF // 128
KC = 8
EPS = 1e-6
NCORES = 8
MLA_SCALE = 96 ** -0.5
HGRN_SCALE = 128 ** -0.5


import types


def _freeze(fn):
    if fn.__closure__ is None:
        return fn
    cells = []
    for c in fn.__closure__:
        try:
            cells.append(types.CellType(c.cell_contents))
        except ValueError:
            cells.append(c)
    g = types.FunctionType(fn.__code__, fn.__globals__, fn.__name__, fn.__defaults__, tuple(cells))
    g.__kwdefaults__ = fn.__kwdefaults__
    return g


class Tok:
    __slots__ = ("sem", "val")

    def __init__(self, sem, val):
        self.sem, self.val = sem, val


class DmaSem:
    def __init__(self, P, name):
        self.sem = P.new_sem(name)
        self.count = 0

    def tok(self):
        return Tok(self.sem, self.count)


class Buf:
    def __init__(self, ap, name=""):
        self.ap, self.name = ap, name
        self.w = {}
        self.r = {}
        self.ds = None


class Eng:
    def __init__(self, P, name):
        self.P, self.name = P, name
        self.ops = []
        self.sem = P.new_sem("p_" + name)
        self.count = 0
        self.waited = {}

    def _wait(self, toks):
        need = {}
        for t in toks:
            k = id(t.sem)
            if t.val > need.get(k, (None, 0))[1]:
                need[k] = (t.sem, t.val)
        for k, (sem, val) in need.items():
            if self.waited.get(k, 0) >= val:
                continue
            self.waited[k] = val
            self.ops.append(("wait", sem, val))

    def emit(self, fn, toks, inc=True):
        self._wait(toks)
        if inc:
            self.count += 1
            self.ops.append(("op", fn, self.sem, 1))
            return Tok(self.sem, self.count)
        self.ops.append(("op", fn, None, 0))
        return None

    def replay(self, e):
        for o in self.ops:
            if o[0] == "wait":
                e.wait_ge(o[1], o[2])
            else:
                ins = o[1](e)
                if o[2] is not None:
                    ins.then_inc(o[2], o[3])


class Prog:
    def __init__(self):
        self.nc = bass.Bass("TRN2", target_bir_lowering=False)
        self.es = ExitStack()
        self.nsem = 0
        self.pe = Eng(self, "pe")
        self.act = Eng(self, "act")
        self.dve = Eng(self, "dve")
        self.pool = Eng(self, "pool")
        self.sp = Eng(self, "sp")
        self.engs = [self.pe, self.act, self.dve, self.pool, self.sp]
        self.dsems = []
        self.free_ds = []
        self.pending_pe = []

    def new_sem(self, name):
        self.nsem += 1
        return self.es.enter_context(self.nc.semaphore(name + str(self.nsem)))

    def get_ds(self):
        if self.free_ds:
            return self.free_ds.pop()
        d = DmaSem(self, "d")
        self.dsems.append(d)
        return d

    def sb(self, name, shape, dt):
        return self.es.enter_context(self.nc.sbuf_tensor(name, list(shape), dt))

    def din(self, name, shape, dt=F32):
        return self.nc.dram_tensor(name, list(shape), dt, kind="ExternalInput").ap()

    def dout(self, name, shape, dt=F32):
        return self.nc.dram_tensor(name, list(shape), dt, kind="ExternalOutput").ap()

    def dint(self, name, shape, dt=F32):
        return self.nc.dram_tensor(name, list(shape), dt, kind="Internal").ap()

    def do(self, eng, fn, r=(), w=(), inc=True):
        fn = _freeze(fn)
        toks = []
        for b in r:
            toks.extend(b.w.values())
        for b in w:
            toks.extend(b.w.values())
            toks.extend(b.r.values())
        if eng is self.pe and not inc:
            eng._wait(toks)
            eng.ops.append(("op", fn, None, 0))
            self.pending_pe.append((r, w))
            return None
        tok = eng.emit(fn, toks, True)
        groups = [(r, w)]
        if eng is self.pe:
            groups += self.pending_pe
            self.pending_pe = []
        for (rr, ww) in groups:
            for b in rr:
                b.r[id(tok.sem)] = tok
            for b in ww:
                b.w = {id(tok.sem): tok}
                b.r = {}
        return tok

    def load(self, q, buf, dst_ap, src_ap, slow=False):
        kw = {"allow_slow_non_contiguous": True} if slow else {}
        if buf.ds is None:
            buf.ds = self.get_ds()
        ds = buf.ds
        toks = [t for t in buf.w.values() if t.sem is not ds.sem] + list(buf.r.values())
        q._wait(toks)
        ds.count += 16
        q.ops.append(("op", lambda e: e.dma_start(out=dst_ap, in_=src_ap, **kw), ds.sem, 16))
        tok = ds.tok()
        buf.w = {id(ds.sem): tok}
        buf.r = {}
        return tok

    def store(self, q, buf, src_ap, dst_ap, slow=False):
        kw = {"allow_slow_non_contiguous": True} if slow else {}
        if buf.ds is None:
            buf.ds = self.get_ds()
        ds = buf.ds
        q._wait(list(buf.w.values()))
        ds.count += 16
        q.ops.append(("op", lambda e: e.dma_start(out=dst_ap, in_=src_ap, **kw), ds.sem, 16))
        tok = ds.tok()
        buf.r[id(ds.sem)] = tok
        return tok

    def barrier(self):
        assert not self.pending_pe
        toks = [Tok(e.sem, e.count) for e in self.engs[:4] if e.count > 0]
        toks += [d.tok() for d in self.dsems if d.count > 0]
        for e in self.engs:
            e._wait(toks)

    def finish(self):
        self.barrier()
        with self.nc.Block() as block:
            @block.tensor
            def _(e):
                self.pe.replay(e)

            @block.scalar
            def _(e):
                self.act.replay(e)

            @block.vector
            def _(e):
                self.dve.replay(e)

            @block.gpsimd
            def _(e):
                self.pool.replay(e)

            @block.sync
            def _(e):
                self.sp.replay(e)
        self.es.close()
        return self.nc


class Arena:
    def __init__(self, P, ncols):
        self.P = P
        self.t = P.sb("arena", [128, ncols], F32)
        self.n = ncols
        self.off = 0
        self.bufs = []

    def reset(self):
        self.P.barrier()
        for b in self.bufs:
            if b.ds is not None:
                self.P.free_ds.append(b.ds)
                b.ds = None
        self.bufs = []
        self.off = 0

    def f32(self, ncols, name=""):
        ncols = (ncols + 7) // 8 * 8
        assert self.off + ncols <= self.n, f"arena overflow {name} {self.off}+{ncols}>{self.n}"
        ap = self.t[:, self.off:self.off + ncols]
        self.off += ncols
        b = Buf(ap, name)
        self.bufs.append(b)
        return b

    def bf16(self, ncols, name=""):
        b = self.f32((ncols + 1) // 2, name)
        b.ap = b.ap.bitcast(BF16)
        return b


def v3(ap, inner):
    return ap.rearrange("p (a b) -> p a b", b=inner)


def tiles_of(TT, with_ctx=True):
    tl = [(0, NCTX, 1)] if with_ctx else []
    for i in range(SEQ_L // TT):
        tl.append((NCTX + i * TT, TT, 0))
    return tl


def build(stop_after=99, debug=False):
    P = Prog()
    nc = P.nc
    xin = P.din("xin", [D, NT])
    cvec = P.din("cvec", [128, KC, 2])
    ada_w = P.din("ada_w", [2, D, 9 * D])
    ada_b = P.din("ada_b", [2, 9 * D])
    norm_g = P.din("norm_g", [2, 3, D])
    ffn_w1 = P.din("ffn_w1", [2, 2, D, DFF])
    ffn_w3 = P.din("ffn_w3", [2, 2, D, DFF])
    ffn_w2 = P.din("ffn_w2", [2, 2, DFF, D])
    e_win = P.din("e_win", [D, 1920])
    e_wkr = P.din("e_wkr", [D, 96])
    e_wkrp = P.din("e_wkrp", [D, 96])
    e_conv = P.din("e_conv", [3, 512])
    e_qg = P.din("e_qg", [256])
    e_wqa = P.din("e_wqa", [256, 8 * 96])
    e_wqb = P.din("e_wqb", [256, 8 * 96])
    e_kvg = P.din("e_kvg", [128])
    e_wkn = P.din("e_wkn", [128, 8 * 96])
    e_wv = P.din("e_wv", [128, 512])
    e_wout = P.din("e_wout", [D, D])
    ropeC = P.din("ropeC", [96, NT])
    ropeS = P.din("ropeS", [96, NT])
    o_win = P.din("o_win", [D, 5 * D])
    o_lb = P.din("o_lb", [2, 2, D])
    o_gn = P.din("o_gn", [128])
    o_wout = P.din("o_wout", [D, D])
    fin_g = P.din("fin_g", [D])
    cmask = P.din("cmask", [64, 128])
    ident = P.din("ident", [128, 128])
    yout = P.dout("yout", [D, SEQ_L])
    selv = P.din("selv", [128, 2])
    mk = P.dout if debug else P.dint
    XT = mk("XT", [D, NT])
    GB = P.dint("GB", [512, NT], BF16)
    CV = P.dint("CV", [512, NT + 3])
    KCX = P.dint("KCX", [768, NCTX], BF16)
    KL = [P.dint(f"KL{i}", [192, SEQ_L], BF16) for i in range(4)]
    KG = [P.dint(f"KG{i}", [384, SEQ_L], BF16) for i in range(4)]
    VCX = P.dint("VCX", [NCTX, 512], BF16)
    VL = [P.dint(f"VL{i}", [SEQ_L // 2, 512], BF16) for i in range(2)]
    VG = [P.dint(f"VG{i}", [SEQ_L, 512], BF16) for i in range(2)]
    CVH = P.dint("CVH", [512, 8])
    CVG = P.dint("CVG", [1024, 8])
    SX = P.dint("SX", [128, 1024])
    SGA = P.dint("SGA", [256, 1024])
    QT = P.dint("QT", [8, 96, NT], BF16)
    BT = mk("BT", [512, NT], BF16) if not debug else P.dout("BT", [512, NT], BF16)
    OF = P.dint("OF", [D, SEQ_L])

    def cv_col(t):
        return 1 + t if t < NCTX else 2 + t

    cst = P.sb("cst", [128, 1024], F32)
    c_off = [0]

    def cbuf(n, name=""):
        ap = cst[:, c_off[0]:c_off[0] + n]
        c_off[0] += n
        assert c_off[0] <= 1024
        return Buf(ap, name)

    CVEC = cbuf(16)
    SC = cbuf(16)
    MOD = cbuf(144)
    ADAB = cbuf(72)
    NG = cbuf(48)
    G = cbuf(48)
    HG = cbuf(48)
    FING = cbuf(8)
    EPSB = cbuf(1)
    CONVW = cbuf(12)
    QG = cbuf(2)
    KVG = cbuf(1)
    LBL = cbuf(32)
    LB = cbuf(16)
    OML = cbuf(16)
    GN = cbuf(1)
    ZERO = cbuf(4)
    SEL = cbuf(2)
    ONESB = Buf(P.sb("onesb", [128, 128], BF16)[:], "ones")
    IDB = Buf(P.sb("idb", [128, 128], BF16)[:], "idb")
    MASK = Buf(P.sb("maskt", [64, 128], F32)[:], "mask")
    psall = P.es.enter_context(nc.psum_tensor("psall", [128, 4096], F32))
    banks = [Buf(psall[:, i * 512:(i + 1) * 512], f"bank{i}") for i in range(8)]
    bank_rr = [0]

    def nb():
        b = banks[bank_rr[0] % 8]
        bank_rr[0] += 1
        return b

    A = Arena(P, 50 * 1024)
    sp, pe, act, dve, pool = P.sp, P.pe, P.act, P.dve, P.pool

    P.load(sp, CVEC, v3(CVEC.ap, 2), cvec[:, :, :])
    for l_ in range(2):
        for j_ in range(3):
            P.load(sp, NG, NG.ap[:, (l_ * 3 + j_) * 8:(l_ * 3 + j_ + 1) * 8], norm_g[l_, j_].rearrange("(k p) -> p k", p=128), slow=True)
    P.load(sp, FING, FING.ap, fin_g.rearrange("(k p) -> p k", p=128), slow=True)
    for c_ in range(4):
        P.load(sp, CONVW, CONVW.ap[:, c_ * 3:(c_ + 1) * 3], e_conv[:, c_ * 128:(c_ + 1) * 128].rearrange("w p -> p w"), slow=True)
    P.load(sp, QG, QG.ap, e_qg.rearrange("(k p) -> p k", p=128), slow=True)
    P.load(sp, KVG, KVG.ap, e_kvg.rearrange("(k p) -> p k", p=128), slow=True)
    for l_ in range(2):
        for d_ in range(2):
            P.load(sp, LBL, LBL.ap[:, (l_ * 2 + d_) * 8:(l_ * 2 + d_ + 1) * 8], o_lb[l_, d_].rearrange("(h p) -> p h", p=128), slow=True)
    P.load(sp, GN, GN.ap, o_gn.rearrange("(k p) -> p k", p=128), slow=True)
    P.load(sp, MASK, MASK.ap, cmask[:, :])
    P.load(sp, SEL, SEL.ap, selv[:, :])
    P.load(pool, IDB, IDB.ap, ident[:, :])
    P.do(dve, lambda e: e.memset(EPSB.ap, EPS), w=[EPSB])
    P.do(dve, lambda e: e.memset(ZERO.ap, 0.0), w=[ZERO])
    P.do(dve, lambda e: e.memset(ONESB.ap, 1.0), w=[ONESB])
    P.do(dve, lambda e: e.tensor_tensor(out=LB.ap, in0=LBL.ap[:, 16:32], in1=LBL.ap[:, 0:16], op=ALU.subtract), r=[LBL], w=[LB])
    P.do(act, lambda e: e.activation(out=LB.ap, in_=LB.ap, func=AF.Sigmoid), r=[LB], w=[LB])
    P.do(dve, lambda e: e.tensor_scalar(out=OML.ap, in0=LB.ap, scalar1=-1.0, scalar2=1.0, op0=ALU.mult, op1=ALU.add), r=[LB], w=[OML])
    P.do(act, lambda e: e.activation(out=SC.ap, in_=CVEC.ap, func=AF.Silu), r=[CVEC], w=[SC])

    def allgather(src, dst):
        ds = P.get_ds()
        ds.count += 1
        pool.ops.append(("op", lambda e: e.collective_compute("AllGather", ALU.bypass, replica_groups=PAIRS,
                                                              ins=[src.opt()], outs=[dst.opt()]), ds.sem, 1))

    def ada_phase(l):
        A.reset()
        P.load(sp, ADAB, ADAB.ap, ada_b[l].rearrange("(j p) -> p j", p=128), slow=True)
        wb = [A.bf16(KC * 1024, f"adaw{i}") for i in range(3)]
        pb = nb()
        SCB = A.bf16(16, "scb")
        P.do(dve, lambda e: e.tensor_copy(out=SCB.ap, in_=SC.ap), r=[SC], w=[SCB])
        sc3 = v3(SCB.ap, 2)
        for j in range(9):
            W = wb[j % 3]
            W3 = v3(W.ap, 1024)
            for kc in range(KC):
                P.load(pool, W, W3[:, kc, :], ada_w[l, kc * 128:(kc + 1) * 128, j * 1024:(j + 1) * 1024])
            for oc in range(8):
                col = (j * 8 + oc) * 2
                for kc in range(KC):
                    P.do(pe, lambda e, W3=W3, kc=kc, oc=oc, col=col: e.matmul(
                        pb.ap[:, col:col + 2], W3[:, kc, oc * 128:(oc + 1) * 128], sc3[:, kc, :],
                        start=(kc == 0), stop=(kc == KC - 1)), r=[W, SCB], w=[pb], inc=(kc == KC - 1))
        ps3 = v3(pb.ap[:, 0:144], 2)
        mod3 = v3(MOD.ap, 72)
        for s in range(2):
            P.do(dve, lambda e, s=s: e.tensor_tensor(out=mod3[:, s, :], in0=ps3[:, :, s], in1=ADAB.ap, op=ALU.add),
                 r=[pb, ADAB], w=[MOD])
        for s in range(2):
            m4 = mod3[:, s, :].rearrange("p (jj t k) -> p jj t k", t=3, k=8)
            g3 = v3(G.ap, 24)[:, s, :].rearrange("p (jj k) -> p jj k", k=8)
            h3 = v3(HG.ap, 24)[:, s, :].rearrange("p (jj k) -> p jj k", k=8)
            ng3 = v3(NG.ap, 24)[:, l, :].rearrange("p (jj k) -> p jj k", k=8)
            P.do(dve, lambda e, m4=m4, g3=g3, ng3=ng3: e.scalar_tensor_tensor(
                out=g3, in0=m4[:, :, 1, :], scalar=1.0, in1=ng3, op0=ALU.add, op1=ALU.mult), r=[MOD, NG], w=[G])
            P.do(dve, lambda e, m4=m4, h3=h3: e.tensor_scalar(
                out=h3, in0=m4[:, :, 2, :], scalar1=0.5, scalar2=None, op0=ALU.mult), r=[MOD], w=[HG])
            P.do(dve, lambda e, m4=m4, h3=h3: e.tensor_copy(out=h3[:, 1, :], in_=m4[:, 1, 2, :]), r=[MOD], w=[HG])

    def modv(s, j):
        return v3(MOD.ap, 72)[:, s, j * 8:(j + 1) * 8]

    def gvec(s, jj):
        return v3(G.ap, 24)[:, s, jj * 8:(jj + 1) * 8]

    def hgvec(s, jj):
        return v3(HG.ap, 24)[:, s, jj * 8:(jj + 1) * 8]

    def rstd_from_ps(pb, n, RS, dim):
        P.do(act, lambda e: e.activation(out=RS.ap[:, 0:n], in_=pb.ap[:, 0:n], func=AF.Sqrt, bias=EPSB.ap[:, 0:1],
                                         scale=1.0 / dim), r=[pb, EPSB], w=[RS])
        P.do(dve, lambda e: e.reciprocal(out=RS.ap[:, 0:n], in_=RS.ap[:, 0:n]), r=[RS], w=[RS])

    def norm_mod(X, U, SQ, RS, TMP, n, s, jj, src, t0):
        x3 = v3(X.ap[:, 0:KC * n], n)
        u3 = v3(U.ap[:, 0:KC * n], n)
        sq3 = v3(SQ.ap[:, 0:KC * n], n)
        P.load(sp, X, x3, src.rearrange("(k p) t -> p k t", p=128)[:, :, t0:t0 + n])
        P.do(act, lambda e: e.activation(out=SQ.ap[:, 0:KC * n], in_=X.ap[:, 0:KC * n], func=AF.Square), r=[X], w=[SQ])
        pb = nb()
        for kc in range(KC):
            P.do(pe, lambda e, kc=kc: e.matmul(pb.ap[:, 0:n], ONESB.ap, sq3[:, kc, :], start=(kc == 0), stop=(kc == KC - 1)),
                 r=[SQ, ONESB], w=[pb], inc=(kc == KC - 1))
        rstd_from_ps(pb, n, RS, D)
        gv = gvec(s, jj)
        sh = modv(s, 3 * jj)
        for kc in range(KC):
            T = TMP[kc % 2]
            P.do(dve, lambda e, kc=kc, T=T: e.tensor_tensor(out=T.ap[:, 0:n], in0=x3[:, kc, :], in1=RS.ap[:, 0:n], op=ALU.mult),
                 r=[X, RS], w=[T])
            P.do(act, lambda e, kc=kc, T=T: e.activation(out=u3[:, kc, :], in_=T.ap[:, 0:n], func=AF.Identity,
                                                         bias=sh[:, kc:kc + 1], scale=gv[:, kc:kc + 1]),
                 r=[T, G, MOD], w=[U])
        return x3, u3

    def load_w_bf16(buf, view3, dram2d, nk, rows_per=128):
        for k in range(nk):
            P.load(pool, buf, view3[:, k, :], dram2d[k * rows_per:(k + 1) * rows_per, :])

    def ffn_phase(l, jj, src, dst, with_ctx=True, TT=512):
        wi = 0 if jj == 0 else 1
        A.reset()
        HJ = NJ // 2 * 128
        W1 = [A.bf16(KC * HJ, "w1a"), A.bf16(KC * HJ, "w1b")]
        W3 = [A.bf16(KC * HJ, "w3a"), A.bf16(KC * HJ, "w3b")]
        W2 = A.bf16(NJ * D, "w2")
        w13 = [v3(W1[i].ap, HJ) for i in range(2)]; w33 = [v3(W3[i].ap, HJ) for i in range(2)]; w23 = v3(W2.ap, D)
        for i in range(2):
            for kc in range(KC):
                P.load(pool, W1[i], w13[i][:, kc, :], ffn_w1[l, wi, kc * 128:(kc + 1) * 128, i * HJ:(i + 1) * HJ])
            for kc in range(KC):
                P.load(pool, W3[i], w33[i][:, kc, :], ffn_w3[l, wi, kc * 128:(kc + 1) * 128, i * HJ:(i + 1) * HJ])
        load_w_bf16(W2, w23, ffn_w2[l, wi], NJ)
        XB = [A.f32(KC * TT, "xa"), A.f32(KC * TT, "xb")]
        U = A.bf16(KC * TT, "u")
        H = A.bf16(NJ * TT, "h")
        RS = A.f32(TT, "rs")
        TMP = [A.f32(TT, "t0"), A.f32(TT, "t1")]
        SA = TMP
        for it, (t0, n, s) in enumerate(tiles_of(TT, with_ctx)):
            X = XB[it % 2]
            h3 = v3(H.ap[:, 0:NJ * n], n)
            x3, u3 = norm_mod(X, U, H, RS, TMP, n, s, jj, src, t0)
            for j in range(NJ):
                pa = nb(); pbk = nb()
                for kc in range(KC):
                    P.do(pe, lambda e, pa=pa, j=j, kc=kc: e.matmul(pa.ap[:, 0:n], w13[j // 11][:, kc, (j % 11) * 128:(j % 11 + 1) * 128], u3[:, kc, :],
                                                                   start=(kc == 0), stop=(kc == KC - 1)),
                         r=[W1[j // 11], U], w=[pa], inc=(kc == KC - 1))
                for kc in range(KC):
                    P.do(pe, lambda e, pbk=pbk, j=j, kc=kc: e.matmul(pbk.ap[:, 0:n], w33[j // 11][:, kc, (j % 11) * 128:(j % 11 + 1) * 128], u3[:, kc, :],
                                                                     start=(kc == 0), stop=(kc == KC - 1)),
                         r=[W3[j // 11], U], w=[pbk], inc=(kc == KC - 1))
                S_ = SA[j % 2]
                P.do(act, lambda e, pa=pa, S_=S_: e.activation(out=S_.ap[:, 0:n], in_=pa.ap[:, 0:n], func=AF.Silu), r=[pa], w=[S_])
                P.do(dve, lambda e, pbk=pbk, S_=S_, j=j: e.tensor_tensor(out=h3[:, j, :], in0=pbk.ap[:, 0:n], in1=S_.ap[:, 0:n], op=ALU.mult),
                     r=[pbk, S_], w=[H])
            hg = hgvec(s, jj)
            for m in range(KC):
                po = nb()
                for j in range(NJ):
                    P.do(pe, lambda e, po=po, j=j, m=m: e.matmul(po.ap[:, 0:n], w23[:, j, m * 128:(m + 1) * 128], h3[:, j, :],
                                                                 start=(j == 0), stop=(j == NJ - 1)),
                         r=[W2, H], w=[po], inc=(j == NJ - 1))
                P.do(dve, lambda e, po=po, m=m: e.scalar_tensor_tensor(out=x3[:, m, :], in0=po.ap[:, 0:n], scalar=hg[:, m:m + 1],
                                                                       in1=x3[:, m, :], op0=ALU.mult, op1=ALU.add),
                     r=[po, HG, X], w=[X])
            P.store(sp, X, x3, dst.rearrange("(k p) t -> p k t", p=128)[:, :, t0:t0 + n])

    def final_phase(src, TT=512):
        A.reset()
        XB = [A.f32(KC * TT, "xa"), A.f32(KC * TT, "xb")]
        SQ = A.bf16(KC * TT, "sq")
        RS = A.f32(TT, "rs")
        for it, (t0, n, s) in enumerate(tiles_of(TT, False)):
            X = XB[it % 2]
            x3 = v3(X.ap[:, 0:KC * n], n)
            sq3 = v3(SQ.ap[:, 0:KC * n], n)
            P.load(sp, X, x3, src.rearrange("(k p) t -> p k t", p=128)[:, :, t0:t0 + n])
            P.do(act, lambda e, X=X: e.activation(out=SQ.ap[:, 0:KC * n], in_=X.ap[:, 0:KC * n], func=AF.Square), r=[X], w=[SQ])
            pb = nb()
            for kc in range(KC):
                P.do(pe, lambda e, pb=pb, kc=kc, sq3=sq3: e.matmul(pb.ap[:, 0:n], ONESB.ap, sq3[:, kc, :], start=(kc == 0), stop=(kc == KC - 1)),
                     r=[SQ, ONESB], w=[pb], inc=(kc == KC - 1))
            rstd_from_ps(pb, n, RS, D)
            for kc in range(KC):
                P.do(dve, lambda e, kc=kc, x3=x3: e.scalar_tensor_tensor(out=x3[:, kc, :], in0=x3[:, kc, :], scalar=FING.ap[:, kc:kc + 1],
                                                                        in1=RS.ap[:, 0:n], op0=ALU.mult, op1=ALU.mult),
                     r=[X, RS, FING], w=[X])
            P.store(sp, X, x3, yout.rearrange("(k p) t -> p k t", p=128)[:, :, t0 - NCTX:t0 - NCTX + n])

    def mix0_proj(src, TT=512):
        A.reset()
        WI = A.bf16(KC * 1920, "win"); wi3 = v3(WI.ap, 1920)
        load_w_bf16(WI, wi3, e_win, KC)
        WKR = A.bf16(KC * 96, "wkr"); wkr3 = v3(WKR.ap, 96)
        load_w_bf16(WKR, wkr3, e_wkr, KC)
        WKRP = A.bf16(KC * 96, "wkrp"); wkrp3 = v3(WKRP.ap, 96)
        load_w_bf16(WKRP, wkrp3, e_wkrp, KC)
        WQA = A.bf16(2 * 768, "wqa"); wqa3 = v3(WQA.ap, 768)
        load_w_bf16(WQA, wqa3, e_wqa, 2)
        WQB = A.bf16(2 * 768, "wqb"); wqb3 = v3(WQB.ap, 768)
        load_w_bf16(WQB, wqb3, e_wqb, 2)
        WKN = A.bf16(768, "wkn")
        P.load(pool, WKN, WKN.ap, e_wkn[:, :])
        WV = A.bf16(512, "wv")
        P.load(pool, WV, WV.ap, e_wv[:, :])
        XB = [A.f32(KC * TT, "xa"), A.f32(KC * TT, "xb")]
        U = A.bf16(KC * TT, "u"); SQ = A.bf16(KC * TT, "sq")
        RS = A.f32(TT, "rs")
        TMP = [A.f32(TT, "t0"), A.f32(TT, "t1")]
        RC = A.f32(TT, "ropec"); RSN = A.f32(TT, "ropes")
        GBt = [A.bf16(TT, "gb0"), A.bf16(TT, "gb1")]
        CVt = [A.f32(TT, "cv0"), A.f32(TT, "cv1")]
        CQ = A.f32(2 * TT, "cq"); NQ = A.bf16(2 * TT, "nq"); SQQ = A.bf16(2 * TT, "sqq")
        CKV = A.f32(TT, "ckv"); NKV = A.bf16(TT, "nkv"); SQK = A.bf16(TT, "sqk")
        RSQ = A.f32(TT, "rsq")
        ROT = A.f32(TT, "rot")
        T1 = [A.f32(TT, "q1a"), A.f32(TT, "q1b")]
        T2 = [A.f32(TT, "q2a"), A.f32(TT, "q2b")]
        QO = [A.bf16(TT, "qo0"), A.bf16(TT, "qo1")]
        KO = [A.bf16(TT, "ko0"), A.bf16(TT, "ko1")]
        VO = [A.bf16(512, "vo0"), A.bf16(512, "vo1")]
        ZT = A.f32(512, "zt")
        P.do(dve, lambda e: e.memset(ZT.ap[:, 0:4], 0.0), w=[ZT])
        for c in range(4):
            for col in (0, NCTX + 1):
                P.store(sp, ZT, ZT.ap[:, 0:1], CV[c * 128:(c + 1) * 128, col:col + 1], slow=True)
        for it, (t0, n, s) in enumerate(tiles_of(TT, True)):
            X = XB[it % 2]
            x3, u3 = norm_mod(X, U, SQ, RS, TMP, n, s, 1, src, t0)
            P.load(sp, RC, RC.ap[0:96, 0:n], ropeC[:, t0:t0 + n])
            P.load(sp, RSN, RSN.ap[0:96, 0:n], ropeS[:, t0:t0 + n])

            def proj(col0, ncols, pb, W3=wi3, Wb=WI):
                for kc in range(KC):
                    P.do(pe, lambda e, kc=kc: e.matmul(pb.ap[0:ncols, 0:n], W3[:, kc, col0:col0 + ncols], u3[:, kc, :],
                                                       start=(kc == 0), stop=(kc == KC - 1)),
                         r=[Wb, U], w=[pb], inc=(kc == KC - 1))

            cq3 = v3(CQ.ap[:, 0:2 * n], n); nq3 = v3(NQ.ap[:, 0:2 * n], n); sqq3 = v3(SQQ.ap[:, 0:2 * n], n)
            for i in range(2):
                pq = nb(); proj(1536 + i * 128, 128, pq)
                P.do(act, lambda e, pq=pq, i=i: e.activation(out=cq3[:, i, :], in_=pq.ap[:, 0:n], func=AF.Copy), r=[pq], w=[CQ])
            P.do(act, lambda e: e.activation(out=SQQ.ap[:, 0:2 * n], in_=CQ.ap[:, 0:2 * n], func=AF.Square), r=[CQ], w=[SQQ])
            pss = nb()
            for i in range(2):
                P.do(pe, lambda e, i=i: e.matmul(pss.ap[:, 0:n], ONESB.ap, sqq3[:, i, :], start=(i == 0), stop=(i == 1)),
                     r=[SQQ, ONESB], w=[pss], inc=(i == 1))
            rstd_from_ps(pss, n, RSQ, 256)
            for i in range(2):
                T = TMP[i % 2]
                P.do(dve, lambda e, i=i, T=T: e.tensor_tensor(out=T.ap[:, 0:n], in0=cq3[:, i, :], in1=RSQ.ap[:, 0:n], op=ALU.mult),
                     r=[CQ, RSQ], w=[T])
                P.do(act, lambda e, i=i, T=T: e.activation(out=nq3[:, i, :], in_=T.ap[:, 0:n], func=AF.Identity, scale=QG.ap[:, i:i + 1]),
                     r=[T, QG], w=[NQ])
            pk = nb(); proj(1792, 128, pk)
            P.do(act, lambda e, pk=pk: e.activation(out=CKV.ap[:, 0:n], in_=pk.ap[:, 0:n], func=AF.Copy), r=[pk], w=[CKV])
            P.do(act, lambda e: e.activation(out=SQK.ap[:, 0:n], in_=CKV.ap[:, 0:n], func=AF.Square), r=[CKV], w=[SQK])
            pss = nb()
            P.do(pe, lambda e, pss=pss: e.matmul(pss.ap[:, 0:n], ONESB.ap, SQK.ap[:, 0:n], start=True, stop=True), r=[SQK, ONESB], w=[pss])
            rstd_from_ps(pss, n, RSQ, 128)
            T = TMP[0]
            P.do(dve, lambda e, T=T: e.tensor_tensor(out=T.ap[:, 0:n], in0=CKV.ap[:, 0:n], in1=RSQ.ap[:, 0:n], op=ALU.mult), r=[CKV, RSQ], w=[T])
            P.do(act, lambda e, T=T: e.activation(out=NKV.ap[:, 0:n], in_=T.ap[:, 0:n], func=AF.Identity, scale=KVG.ap[:, 0:1]), r=[T, KVG], w=[NKV])
            pr = nb(); proj(0, 96, pr, wkr3, WKR)
            prp = nb(); proj(0, 96, prp, wkrp3, WKRP)
            t1 = T1[0]; t2 = T2[0]
            P.do(dve, lambda e, pr=pr, t1=t1: e.tensor_tensor(out=t1.ap[0:96, 0:n], in0=pr.ap[0:96, 0:n], in1=RC.ap[0:96, 0:n], op=ALU.mult),
                 r=[pr, RC], w=[t1])
            P.do(dve, lambda e, prp=prp, t2=t2: e.tensor_tensor(out=t2.ap[0:96, 0:n], in0=prp.ap[0:96, 0:n], in1=RSN.ap[0:96, 0:n], op=ALU.mult),
                 r=[prp, RSN], w=[t2])
            P.do(pool, lambda e, t1=t1, t2=t2: e.tensor_tensor(out=ROT.ap[0:96, 0:n], in0=t1.ap[0:96, 0:n], in1=t2.ap[0:96, 0:n], op=ALU.add),
                 r=[t1, t2], w=[ROT])
            for c in range(4):
                pg = nb(); proj(c * 128, 128, pg)
                gbt = GBt[c % 2]
                P.do(act, lambda e, pg=pg, gbt=gbt: e.activation(out=gbt.ap[:, 0:n], in_=pg.ap[:, 0:n], func=AF.Copy), r=[pg], w=[gbt])
                P.store(sp, gbt, gbt.ap[:, 0:n], GB[c * 128:(c + 1) * 128, t0:t0 + n])
                pc = nb(); proj(512 + c * 128, 128, pc)
                pv = nb(); proj(1024 + c * 128, 128, pv)
                T = TMP[c % 2]
                cvt = CVt[c % 2]
                P.do(act, lambda e, pc=pc, T=T: e.activation(out=T.ap[:, 0:n], in_=pc.ap[:, 0:n], func=AF.Copy), r=[pc], w=[T])
                P.do(dve, lambda e, pv=pv, T=T, cvt=cvt: e.tensor_tensor(out=cvt.ap[:, 0:n], in0=pv.ap[:, 0:n], in1=T.ap[:, 0:n], op=ALU.mult),
                     r=[pv, T], w=[cvt])
                cc = cv_col(t0)
                P.store(sp, cvt, cvt.ap[:, 0:n], CV[c * 128:(c + 1) * 128, cc:cc + n])
                if t0 + n == NT:
                    P.store(sp, cvt, cvt.ap[:, n - 1:n], CVH[c * 128:(c + 1) * 128, 0:1], slow=True)
            for h in range(8):
                pa = nb(); pbk = nb()
                for i in range(2):
                    P.do(pe, lambda e, i=i, h=h, pa=pa: e.matmul(pa.ap[0:96, 0:n], wqa3[:, i, h * 96:(h + 1) * 96], nq3[:, i, :],
                                                                 start=(i == 0), stop=(i == 1)), r=[WQA, NQ], w=[pa], inc=(i == 1))
                for i in range(2):
                    P.do(pe, lambda e, i=i, h=h, pbk=pbk: e.matmul(pbk.ap[0:96, 0:n], wqb3[:, i, h * 96:(h + 1) * 96], nq3[:, i, :],
                                                                   start=(i == 0), stop=(i == 1)), r=[WQB, NQ], w=[pbk], inc=(i == 1))
                t1 = T1[h % 2]; t2 = T2[h % 2]; qo = QO[h % 2]
                P.do(dve, lambda e, pa=pa, t1=t1: e.tensor_tensor(out=t1.ap[0:96, 0:n], in0=pa.ap[0:96, 0:n], in1=RC.ap[0:96, 0:n], op=ALU.mult),
                     r=[pa, RC], w=[t1])
                P.do(dve, lambda e, pbk=pbk, t2=t2: e.tensor_tensor(out=t2.ap[0:96, 0:n], in0=pbk.ap[0:96, 0:n], in1=RSN.ap[0:96, 0:n], op=ALU.mult),
                     r=[pbk, RSN], w=[t2])
                P.do(pool, lambda e, t1=t1, t2=t2, qo=qo: e.tensor_tensor(out=qo.ap[0:96, 0:n], in0=t1.ap[0:96, 0:n], in1=t2.ap[0:96, 0:n], op=ALU.add),
                     r=[t1, t2], w=[qo])
                P.store(sp, qo, qo.ap[0:96, 0:n], QT[h, :, t0:t0 + n])
            for h in range(8):
                pkh = nb()
                P.do(pe, lambda e, pkh=pkh, h=h: e.matmul(pkh.ap[0:96, 0:n], WKN.ap[:, h * 96:(h + 1) * 96], NKV.ap[:, 0:n], start=True, stop=True),
                     r=[WKN, NKV], w=[pkh])
                ko = KO[h % 2]
                P.do(dve, lambda e, pkh=pkh, ko=ko: e.tensor_tensor(out=ko.ap[0:96, 0:n], in0=pkh.ap[0:96, 0:n], in1=ROT.ap[0:96, 0:n], op=ALU.add),
                     r=[pkh, ROT], w=[ko])
                if s == 1:
                    P.store(sp, ko, ko.ap[0:96, 0:n], KCX[h * 96:(h + 1) * 96, t0:t0 + n])
                else:
                    P.store(sp, ko, ko.ap[0:96, 0:n], KL[h // 2][(h % 2) * 96:(h % 2 + 1) * 96, t0 - NCTX:t0 - NCTX + n])
            for tb in range(n // 128):
                pvv = nb()
                P.do(pe, lambda e, pvv=pvv, tb=tb: e.matmul(pvv.ap[:, 0:512], NKV.ap[:, tb * 128:(tb + 1) * 128], WV.ap, start=True, stop=True),
                     r=[NKV, WV], w=[pvv])
                vo = VO[tb % 2]
                P.do(act, lambda e, pvv=pvv, vo=vo: e.activation(out=vo.ap, in_=pvv.ap, func=AF.Copy), r=[pvv], w=[vo])
                if s == 1:
                    P.store(sp, vo, vo.ap, VCX[t0 + tb * 128:t0 + (tb + 1) * 128, :])
                else:
                    tl_ = t0 - NCTX + tb * 128
                    P.store(sp, vo, vo.ap, VL[tl_ // 2048][tl_ % 2048:tl_ % 2048 + 128, :])

    def mix0_exchange():
        A.reset()
        for i in range(4):
            allgather(KL[i], KG[i])
        for i in range(2):
            allgather(VL[i], VG[i])
        allgather(CVH, CVG)
        A.reset()
        HB = A.f32(64, "hb"); HO = A.f32(8, "ho")
        hb3 = v3(HB.ap, 8)
        P.load(sp, HB, hb3, CVG.rearrange("(r p) k -> p r k", p=128))
        P.do(dve, lambda e: e.tensor_scalar(out=HO.ap[:, 0:4], in0=hb3[:, 0:4, 0], scalar1=SEL.ap[:, 0:1], scalar2=None, op0=ALU.mult), r=[HB, SEL], w=[HO])
        P.do(dve, lambda e: e.scalar_tensor_tensor(out=HO.ap[:, 0:4], in0=hb3[:, 4:8, 0], scalar=SEL.ap[:, 1:2], in1=HO.ap[:, 0:4],
                                                   op0=ALU.mult, op1=ALU.add), r=[HB, SEL, HO], w=[HO])
        for c in range(4):
            P.store(sp, HO, HO.ap[:, c:c + 1], CV[c * 128:(c + 1) * 128, NT + 2:NT + 3], slow=True)

    def mix0_attn():
        A.reset()
        NKT = NKEY // 128
        KH = [A.bf16(NKEY, "kh0"), A.bf16(NKEY, "kh1")]
        QH = [A.bf16(NT, "qh0"), A.bf16(NT, "qh1")]
        VA = [A.bf16(NKT * 128, "va0"), A.bf16(NKT * 128, "va1")]
        PT = [[A.bf16(512, f"pt{i}a"), A.bf16(512, f"pt{i}b")] for i in range(3)]
        RCP = A.f32(1024, "rcp")
        BO = [A.bf16(1024, "bo0"), A.bf16(1024, "bo1")]
        SP_ = [[Buf(psall[:, (2 * i + j) * 512:(2 * i + j + 1) * 512], f"s{i}{j}") for j in range(2)] for i in range(2)]
        OP_ = [[Buf(psall[:, (4 + 2 * i + j) * 512:(4 + 2 * i + j + 1) * 512], f"o{i}{j}") for j in range(2)] for i in range(2)]
        for i in range(2):
            va3 = v3(VA[i].ap, 128)
            P.do(dve, lambda e, va3=va3: e.memset(va3[:, :, 64:128], 1.0), w=[VA[i]])
        it = 0
        qtiles = [(0, NCTX, 2)] + [(NCTX + i * 1024, 1024, NKT) for i in range(SEQ_L // 1024)]
        for h in range(8):
            K_ = KH[h % 2]; Q_ = QH[h % 2]; V_ = VA[h % 2]
            va3 = v3(V_.ap, 128)
            P.load(sp, K_, K_.ap[0:96, 0:NCTX], KCX[h * 96:(h + 1) * 96, :])
            for r_ in range(2):
                P.load(sp, K_, K_.ap[0:96, NCTX + r_ * SEQ_L:NCTX + (r_ + 1) * SEQ_L],
                       KG[h // 2][r_ * 192 + (h % 2) * 96:r_ * 192 + (h % 2 + 1) * 96, :])
            P.load(sp, Q_, Q_.ap[0:96, :], QT[h, :, :])
            P.load(sp, V_, va3[:, 0:2, 0:64], VCX[:, h * 64:(h + 1) * 64].rearrange("(kt p) d -> p kt d", p=128))
            for r_ in range(2):
                for j_ in range(2):
                    k0 = 2 + r_ * 32 + j_ * 16
                    P.load(sp, V_, va3[:, k0:k0 + 16, 0:64],
                           VG[j_][r_ * 2048:(r_ + 1) * 2048, h * 64:(h + 1) * 64].rearrange("(kt p) d -> p kt d", p=128))
            for (q0, nq, nkt) in qtiles:
                O_ = OP_[it % 2]
                halves = [(0, min(512, nq))] + ([(512, 512)] if nq > 512 else [])

                def emit_qk(kt):
                    for hi, (c0, cn) in enumerate(halves):
                        S_ = SP_[kt % 2][hi]
                        P.do(pe, lambda e, S_=S_, kt=kt, c0=c0, cn=cn: e.matmul(
                            S_.ap[:, 0:cn], K_.ap[0:96, kt * 128:(kt + 1) * 128], Q_.ap[0:96, q0 + c0:q0 + c0 + cn],
                            start=True, stop=True), r=[K_, Q_], w=[S_])

                emit_qk(0)
                for kt in range(nkt):
                    if kt + 1 < nkt:
                        emit_qk(kt + 1)
                    for hi, (c0, cn) in enumerate(halves):
                        S_ = SP_[kt % 2][hi]; pt = PT[kt % 3][hi]
                        P.do(act, lambda e, S_=S_, pt=pt, cn=cn: e.activation(out=pt.ap[:, 0:cn], in_=S_.ap[:, 0:cn], func=AF.Exp, scale=MLA_SCALE),
                             r=[S_], w=[pt])
                    for hi, (c0, cn) in enumerate(halves):
                        pt = PT[kt % 3][hi]; Oh = O_[hi]
                        P.do(pe, lambda e, Oh=Oh, kt=kt, cn=cn, pt=pt: e.matmul(
                            Oh.ap[:, 0:cn], va3[:, kt, :], pt.ap[:, 0:cn],
                            start=(kt == 0), stop=(kt == nkt - 1)), r=[V_, pt], w=[Oh])
                bo = BO[it % 2]
                for hi, (c0, cn) in enumerate(halves):
                    Oh = O_[hi]
                    P.do(dve, lambda e, Oh=Oh, c0=c0, cn=cn: e.reciprocal(out=RCP.ap[64:128, c0:c0 + cn], in_=Oh.ap[64:128, 0:cn]), r=[Oh], w=[RCP])
                    P.do(dve, lambda e, Oh=Oh, bo=bo, c0=c0, cn=cn: e.tensor_tensor(out=bo.ap[0:64, c0:c0 + cn], in0=Oh.ap[0:64, 0:cn], in1=RCP.ap[64:128, c0:c0 + cn], op=ALU.mult),
                         r=[Oh, RCP], w=[bo])
                P.store(sp, bo, bo.ap[0:64, 0:nq], BT[h * 64:(h + 1) * 64, q0:q0 + nq])
                it += 1

    def mix0_out(src, dst, TT=512):
        A.reset()
        WO = A.bf16(KC * D, "wo"); wo3 = v3(WO.ap, D)
        load_w_bf16(WO, wo3, e_wout, KC)
        XB = [A.f32(KC * TT, "xa"), A.f32(KC * TT, "xb")]
        GBt = [A.bf16(4 * TT, "gb0"), A.bf16(4 * TT, "gb1")]
        CVw = [A.f32(4 * (TT + 2), "cvw0"), A.f32(4 * (TT + 2), "cvw1")]
        BTt = [A.bf16(4 * TT, "bt0"), A.bf16(4 * TT, "bt1")]
        ACC = [A.f32(TT, "acc0"), A.f32(TT, "acc1")]
        AT = A.bf16(4 * TT, "at")
        cw3 = v3(CONVW.ap, 3)
        for it, (t0, n, s) in enumerate(tiles_of(TT, True)):
            X = XB[it % 2]; gb = GBt[it % 2]; cvw = CVw[it % 2]; bt = BTt[it % 2]
            x3 = v3(X.ap[:, 0:KC * n], n)
            gb3 = v3(gb.ap[:, 0:4 * n], n); cv3 = v3(cvw.ap[:, 0:4 * (n + 2)], n + 2); bt3 = v3(bt.ap[:, 0:4 * n], n)
            at3 = v3(AT.ap[:, 0:4 * n], n)
            P.load(sp, X, x3, src.rearrange("(k p) t -> p k t", p=128)[:, :, t0:t0 + n])
            P.load(sp, gb, gb3, GB.rearrange("(c p) t -> p c t", p=128)[:, :, t0:t0 + n])
            cc = cv_col(t0)
            P.load(sp, cvw, cv3, CV.rearrange("(c p) t -> p c t", p=128)[:, :, cc - 1:cc + n + 1])
            P.load(sp, bt, bt3, BT.rearrange("(c p) t -> p c t", p=128)[:, :, t0:t0 + n])
            for c in range(4):
                acc = ACC[c % 2]
                P.do(dve, lambda e, c=c, acc=acc: e.tensor_scalar(out=acc.ap[:, 0:n], in0=cv3[:, c, 0:n], scalar1=cw3[:, c, 0:1], scalar2=None, op0=ALU.mult),
                     r=[cvw, CONVW], w=[acc])
                P.do(dve, lambda e, c=c, acc=acc: e.scalar_tensor_tensor(out=acc.ap[:, 0:n], in0=cv3[:, c, 1:n + 1], scalar=cw3[:, c, 1:2], in1=acc.ap[:, 0:n],
                                                                          op0=ALU.mult, op1=ALU.add), r=[cvw, CONVW, acc], w=[acc])
                P.do(dve, lambda e, c=c, acc=acc: e.scalar_tensor_tensor(out=acc.ap[:, 0:n], in0=cv3[:, c, 2:n + 2], scalar=cw3[:, c, 2:3], in1=acc.ap[:, 0:n],
                                                                          op0=ALU.mult, op1=ALU.add), r=[cvw, CONVW, acc], w=[acc])
                P.do(dve, lambda e, c=c, acc=acc: e.tensor_tensor(out=at3[:, c, :], in0=acc.ap[:, 0:n], in1=gb3[:, c, :], op=ALU.mult),
                     r=[acc, gb], w=[AT])
            g5 = hgvec(s, 1)
            for m in range(KC):
                po = nb()
                for c in range(8):
                    rhs = at3[:, c, :] if c < 4 else bt3[:, c - 4, :]
                    P.do(pe, lambda e, po=po, c=c, m=m, rhs=rhs: e.matmul(po.ap[:, 0:n], wo3[:, c, m * 128:(m + 1) * 128], rhs,
                                                                         start=(c == 0), stop=(c == 7)),
                         r=[WO, AT, bt], w=[po], inc=(c == 7))
                P.do(dve, lambda e, po=po, m=m: e.scalar_tensor_tensor(out=x3[:, m, :], in0=po.ap[:, 0:n], scalar=g5[:, m:m + 1],
                                                                       in1=x3[:, m, :], op0=ALU.mult, op1=ALU.add),
                     r=[po, HG, X], w=[X])
            P.store(sp, X, x3, dst.rearrange("(k p) t -> p k t", p=128)[:, :, t0:t0 + n])

    def mix1_dir(src, dst, direction, TT=256):
        bwd = direction == 1
        A.reset()
        ncols = 3
        WIN = A.bf16(KC * ncols * D, "owin"); win3 = v3(WIN.ap, ncols * D)
        for blk, srcblk in enumerate([0, 1, 2 + direction]):
            for kc in range(KC):
                P.load(pool, WIN, win3[:, kc, blk * D:(blk + 1) * D], o_win[kc * 128:(kc + 1) * 128, srcblk * D:(srcblk + 1) * D])
        NCH = TT // 64
        XB = [A.f32(KC * TT, "xa"), A.f32(KC * TT, "xb")]
        U2 = [A.bf16(KC * TT, "u0"), A.bf16(KC * TT, "u1")]
        SQn = A.bf16(KC * TT, "sqn")
        RS = A.f32(TT, "rs")
        TMP = [A.f32(TT, "t0"), A.f32(TT, "t1")]
        TMP2 = [A.f32(TT, "t2"), A.f32(TT, "t3")]
        QF2 = [A.f32(8 * TT, "qf0"), A.f32(8 * TT, "qf1")]; FF2 = [A.f32(8 * TT, "ff0"), A.f32(8 * TT, "ff1")]
        LF = A.f32(8 * TT, "lf"); BB = A.f32(8 * TT, "bb"); EE = A.f32(8 * TT, "ee")
        QT_ = A.bf16(8 * TT, "qt"); KTL = A.bf16(8 * TT, "ktl"); KE = A.bf16(8 * TT, "ke")
        VV2 = [A.bf16(NCH * 8 * 128, "vv0"), A.bf16(NCH * 8 * 128, "vv1")]
        OO = A.f32(8 * TT, "oo")
        DEC = A.f32(8 * NCH, "dec")
        M64 = A.f32(8 * TT, "m64")
        ST = A.f32(8 * 128, "st"); STBS = [A.bf16(8 * 128, "stb0"), A.bf16(8 * 128, "stb1")]
        KET = [A.bf16(1024, f"ket{i}") for i in range(2)]
        ATM = [A.bf16(512, f"atm{i}") for i in range(2)]
        stb_i = [0]; k_it = [0]
        st3 = v3(ST.ap, 128)
        P.do(dve, lambda e: e.memset(M64.ap, 1.0), w=[M64])
        P.do(dve, lambda e: e.memset(v3(M64.ap, 64)[:, :, 0:1], 0.0), w=[M64])
        mask_ap = MASK.ap[:, 64:128] if bwd else MASK.ap[:, 0:64]
        tl = tiles_of(TT, True)
        ctx_tiles = [t for t in tl if t[2] == 1]
        lat_tiles = [t for t in tl if t[2] == 0]
        order = (ctx_tiles + lat_tiles) if not bwd else lat_tiles[::-1]
        if not bwd:
            P.do(dve, lambda e: e.memset(ST.ap, 0.0), w=[ST])
            P.do(dve, lambda e: e.memset(STBS[0].ap, 0.0), w=[STBS[0]])
        else:
            P.load(sp, ST, ST.ap, SGA[0:128, :])
            P.load(sp, OO, OO.ap[:, 0:1024], SGA[128:256, :])
            P.do(dve, lambda e: e.tensor_scalar(out=ST.ap, in0=ST.ap, scalar1=SEL.ap[:, 0:1], scalar2=None, op0=ALU.mult), r=[ST, SEL], w=[ST])
            P.do(dve, lambda e: e.scalar_tensor_tensor(out=ST.ap, in0=OO.ap[:, 0:1024], scalar=SEL.ap[:, 1:2], in1=ST.ap, op0=ALU.mult, op1=ALU.add),
                 r=[OO, SEL, ST], w=[ST])
            P.do(pool, lambda e: e.tensor_copy(out=STBS[0].ap, in_=ST.ap), r=[ST], w=[STBS[0]])
        lbv = v3(LB.ap, 8)[:, direction, :]
        omlv = v3(OML.ap, 8)[:, direction, :]
        tctx = {}

        def stageA(it, gen=None):
            (t0, n, s) = order[it]
            X = XB[it % 2]; U = U2[it % 2]; QF = QF2[it % 2]; FF = FF2[it % 2]; VV = VV2[it % 2]
            x3, u3 = norm_mod(X, U, SQn, RS, TMP, n, s, 1, src, t0)
            qf3 = v3(QF.ap, n); ff3 = v3(FF.ap, n)
            vv4 = VV.ap.rearrange("p (c h d) -> p c h d", h=8, d=128)
            tctx[it] = (x3, u3)
            for h in range(8):
                pq = nb()
                for kc in range(KC):
                    P.do(pe, lambda e, kc=kc, h=h, pq=pq: e.matmul(pq.ap[:, 0:n], win3[:, kc, h * 128:(h + 1) * 128], u3[:, kc, :],
                                                                   start=(kc == 0), stop=(kc == KC - 1)), r=[WIN, U], w=[pq], inc=(kc == KC - 1))
                P.do(act, lambda e, pq=pq, h=h: e.activation(out=qf3[:, h, :], in_=pq.ap[:, 0:n], func=AF.Copy), r=[pq], w=[QF])
                pz = nb()
                for kc in range(KC):
                    P.do(pe, lambda e, kc=kc, h=h, pz=pz: e.matmul(pz.ap[:, 0:n], win3[:, kc, 2 * D + h * 128:2 * D + (h + 1) * 128], u3[:, kc, :],
                                                                   start=(kc == 0), stop=(kc == KC - 1)), r=[WIN, U], w=[pz], inc=(kc == KC - 1))
                P.do(act, lambda e, pz=pz, h=h: e.activation(out=ff3[:, h, :], in_=pz.ap[:, 0:n], func=AF.Exp, scale=-1.0), r=[pz], w=[FF])
                for c in range(n // 64):
                    pvv = nb()
                    for kc in range(KC):
                        P.do(pe, lambda e, kc=kc, h=h, c=c, pvv=pvv: e.matmul(pvv.ap[0:64, 0:128], u3[:, kc, c * 64:(c + 1) * 64],
                                                                              win3[:, kc, D + h * 128:D + (h + 1) * 128],
                                                                              start=(kc == 0), stop=(kc == KC - 1)),
                             r=[WIN, U], w=[pvv], inc=(kc == KC - 1))
                    P.do(act, lambda e, pvv=pvv, c=c, h=h: e.activation(out=vv4[0:64, c, h, :], in_=pvv.ap[0:64, 0:128], func=AF.Copy), r=[pvv], w=[VV])
                if gen is not None:
                    for _ in range(3):
                        next(gen, None)

        def stageB(it):
            (t0, n, s) = order[it]
            QF = QF2[it % 2]; FF = FF2[it % 2]
            ff3 = v3(FF.ap, n)
            NN = 8 * n
            P.do(dve, lambda e: e.tensor_scalar(out=FF.ap[:, 0:NN], in0=FF.ap[:, 0:NN], scalar1=1.0, scalar2=None, op0=ALU.add), r=[FF], w=[FF])
            yield
            P.do(dve, lambda e: e.reciprocal(out=FF.ap[:, 0:NN], in_=FF.ap[:, 0:NN]), r=[FF], w=[FF])
            yield
            P.do(dve, lambda e: e.tensor_tensor(out=ff3, in0=ff3, in1=omlv.unsqueeze(2).broadcast_to([128, 8, n]), op=ALU.mult), r=[FF, OML], w=[FF])
            yield
            P.do(dve, lambda e: e.tensor_tensor(out=ff3, in0=ff3, in1=lbv.unsqueeze(2).broadcast_to([128, 8, n]), op=ALU.add), r=[FF, LB], w=[FF])
            yield
            P.do(act, lambda e: e.activation(out=LF.ap[:, 0:NN], in_=FF.ap[:, 0:NN], func=AF.Ln), r=[FF], w=[LF])
            yield
            P.do(dve, lambda e: e.tensor_scalar(out=FF.ap[:, 0:NN], in0=FF.ap[:, 0:NN], scalar1=-1.0, scalar2=1.0, op0=ALU.mult, op1=ALU.add), r=[FF], w=[FF])
            yield
            P.do(dve, lambda e: e.tensor_tensor_scan(out=BB.ap[:, 0:NN], data0=M64.ap[:, 0:NN], data1=LF.ap[:, 0:NN], initial=0.0,
                                                     op0=ALU.mult, op1=ALU.add), r=[M64, LF], w=[BB])
            yield
            bb4 = v3(BB.ap[:, 0:NN], 64)
            tot = bb4[:, :, 63:64]
            nchk = NN // 64
            totb = tot.broadcast_to([128, nchk, 64])
            P.do(act, lambda e: e.activation(out=DEC.ap[:, 0:nchk], in_=bb4[:, :, 63], func=AF.Exp), r=[BB], w=[DEC])
            yield
            if bwd:
                P.do(dve, lambda e: e.tensor_tensor(out=LF.ap[:, 0:NN], in0=LF.ap[:, 0:NN], in1=BB.ap[:, 0:NN], op=ALU.subtract), r=[LF, BB], w=[LF])
                yield
                P.do(dve, lambda e: e.tensor_tensor(out=v3(LF.ap[:, 0:NN], 64), in0=v3(LF.ap[:, 0:NN], 64), in1=totb, op=ALU.add), r=[LF, BB], w=[LF])
                yield
                Bcur = LF
            else:
                Bcur = BB
            P.do(act, lambda e: e.activation(out=EE.ap[:, 0:NN], in_=Bcur.ap[:, 0:NN], func=AF.Exp), r=[Bcur], w=[EE])
            yield
            P.do(dve, lambda e: e.scalar_tensor_tensor(out=QT_.ap[:, 0:NN], in0=QF.ap[:, 0:NN], scalar=HGRN_SCALE, in1=EE.ap[:, 0:NN],
                                                       op0=ALU.mult, op1=ALU.mult), r=[QF, EE], w=[QT_])
            yield
            P.do(act, lambda e: e.activation(out=EE.ap[:, 0:NN], in_=Bcur.ap[:, 0:NN], func=AF.Exp, scale=-1.0), r=[Bcur], w=[EE])
            yield
            P.do(pool, lambda e: e.tensor_tensor(out=KTL.ap[:, 0:NN], in0=FF.ap[:, 0:NN], in1=EE.ap[:, 0:NN], op=ALU.mult), r=[FF, EE], w=[KTL])
            yield
            P.do(pool, lambda e: e.tensor_tensor(out=v3(QF.ap[:, 0:NN], 64), in0=totb, in1=v3(Bcur.ap[:, 0:NN], 64), op=ALU.subtract), r=[Bcur, BB], w=[QF])
            yield
            P.do(act, lambda e: e.activation(out=EE.ap[:, 0:NN], in_=QF.ap[:, 0:NN], func=AF.Exp), r=[QF], w=[EE])
            yield
            P.do(dve, lambda e: e.tensor_tensor(out=KE.ap[:, 0:NN], in0=FF.ap[:, 0:NN], in1=EE.ap[:, 0:NN], op=ALU.mult), r=[FF, EE], w=[KE])
            yield

        def stageC(it):
            (t0, n, s) = order[it]
            X = XB[it % 2]; U = U2[it % 2]; VV = VV2[it % 2]; OFt = QF2[it % 2]
            x3, u3 = tctx.pop(it)
            NN = 8 * n
            nchk = NN // 64
            qt3 = v3(QT_.ap, n); kt3 = v3(KTL.ap, n); ke3 = v3(KE.ap, n); oo3 = v3(OO.ap, n)
            vv4 = VV.ap.rearrange("p (c h d) -> p c h d", h=8, d=128)
            dec3 = v3(DEC.ap[:, 0:nchk], n // 64)
            chunks = list(range(n // 64))
            if bwd:
                chunks = chunks[::-1]
            kets, atms = {}, {}

            def stage1(ci):
                c = chunks[ci]
                cs = slice(c * 64, (c + 1) * 64)
                ket = KET[ci % 2]; atm = ATM[ci % 2]
                kets[c], atms[c] = ket, atm
                ptr = nb()
                ptb = ptr.ap.bitcast(BF16)
                for h in range(8):
                    P.do(pe, lambda e, h=h, cs=cs, ptb=ptb: e.transpose(ptb[0:64, h * 128:(h + 1) * 128], ke3[:, h, cs], IDB.ap),
                         r=[KE, IDB], w=[ptr], inc=(h == 7))
                P.do(act, lambda e, ptb=ptb, ket=ket: e.activation(out=ket.ap[0:64, 0:1024], in_=ptb[0:64, 0:1024], func=AF.Copy), r=[ptr], w=[ket])
                if s == 0:
                    pat = nb()
                    for h in range(8):
                        P.do(pe, lambda e, pat=pat, h=h, cs=cs: e.matmul(pat.ap[0:64, h * 64:(h + 1) * 64], kt3[:, h, cs], qt3[:, h, cs], start=True, stop=True),
                             r=[KTL, QT_], w=[pat], inc=(h == 7))
                    P.do(dve, lambda e, pat=pat, atm=atm: e.tensor_tensor(out=v3(atm.ap[0:64, 0:512], 64), in0=v3(pat.ap[0:64, 0:512], 64),
                                                                          in1=mask_ap.unsqueeze(1).broadcast_to([64, 8, 64]), op=ALU.mult),
                         r=[pat, MASK], w=[atm])

            stage1(0)
            if len(chunks) > 1:
                stage1(1)

            def emit_ds(c):
                ket = kets[c]
                pd = [nb(), nb()]
                for h in range(8):
                    P.do(pe, lambda e, pd=pd, ket=ket, c=c, h=h: e.matmul(pd[h // 4].ap[:, (h % 4) * 128:(h % 4 + 1) * 128], ket.ap[0:64, h * 128:(h + 1) * 128],
                                                                          vv4[0:64, c, h, :], start=True, stop=True),
                         r=[ket, VV], w=[pd[h // 4]], inc=(h % 4 == 3))
                return pd

            pds = {chunks[0]: emit_ds(chunks[0])}
            for ci, c in enumerate(chunks):
                cs = slice(c * 64, (c + 1) * 64)
                if ci + 1 < len(chunks):
                    pds[chunks[ci + 1]] = emit_ds(chunks[ci + 1])
                stb_cur = STBS[stb_i[0] % 2]; stb_nxt = STBS[(stb_i[0] + 1) % 2]
                stb_i[0] += 1
                if s == 0:
                    atm = atms[c]
                    po = nb()
                    stc3 = v3(stb_cur.ap, 128)
                    for h in range(8):
                        P.do(pe, lambda e, po=po, atm=atm, c=c, h=h: e.matmul(po.ap[:, h * 64:(h + 1) * 64], vv4[0:64, c, h, :], atm.ap[0:64, h * 64:(h + 1) * 64],
                                                                              start=True, stop=False), r=[VV, atm], w=[po], inc=False)
                        P.do(pe, lambda e, po=po, h=h, cs=cs, stc3=stc3: e.matmul(po.ap[:, h * 64:(h + 1) * 64], stc3[:, h, :], qt3[:, h, cs], start=False, stop=True),
                             r=[stb_cur, QT_], w=[po], inc=(h == 7))
                    P.do(act, lambda e, po=po, cs=cs: e.activation(out=oo3[:, :, cs], in_=v3(po.ap[:, 0:512], 64), func=AF.Copy), r=[po], w=[OO])
                pd = pds[c]
                P.do(dve, lambda e, c=c: e.tensor_tensor(out=st3, in0=st3, in1=dec3[:, :, c:c + 1].broadcast_to([128, 8, 128]), op=ALU.mult),
                     r=[ST, DEC], w=[ST])
                for half in range(2):
                    P.do(dve, lambda e, half=half, pd=pd: e.tensor_tensor(out=ST.ap[:, half * 512:(half + 1) * 512], in0=ST.ap[:, half * 512:(half + 1) * 512],
                                                                          in1=pd[half].ap[:, 0:512], op=ALU.add), r=[ST, pd[half]], w=[ST])
                P.do(pool, lambda e, stb_nxt=stb_nxt: e.tensor_copy(out=stb_nxt.ap, in_=ST.ap), r=[ST], w=[stb_nxt])
                if ci + 2 < len(chunks):
                    stage1(ci + 2)
            if s == 1:
                return
            if not bwd:
                P.store(sp, OO, oo3, OF.rearrange("(h p) t -> p h t", p=128)[:, :, t0 - NCTX:t0 - NCTX + n])
                return
            P.load(sp, OFt, v3(OFt.ap, n), OF.rearrange("(h p) t -> p h t", p=128)[:, :, t0 - NCTX:t0 - NCTX + n])
            P.do(dve, lambda e: e.tensor_tensor(out=OO.ap[:, 0:NN], in0=OO.ap[:, 0:NN], in1=OFt.ap[:, 0:NN], op=ALU.add), r=[OO, OFt], w=[OO])
            P.store(sp, OO, oo3, OF.rearrange("(h p) t -> p h t", p=128)[:, :, t0 - NCTX:t0 - NCTX + n])

        stageA(0)
        for it in range(len(order)):
            gen = stageB(it)
            if it + 1 < len(order):
                stageA(it + 1, gen)
            for _ in gen:
                pass
            stageC(it)
        if not bwd:
            P.store(sp, ST, ST.ap, SX[:, :])

    def mix1_out(src, dst, TT=512):
        A.reset()
        WG = A.bf16(KC * D, "owg"); wg3 = v3(WG.ap, D)
        for kc in range(KC):
            P.load(pool, WG, wg3[:, kc, :], o_win[kc * 128:(kc + 1) * 128, 4 * D:5 * D])
        WO = A.bf16(KC * D, "owo"); wo3 = v3(WO.ap, D)
        load_w_bf16(WO, wo3, o_wout, KC)
        XB = [A.f32(KC * TT, "xa"), A.f32(KC * TT, "xb")]
        U2 = [A.bf16(KC * TT, "u0"), A.bf16(KC * TT, "u1")]; SQ = A.bf16(KC * TT, "sq")
        RS = A.f32(TT, "rs")
        TMP = [A.f32(TT, "t0"), A.f32(TT, "t1")]
        TMP2 = [A.f32(TT, "t2"), A.f32(TT, "t3")]
        OO2 = [A.f32(8 * TT, "oo0"), A.f32(8 * TT, "oo1")]
        KE2 = [A.bf16(8 * TT, "osq0"), A.bf16(8 * TT, "osq1")]; RR = A.bf16(8 * TT, "rr")
        SG2 = [A.f32(8 * TT, "sgall0"), A.f32(8 * TT, "sgall1")]
        tl = tiles_of(TT, False)
        tctx = {}

        def stageP(it, gen=None):
            (t0, n, s) = tl[it]
            X = XB[it % 2]; OO = OO2[it % 2]; U = U2[it % 2]; KE = KE2[it % 2]; SGA_ = SG2[it % 2]
            NN = 8 * n
            oo3 = v3(OO.ap, n); ke3 = v3(KE.ap, n)
            P.load(sp, OO, oo3, OF.rearrange("(h p) t -> p h t", p=128)[:, :, t0 - NCTX:t0 - NCTX + n])
            x3, u3 = norm_mod(X, U, SQ, RS, TMP, n, s, 1, src, t0)
            P.do(act, lambda e: e.activation(out=KE.ap[:, 0:NN], in_=OO.ap[:, 0:NN], func=AF.Square), r=[OO], w=[KE])
            sg3 = v3(SGA_.ap, n)
            tctx[it] = (x3, u3)
            for h in range(8):
                pg = nb()
                for kc in range(KC):
                    P.do(pe, lambda e, kc=kc, h=h, pg=pg: e.matmul(pg.ap[:, 0:n], wg3[:, kc, h * 128:(h + 1) * 128], u3[:, kc, :],
                                                                   start=(kc == 0), stop=(kc == KC - 1)), r=[WG, U], w=[pg], inc=(kc == KC - 1))
                P.do(act, lambda e, pg=pg, h=h: e.activation(out=sg3[:, h, :], in_=pg.ap[:, 0:n], func=AF.Silu), r=[pg], w=[SGA_])
                if gen is not None:
                    next(gen, None)

        def stageR(it):
            (t0, n, s) = tl[it]
            X = XB[it % 2]; OO = OO2[it % 2]; KE = KE2[it % 2]; SGA_ = SG2[it % 2]
            oo3 = v3(OO.ap, n); ke3 = v3(KE.ap, n); sg3 = v3(SGA_.ap, n); rr3 = v3(RR.ap, n)
            for h in range(8):
                pss = nb()
                P.do(pe, lambda e, pss=pss, h=h: e.matmul(pss.ap[:, 0:n], ONESB.ap, ke3[:, h, :], start=True, stop=True), r=[KE, ONESB], w=[pss])
                T = TMP2[h % 2]
                P.do(act, lambda e, pss=pss, T=T: e.activation(out=T.ap[:, 0:n], in_=pss.ap[:, 0:n], func=AF.Ln, bias=EPSB.ap[:, 0:1], scale=1.0 / 128),
                     r=[pss, EPSB], w=[T])
                P.do(act, lambda e, T=T: e.activation(out=T.ap[:, 0:n], in_=T.ap[:, 0:n], func=AF.Exp, scale=-0.5), r=[T], w=[T])
                P.do(dve, lambda e, T=T, h=h: e.scalar_tensor_tensor(out=oo3[:, h, :], in0=oo3[:, h, :], scalar=GN.ap[:, 0:1], in1=T.ap[:, 0:n],
                                                                     op0=ALU.mult, op1=ALU.mult), r=[OO, GN, T], w=[OO])
                P.do(pool, lambda e, h=h: e.tensor_tensor(out=rr3[:, h, :], in0=oo3[:, h, :], in1=sg3[:, h, :], op=ALU.mult), r=[SGA_, OO], w=[RR])
                yield

        def stageW(it):
            (t0, n, s) = tl[it]
            X = XB[it % 2]
            x3, u3 = tctx.pop(it)
            rr3 = v3(RR.ap, n)
            g5 = hgvec(s, 1)
            for m in range(KC):
                po = nb()
                for h in range(8):
                    P.do(pe, lambda e, po=po, h=h, m=m: e.matmul(po.ap[:, 0:n], wo3[:, h, m * 128:(m + 1) * 128], rr3[:, h, :],
                                                                 start=(h == 0), stop=(h == 7)), r=[WO, RR], w=[po], inc=(h == 7))
                P.do(dve, lambda e, po=po, m=m: e.scalar_tensor_tensor(out=x3[:, m, :], in0=po.ap[:, 0:n], scalar=g5[:, m:m + 1],
                                                                       in1=x3[:, m, :], op0=ALU.mult, op1=ALU.add),
                     r=[po, HG, X], w=[X])
            P.store(sp, X, x3, dst.rearrange("(k p) t -> p k t", p=128)[:, :, t0:t0 + n])

        stageP(0)
        for it in range(len(tl)):
            gen = stageR(it)
            if it + 1 < len(tl):
                stageP(it + 1, gen)
            for _ in gen:
                pass
            stageW(it)


    def mix1_exchange():
        A.reset()
        allgather(SX, SGA)

    steps = []
    steps.append(lambda: ada_phase(0))
    steps.append(lambda: ffn_phase(0, 0, xin, XT))
    steps.append(lambda: mix0_proj(XT))
    steps.append(lambda: mix0_exchange())
    steps.append(lambda: mix0_attn())
    steps.append(lambda: mix0_out(XT, XT))
    steps.append(lambda: ffn_phase(0, 2, XT, XT))
    steps.append(lambda: ada_phase(1))
    steps.append(lambda: ffn_phase(1, 0, XT, XT))
    steps.append(lambda: mix1_dir(XT, XT, 0))
    steps.append(lambda: mix1_exchange())
    steps.append(lambda: mix1_dir(XT, XT, 1))
    steps.append(lambda: mix1_out(XT, XT))
    steps.append(lambda: ffn_phase(1, 2, XT, XT, with_ctx=False))
    steps.append(lambda: final_phase(XT))
    for i, st in enumerate(steps):
        if i >= stop_after:
            break
        st()
    return P.finish()


def _rope_tables(pos):
    row = (pos // 64).astype(np.float32)
    col = (pos % 64).astype(np.float32)
    inv = (10000.0 ** (-np.arange(8, dtype=np.float32) / 8)).astype(np.float32)
    ang = np.stack([row[:, None] * inv, col[:, None] * inv], axis=1)
    cos, sin = np.cos(ang).astype(np.float32), np.sin(ang).astype(np.float32)
    C = np.ones((96, NT), np.float32)
    S = np.zeros((96, NT), np.float32)
    for ax in range(2):
        for half in range(2):
            r0 = 64 + ax * 16 + half * 8
            C[r0:r0 + 8, NCTX:] = cos[:, ax, :].T
            S[r0:r0 + 8, NCTX:] = (-sin[:, ax, :].T) if half == 0 else sin[:, ax, :].T
    return C, S


_PERM = np.concatenate([np.arange(8, 16), np.arange(0, 8), np.arange(24, 32), np.arange(16, 24)])

_NC_CACHE = {}


def _prep_inputs(inp):
    f = lambda a: np.ascontiguousarray(np.asarray(a, dtype=np.float32))
    x, c, ctx, c_ctx = f(inp["x"]), f(inp["c"]), f(inp["ctx"]), f(inp["c_ctx"])
    ewin = f(inp["even_w_in"])[0]
    wuq = f(inp["mla_w_uq"])[0].reshape(256, 8, 96)
    wukv = f(inp["mla_w_ukv"])[0].reshape(128, 8, 128)
    wkr = np.zeros((D, 96), np.float32); wkr[:, 64:] = ewin[:, 1920:1952]
    wkrp = np.zeros((D, 96), np.float32); wkrp[:, 64:] = ewin[:, 1920:1952][:, _PERM]
    wqb = np.zeros((256, 8, 96), np.float32); wqb[:, :, 64:] = wuq[:, :, 64:][:, :, _PERM]
    wkn = np.zeros((128, 8, 96), np.float32); wkn[:, :, :64] = wukv[:, :, :64]
    wv = np.ascontiguousarray(wukv[:, :, 64:]).reshape(128, 512)
    tri = np.triu(np.ones((64, 64), np.float32))
    cmask = np.concatenate([tri, tri.T], axis=1)
    conv = f(inp["even_conv_w"])[0]
    owin = f(inp["odd_w_in"])[0].reshape(D, 5, D)
    olb = f(inp["hgrn_lb_logits"])
    shared = {
        "ada_w": f(inp["ada_w"]), "ada_b": f(inp["ada_b"]), "norm_g": f(inp["norm_g"]),
        "ffn_w1": f(inp["ffn_w1"]), "ffn_w3": f(inp["ffn_w3"]), "ffn_w2": f(inp["ffn_w2"]),
        "e_win": np.ascontiguousarray(ewin[:, :1920]), "e_wkr": wkr, "e_wkrp": wkrp,
        "e_qg": f(inp["mla_q_norm_g"])[0],
        "e_wqa": np.ascontiguousarray(wuq).reshape(256, 768), "e_wqb": wqb.reshape(256, 768),
        "e_kvg": f(inp["mla_kv_norm_g"])[0], "e_wkn": wkn.reshape(128, 768), "e_wv": wv,
        "e_wout": f(inp["even_w_out"])[0],
        "o_gn": f(inp["hgrn_g_norm_g"])[0],
        "o_wout": f(inp["odd_w_out"])[0], "fin_g": f(inp["final_norm_g"]),
        "cmask": cmask, "ident": np.eye(128, dtype=np.float32),
    }
    per_half = []
    for s in range(2):
        pos = np.arange(SEQ_L) if s == 0 else (SEQ - 1 - np.arange(SEQ_L))
        C, S = _rope_tables(pos)
        d1, d2 = (0, 1) if s == 0 else (1, 0)
        owin_p = np.ascontiguousarray(owin[:, [0, 1, 2 + d1, 2 + d2, 4], :]).reshape(D, 5 * D)
        olb_p = np.ascontiguousarray(olb[:, [d1, d2], :])
        conv_p = np.ascontiguousarray(conv if s == 0 else conv[::-1])
        sel = np.zeros((128, 2), np.float32); sel[:, 1 - s] = 1.0
        per_half.append({"ropeC": C, "ropeS": S, "o_win": owin_p, "o_lb": olb_p, "e_conv": conv_p, "selv": sel})
    maps = []
    for core in range(NCORES):
        b, s = core // 2, core % 2
        m = dict(shared)
        m.update(per_half[s])
        if s == 0:
            loc = np.concatenate([ctx[b], x[b, :SEQ_L]], axis=0)
        else:
            loc = np.concatenate([ctx[b][::-1], x[b, SEQ_L:][::-1]], axis=0)
        m["xin"] = np.ascontiguousarray(loc.T)
        cv = np.stack([c[b], c_ctx], axis=1)
        m["cvec"] = np.ascontiguousarray(cv.reshape(KC, 128, 2).transpose(1, 0, 2))
        maps.append(m)
    return maps


def kernel(**inputs):
    maps = _prep_inputs(inputs)
    if "nc" not in _NC_CACHE:
        _NC_CACHE["nc"] = build()
    res = run_bass_kernel_spmd(_NC_CACHE["nc"], maps, core_ids=list(range(NCORES)))
    out = np.empty((NCORES // 2, SEQ, D), np.float32)
    for core in range(NCORES):
        b, s = core // 2, core % 2
        y = res.results[core]["yout"].T
        if s == 0:
            out[b, :SEQ_L] = y
        else:
            out[b, SEQ_L:] = y[::-1]
    return out
```

```python
import numpy as np
from contextlib import ExitStack
import concourse.bass as bass
import concourse.mybir as mybir
from concourse.bass_utils import run_bass_kernel_spmd

F32 = mybir.dt.float32
BF16 = mybir.dt.bfloat16
AF = mybir.ActivationFunctionType
ALU = mybir.AluOpType

D = 1024
NCTX = 256
SEQ = 8192
SEQ_L = SEQ // 2
NT = NCTX + SEQ_L
NKEY = NCTX + SEQ
PAIRS = [[0, 1], [2, 3], [4, 5], [6, 7]]
DFF = 2816
NJ = DFF // 128
KC = 8
EPS = 1e-6
NCORES = 8
MLA_SCALE = 96 ** -0.5
HGRN_SCALE = 128 ** -0.5


import types


def _freeze(fn):
    if fn.__closure__ is None:
        return fn
    cells = []
    for c in fn.__closure__:
        try:
            cells.append(types.CellType(c.cell_contents))
        except ValueError:
            cells.append(c)
    g = types.FunctionType(fn.__code__, fn.__globals__, fn.__name__, fn.__defaults__, tuple(cells))
    g.__kwdefaults__ = fn.__kwdefaults__
    return g


class Tok:
    __slots__ = ("sem", "val")

    def __init__(self, sem, val):
        self.sem, self.val = sem, val


class DmaSem:
    def __init__(self, P, name):
        self.sem = P.new_sem(name)
        self.count = 0

    def tok(self):
        return Tok(self.sem, self.count)


class Buf:
    def __init__(self, ap, name=""):
        self.ap, self.name = ap, name
        self.w = {}
        self.r = {}
        self.ds = None


class Eng:
    def __init__(self, P, name):
        self.P, self.name = P, name
        self.ops = []
        self.sem = P.new_sem("p_" + name)
        self.count = 0
        self.waited = {}

    def _wait(self, toks):
        need = {}
        for t in toks:
            k = id(t.sem)
            if t.val > need.get(k, (None, 0))[1]:
                need[k] = (t.sem, t.val)
        for k, (sem, val) in need.items():
            if self.waited.get(k, 0) >= val:
                continue
            self.waited[k] = val
            self.ops.append(("wait", sem, val))

    def emit(self, fn, toks, inc=True):
        self._wait(toks)
        if inc:
            self.count += 1
            self.ops.append(("op", fn, self.sem, 1))
            return Tok(self.sem, self.count)
        self.ops.append(("op", fn, None, 0))
        return None

    def replay(self, e):
        for o in self.ops:
            if o[0] == "wait":
                e.wait_ge(o[1], o[2])
            else:
                ins = o[1](e)
                if o[2] is not None:
                    ins.then_inc(o[2], o[3])


class Prog:
    def __init__(self):
        self.nc = bass.Bass("TRN2", target_bir_lowering=False)
        self.es = ExitStack()
        self.nsem = 0
        self.pe = Eng(self, "pe")
        self.act = Eng(self, "act")
        self.dve = Eng(self, "dve")
        self.pool = Eng(self, "pool")
        self.sp = Eng(self, "sp")
        self.engs = [self.pe, self.act, self.dve, self.pool, self.sp]
        self.dsems = []
        self.free_ds = []
        self.pending_pe = []

    def new_sem(self, name):
        self.nsem += 1
        return self.es.enter_context(self.nc.semaphore(name + str(self.nsem)))

    def get_ds(self):
        if self.free_ds:
            return self.free_ds.pop()
        d = DmaSem(self, "d")
        self.dsems.append(d)
        return d

    def sb(self, name, shape, dt):
        return self.es.enter_context(self.nc.sbuf_tensor(name, list(shape), dt))

    def din(self, name, shape, dt=F32):
        return self.nc.dram_tensor(name, list(shape), dt, kind="ExternalInput").ap()

    def dout(self, name, shape, dt=F32):
        return self.nc.dram_tensor(name, list(shape), dt, kind="ExternalOutput").ap()

    def dint(self, name, shape, dt=F32):
        return self.nc.dram_tensor(name, list(shape), dt, kind="Internal").ap()

    def do(self, eng, fn, r=(), w=(), inc=True):
        fn = _freeze(fn)
        toks = []
        for b in r:
            toks.extend(b.w.values())
        for b in w:
            toks.extend(b.w.values())
            toks.extend(b.r.values())
        if eng is self.pe and not inc:
            eng._wait(toks)
            eng.ops.append(("op", fn, None, 0))
            self.pending_pe.append((r, w))
            return None
        tok = eng.emit(fn, toks, True)
        groups = [(r, w)]
        if eng is self.pe:
            groups += self.pending_pe
            self.pending_pe = []
        for (rr, ww) in groups:
            for b in rr:
                b.r[id(tok.sem)] = tok
            for b in ww:
                b.w = {id(tok.sem): tok}
                b.r = {}
        return tok

    def load(self, q, buf, dst_ap, src_ap, slow=False):
        kw = {"allow_slow_non_contiguous": True} if slow else {}
        if buf.ds is None:
            buf.ds = self.get_ds()
        ds = buf.ds
        toks = [t for t in buf.w.values() if t.sem is not ds.sem] + list(buf.r.values())
        q._wait(toks)
        ds.count += 16
        q.ops.append(("op", lambda e: e.dma_start(out=dst_ap, in_=src_ap, **kw), ds.sem, 16))
        tok = ds.tok()
        buf.w = {id(ds.sem): tok}
        buf.r = {}
        return tok

    def store(self, q, buf, src_ap, dst_ap, slow=False):
        kw = {"allow_slow_non_contiguous": True} if slow else {}
        if buf.ds is None:
            buf.ds = self.get_ds()
        ds = buf.ds
        q._wait(list(buf.w.values()))
        ds.count += 16
        q.ops.append(("op", lambda e: e.dma_start(out=dst_ap, in_=src_ap, **kw), ds.sem, 16))
        tok = ds.tok()
        buf.r[id(ds.sem)] = tok
        return tok

    def barrier(self):
        assert not self.pending_pe
        toks = [Tok(e.sem, e.count) for e in self.engs[:4] if e.count > 0]
        toks += [d.tok() for d in self.dsems if d.count > 0]
        for e in self.engs:
            e._wait(toks)

    def finish(self):
        self.barrier()
        with self.nc.Block() as block:
            @block.tensor
            def _(e):
                self.pe.replay(e)

            @block.scalar
            def _(e):
                self.act.replay(e)

            @block.vector
            def _(e):
                self.dve.replay(e)

            @block.gpsimd
            def _(e):
                self.pool.replay(e)

            @block.sync
            def _(e):
                self.sp.replay(e)
        self.es.close()
        return self.nc


class Arena:
    def __init__(self, P, ncols):
        self.P = P
        self.t = P.sb("arena", [128, ncols], F32)
        self.n = ncols
        self.off = 0
        self.bufs = []

    def reset(self):
        self.P.barrier()
        for b in self.bufs:
            if b.ds is not None:
                self.P.free_ds.append(b.ds)
                b.ds = None
        self.bufs = []
        self.off = 0

    def f32(self, ncols, name=""):
        ncols = (ncols + 7) // 8 * 8
        assert self.off + ncols <= self.n, f"arena overflow {name} {self.off}+{ncols}>{self.n}"
        ap = self.t[:, self.off:self.off + ncols]
        self.off += ncols
        b = Buf(ap, name)
        self.bufs.append(b)
        return b

    def bf16(self, ncols, name=""):
        b = self.f32((ncols + 1) // 2, name)
        b.ap = b.ap.bitcast(BF16)
        return b


def v3(ap, inner):
    return ap.rearrange("p (a b) -> p a b", b=inner)


def tiles_of(TT, with_ctx=True):
    tl = [(0, NCTX, 1)] if with_ctx else []
    for i in range(SEQ_L // TT):
        tl.append((NCTX + i * TT, TT, 0))
    return tl


def build(stop_after=99, debug=False):
    P = Prog()
    nc = P.nc
    xin = P.din("xin", [D, NT])
    cvec = P.din("cvec", [128, KC, 2])
    ada_w = P.din("ada_w", [2, D, 9 * D])
    ada_b = P.din("ada_b", [2, 9 * D])
    norm_g = P.din("norm_g", [2, 3, D])
    ffn_w1 = P.din("ffn_w1", [2, 2, D, DFF])
    ffn_w3 = P.din("ffn_w3", [2, 2, D, DFF])
    ffn_w2 = P.din("ffn_w2", [2, 2, DFF, D])
    e_win = P.din("e_win", [D, 1920])
    e_wkr = P.din("e_wkr", [D, 96])
    e_wkrp = P.din("e_wkrp", [D, 96])
    e_conv = P.din("e_conv", [3, 512])
    e_qg = P.din("e_qg", [256])
    e_wqa = P.din("e_wqa", [256, 8 * 96])
    e_wqb = P.din("e_wqb", [256, 8 * 96])
    e_kvg = P.din("e_kvg", [128])
    e_wkn = P.din("e_wkn", [128, 8 * 96])
    e_wv = P.din("e_wv", [128, 512])
    e_wout = P.din("e_wout", [D, D])
    ropeC = P.din("ropeC", [96, NT])
    ropeS = P.din("ropeS", [96, NT])
    o_win = P.din("o_win", [D, 5 * D])
    o_lb = P.din("o_lb", [2, 2, D])
    o_gn = P.din("o_gn", [128])
    o_wout = P.din("o_wout", [D, D])
    fin_g = P.din("fin_g", [D])
    cmask = P.din("cmask", [64, 128])
    ident = P.din("ident", [128, 128])
    yout = P.dout("yout", [D, SEQ_L])
    selv = P.din("selv", [128, 2])
    mk = P.dout if debug else P.dint
    XT = mk("XT", [D, NT])
    GB = P.dint("GB", [512, NT], BF16)
    CV = P.dint("CV", [512, NT + 3])
    KCX = P.dint("KCX", [768, NCTX], BF16)
    KL = [P.dint(f"KL{i}", [192, SEQ_L], BF16) for i in range(4)]
    KG = [P.dint(f"KG{i}", [384, SEQ_L], BF16) for i in range(4)]
    VCX = P.dint("VCX", [NCTX, 512], BF16)
    VL = [P.dint(f"VL{i}", [SEQ_L // 2, 512], BF16) for i in range(2)]
    VG = [P.dint(f"VG{i}", [SEQ_L, 512], BF16) for i in range(2)]
    CVH = P.dint("CVH", [512, 8])
    CVG = P.dint("CVG", [1024, 8])
    SX = P.dint("SX", [128, 1024])
    SGA = P.dint("SGA", [256, 1024])
    QT = P.dint("QT", [8, 96, NT], BF16)
    BT = mk("BT", [512, NT], BF16) if not debug else P.dout("BT", [512, NT], BF16)
    OF = P.dint("OF", [D, SEQ_L])

    def cv_col(t):
        return 1 + t if t < NCTX else 2 + t

    cst = P.sb("cst", [128, 1024], F32)
    c_off = [0]

    def cbuf(n, name=""):
        ap = cst[:, c_off[0]:c_off[0] + n]
        c_off[0] += n
        assert c_off[0] <= 1024
        return Buf(ap, name)

    CVEC = cbuf(16)
    SC = cbuf(16)
    MOD = cbuf(144)
    ADAB = cbuf(72)
    NG = cbuf(48)
    G = cbuf(48)
    HG = cbuf(48)
    FING = cbuf(8)
    EPSB = cbuf(1)
    CONVW = cbuf(12)
    QG = cbuf(2)
    KVG = cbuf(1)
    LBL = cbuf(32)
    LB = cbuf(16)
    OML = cbuf(16)
    GN = cbuf(1)
    ZERO = cbuf(4)
    SEL = cbuf(2)
    ONESB = Buf(P.sb("onesb", [128, 128], BF16)[:], "ones")
    IDB = Buf(P.sb("idb", [128, 128], BF16)[:], "idb")
    MASK = Buf(P.sb("maskt", [64, 128], F32)[:], "mask")
    psall = P.es.enter_context(nc.psum_tensor("psall", [128, 4096], F32))
    banks = [Buf(psall[:, i * 512:(i + 1) * 512], f"bank{i}") for i in range(8)]
    bank_rr = [0]

    def nb():
        b = banks[bank_rr[0] % 8]
        bank_rr[0] += 1
        return b

    A = Arena(P, 50 * 1024)
    sp, pe, act, dve, pool = P.sp, P.pe, P.act, P.dve, P.pool

    P.load(sp, CVEC, v3(CVEC.ap, 2), cvec[:, :, :])
    for l_ in range(2):
        for j_ in range(3):
            P.load(sp, NG, NG.ap[:, (l_ * 3 + j_) * 8:(l_ * 3 + j_ + 1) * 8], norm_g[l_, j_].rearrange("(k p) -> p k", p=128), slow=True)
    P.load(sp, FING, FING.ap, fin_g.rearrange("(k p) -> p k", p=128), slow=True)
    for c_ in range(4):
        P.load(sp, CONVW, CONVW.ap[:, c_ * 3:(c_ + 1) * 3], e_conv[:, c_ * 128:(c_ + 1) * 128].rearrange("w p -> p w"), slow=True)
    P.load(sp, QG, QG.ap, e_qg.rearrange("(k p) -> p k", p=128), slow=True)
    P.load(sp, KVG, KVG.ap, e_kvg.rearrange("(k p) -> p k", p=128), slow=True)
    for l_ in range(2):
        for d_ in range(2):
            P.load(sp, LBL, LBL.ap[:, (l_ * 2 + d_) * 8:(l_ * 2 + d_ + 1) * 8], o_lb[l_, d_].rearrange("(h p) -> p h", p=128), slow=True)
    P.load(sp, GN, GN.ap, o_gn.rearrange("(k p) -> p k", p=128), slow=True)
    P.load(sp, MASK, MASK.ap, cmask[:, :])
    P.load(sp, SEL, SEL.ap, selv[:, :])
    P.load(pool, IDB, IDB.ap, ident[:, :])
    P.do(dve, lambda e: e.memset(EPSB.ap, EPS), w=[EPSB])
    P.do(dve, lambda e: e.memset(ZERO.ap, 0.0), w=[ZERO])
    P.do(dve, lambda e: e.memset(ONESB.ap, 1.0), w=[ONESB])
    P.do(dve, lambda e: e.tensor_tensor(out=LB.ap, in0=LBL.ap[:, 16:32], in1=LBL.ap[:, 0:16], op=ALU.subtract), r=[LBL], w=[LB])
    P.do(act, lambda e: e.activation(out=LB.ap, in_=LB.ap, func=AF.Sigmoid), r=[LB], w=[LB])
    P.do(dve, lambda e: e.tensor_scalar(out=OML.ap, in0=LB.ap, scalar1=-1.0, scalar2=1.0, op0=ALU.mult, op1=ALU.add), r=[LB], w=[OML])
    P.do(act, lambda e: e.activation(out=SC.ap, in_=CVEC.ap, func=AF.Silu), r=[CVEC], w=[SC])

    def allgather(src, dst):
        ds = P.get_ds()
        ds.count += 1
        pool.ops.append(("op", lambda e: e.collective_compute("AllGather", ALU.bypass, replica_groups=PAIRS,
                                                              ins=[src.opt()], outs=[dst.opt()]), ds.sem, 1))

    def ada_phase(l):
        A.reset()
        P.load(sp, ADAB, ADAB.ap, ada_b[l].rearrange("(j p) -> p j", p=128), slow=True)
        wb = [A.bf16(KC * 1024, f"adaw{i}") for i in range(3)]
        pb = nb()
        SCB = A.bf16(16, "scb")
        P.do(dve, lambda e: e.tensor_copy(out=SCB.ap, in_=SC.ap), r=[SC], w=[SCB])
        sc3 = v3(SCB.ap, 2)
        for j in range(9):
            W = wb[j % 3]
            W3 = v3(W.ap, 1024)
            for kc in range(KC):
                P.load(pool, W, W3[:, kc, :], ada_w[l, kc * 128:(kc + 1) * 128, j * 1024:(j + 1) * 1024])
            for oc in range(8):
                col = (j * 8 + oc) * 2
                for kc in range(KC):
                    P.do(pe, lambda e, W3=W3, kc=kc, oc=oc, col=col: e.matmul(
                        pb.ap[:, col:col + 2], W3[:, kc, oc * 128:(oc + 1) * 128], sc3[:, kc, :],
                        start=(kc == 0), stop=(kc == KC - 1)), r=[W, SCB], w=[pb], inc=(kc == KC - 1))
        ps3 = v3(pb.ap[:, 0:144], 2)
        mod3 = v3(MOD.ap, 72)
        for s in range(2):
            P.do(dve, lambda e, s=s: e.tensor_tensor(out=mod3[:, s, :], in0=ps3[:, :, s], in1=ADAB.ap, op=ALU.add),
                 r=[pb, ADAB], w=[MOD])
        for s in range(2):
            m4 = mod3[:, s, :].rearrange("p (jj t k) -> p jj t k", t=3, k=8)
            g3 = v3(G.ap, 24)[:, s, :].rearrange("p (jj k) -> p jj k", k=8)
            h3 = v3(HG.ap, 24)[:, s, :].rearrange("p (jj k) -> p jj k", k=8)
            ng3 = v3(NG.ap, 24)[:, l, :].rearrange("p (jj k) -> p jj k", k=8)
            P.do(dve, lambda e, m4=m4, g3=g3, ng3=ng3: e.scalar_tensor_tensor(
                out=g3, in0=m4[:, :, 1, :], scalar=1.0, in1=ng3, op0=ALU.add, op1=ALU.mult), r=[MOD, NG], w=[G])
            P.do(dve, lambda e, m4=m4, h3=h3: e.tensor_scalar(
                out=h3, in0=m4[:, :, 2, :], scalar1=0.5, scalar2=None, op0=ALU.mult), r=[MOD], w=[HG])
            P.do(dve, lambda e, m4=m4, h3=h3: e.tensor_copy(out=h3[:, 1, :], in_=m4[:, 1, 2, :]), r=[MOD], w=[HG])

    def modv(s, j):
        return v3(MOD.ap, 72)[:, s, j * 8:(j + 1) * 8]

    def gvec(s, jj):
        return v3(G.ap, 24)[:, s, jj * 8:(jj + 1) * 8]

    def hgvec(s, jj):
        return v3(HG.ap, 24)[:, s, jj * 8:(jj + 1) * 8]

    def rstd_from_ps(pb, n, RS, dim, lnexp=False):
        if lnexp:
            P.do(act, lambda e: e.activation(out=RS.ap[:, 0:n], in_=pb.ap[:, 0:n], func=AF.Ln, bias=EPSB.ap[:, 0:1],
                                             scale=1.0 / dim), r=[pb, EPSB], w=[RS])
            P.do(act, lambda e: e.activation(out=RS.ap[:, 0:n], in_=RS.ap[:, 0:n], func=AF.Exp, scale=-0.5), r=[RS], w=[RS])
            return
        P.do(act, lambda e: e.activation(out=RS.ap[:, 0:n], in_=pb.ap[:, 0:n], func=AF.Sqrt, bias=EPSB.ap[:, 0:1],
                                         scale=1.0 / dim), r=[pb, EPSB], w=[RS])
        P.do(dve, lambda e: e.reciprocal(out=RS.ap[:, 0:n], in_=RS.ap[:, 0:n]), r=[RS], w=[RS])

    def norm_mod(X, U, SQ, RS, TMP, n, s, jj, src, t0, lnexp=False):
        x3 = v3(X.ap[:, 0:KC * n], n)
        u3 = v3(U.ap[:, 0:KC * n], n)
        sq3 = v3(SQ.ap[:, 0:KC * n], n)
        P.load(sp, X, x3, src.rearrange("(k p) t -> p k t", p=128)[:, :, t0:t0 + n])
        P.do(act, lambda e: e.activation(out=SQ.ap[:, 0:KC * n], in_=X.ap[:, 0:KC * n], func=AF.Square), r=[X], w=[SQ])
        pb = nb()
        for kc in range(KC):
            P.do(pe, lambda e, kc=kc: e.matmul(pb.ap[:, 0:n], ONESB.ap, sq3[:, kc, :], start=(kc == 0), stop=(kc == KC - 1)),
                 r=[SQ, ONESB], w=[pb], inc=(kc == KC - 1))
        rstd_from_ps(pb, n, RS, D, lnexp)
        gv = gvec(s, jj)
        sh = modv(s, 3 * jj)
        for kc in range(KC):
            T = TMP[kc % 2]
            P.do(dve, lambda e, kc=kc, T=T: e.tensor_tensor(out=T.ap[:, 0:n], in0=x3[:, kc, :], in1=RS.ap[:, 0:n], op=ALU.mult),
                 r=[X, RS], w=[T])
            P.do(act, lambda e, kc=kc, T=T: e.activation(out=u3[:, kc, :], in_=T.ap[:, 0:n], func=AF.Identity,
                                                         bias=sh[:, kc:kc + 1], scale=gv[:, kc:kc + 1]),
                 r=[T, G, MOD], w=[U])
        return x3, u3

    def load_w_bf16(buf, view3, dram2d, nk, rows_per=128):
        for k in range(nk):
            P.load(pool, buf, view3[:, k, :], dram2d[k * rows_per:(k + 1) * rows_per, :])

    def ffn_phase(l, jj, src, dst, with_ctx=True, TT=512):
        wi = 0 if jj == 0 else 1
        A.reset()
        HJ = NJ // 2 * 128
        W1 = [A.bf16(KC * HJ, "w1a"), A.bf16(KC * HJ, "w1b")]
        W3 = [A.bf16(KC * HJ, "w3a"), A.bf16(KC * HJ, "w3b")]
        W2 = A.bf16(NJ * D, "w2")
        w13 = [v3(W1[i].ap, HJ) for i in range(2)]; w33 = [v3(W3[i].ap, HJ) for i in range(2)]; w23 = v3(W2.ap, D)
        for i in range(2):
            for kc in range(KC):
                P.load(pool, W1[i], w13[i][:, kc, :], ffn_w1[l, wi, kc * 128:(kc + 1) * 128, i * HJ:(i + 1) * HJ])
            for kc in range(KC):
                P.load(pool, W3[i], w33[i][:, kc, :], ffn_w3[l, wi, kc * 128:(kc + 1) * 128, i * HJ:(i + 1) * HJ])
        load_w_bf16(W2, w23, ffn_w2[l, wi], NJ)
        XB = [A.f32(KC * TT, "xa"), A.f32(KC * TT, "xb")]
        U = A.bf16(KC * TT, "u")
        H = A.bf16(NJ * TT, "h")
        RS = A.f32(TT, "rs")
        TMP = [A.f32(TT, "t0"), A.f32(TT, "t1")]
        SA = TMP
        for it, (t0, n, s) in enumerate(tiles_of(TT, with_ctx)):
            X = XB[it % 2]
            h3 = v3(H.ap[:, 0:NJ * n], n)
            x3, u3 = norm_mod(X, U, H, RS, TMP, n, s, jj, src, t0)
            for j in range(NJ):
                pa = nb(); pbk = nb()
                for kc in range(KC):
                    P.do(pe, lambda e, pa=pa, j=j, kc=kc: e.matmul(pa.ap[:, 0:n], w13[j // 11][:, kc, (j % 11) * 128:(j % 11 + 1) * 128], u3[:, kc, :],
                                                                   start=(kc == 0), stop=(kc == KC - 1)),
                         r=[W1[j // 11], U], w=[pa], inc=(kc == KC - 1))
                for kc in range(KC):
                    P.do(pe, lambda e, pbk=pbk, j=j, kc=kc: e.matmul(pbk.ap[:, 0:n], w33[j // 11][:, kc, (j % 11) * 128:(j % 11 + 1) * 128], u3[:, kc, :],
                                                                     start=(kc == 0), stop=(kc == KC - 1)),
                         r=[W3[j // 11], U], w=[pbk], inc=(kc == KC - 1))
                S_ = SA[j % 2]
                P.do(act, lambda e, pa=pa, S_=S_: e.activation(out=S_.ap[:, 0:n], in_=pa.ap[:, 0:n], func=AF.Silu), r=[pa], w=[S_])
                P.do(dve, lambda e, pbk=pbk, S_=S_, j=j: e.tensor_tensor(out=h3[:, j, :], in0=pbk.ap[:, 0:n], in1=S_.ap[:, 0:n], op=ALU.mult),
                     r=[pbk, S_], w=[H])
            hg = hgvec(s, jj)
            for m in range(KC):
                po = nb()
                for j in range(NJ):
                    P.do(pe, lambda e, po=po, j=j, m=m: e.matmul(po.ap[:, 0:n], w23[:, j, m * 128:(m + 1) * 128], h3[:, j, :],
                                                                 start=(j == 0), stop=(j == NJ - 1)),
                         r=[W2, H], w=[po], inc=(j == NJ - 1))
                P.do(dve, lambda e, po=po, m=m: e.scalar_tensor_tensor(out=x3[:, m, :], in0=po.ap[:, 0:n], scalar=hg[:, m:m + 1],
                                                                       in1=x3[:, m, :], op0=ALU.mult, op1=ALU.add),
                     r=[po, HG, X], w=[X])
            P.store(sp, X, x3, dst.rearrange("(k p) t -> p k t", p=128)[:, :, t0:t0 + n])

    def final_phase(src, TT=512):
        A.reset()
        XB = [A.f32(KC * TT, "xa"), A.f32(KC * TT, "xb")]
        SQ = A.bf16(KC * TT, "sq")
        RS = A.f32(TT, "rs")
        for it, (t0, n, s) in enumerate(tiles_of(TT, False)):
            X = XB[it % 2]
            x3 = v3(X.ap[:, 0:KC * n], n)
            sq3 = v3(SQ.ap[:, 0:KC * n], n)
            P.load(sp, X, x3, src.rearrange("(k p) t -> p k t", p=128)[:, :, t0:t0 + n])
            P.do(act, lambda e, X=X: e.activation(out=SQ.ap[:, 0:KC * n], in_=X.ap[:, 0:KC * n], func=AF.Square), r=[X], w=[SQ])
            pb = nb()
            for kc in range(KC):
                P.do(pe, lambda e, pb=pb, kc=kc, sq3=sq3: e.matmul(pb.ap[:, 0:n], ONESB.ap, sq3[:, kc, :], start=(kc == 0), stop=(kc == KC - 1)),
                     r=[SQ, ONESB], w=[pb], inc=(kc == KC - 1))
            rstd_from_ps(pb, n, RS, D)
            for kc in range(KC):
                P.do(dve, lambda e, kc=kc, x3=x3: e.scalar_tensor_tensor(out=x3[:, kc, :], in0=x3[:, kc, :], scalar=FING.ap[:, kc:kc + 1],
                                                                        in1=RS.ap[:, 0:n], op0=ALU.mult, op1=ALU.mult),
                     r=[X, RS, FING], w=[X])
            P.store(sp, X, x3, yout.rearrange("(k p) t -> p k t", p=128)[:, :, t0 - NCTX:t0 - NCTX + n])

    def mix0_proj(src, TT=512):
        A.reset()
        WI = A.bf16(KC * 1920, "win"); wi3 = v3(WI.ap, 1920)
        load_w_bf16(WI, wi3, e_win, KC)
        WKR = A.bf16(KC * 96, "wkr"); wkr3 = v3(WKR.ap, 96)
        load_w_bf16(WKR, wkr3, e_wkr, KC)
        WKRP = A.bf16(KC * 96, "wkrp"); wkrp3 = v3(WKRP.ap, 96)
        load_w_bf16(WKRP, wkrp3, e_wkrp, KC)
        WQA = A.bf16(2 * 768, "wqa"); wqa3 = v3(WQA.ap, 768)
        load_w_bf16(WQA, wqa3, e_wqa, 2)
        WQB = A.bf16(2 * 768, "wqb"); wqb3 = v3(WQB.ap, 768)
        load_w_bf16(WQB, wqb3, e_wqb, 2)
        WKN = A.bf16(768, "wkn")
        P.load(pool, WKN, WKN.ap, e_wkn[:, :])
        WV = A.bf16(512, "wv")
        P.load(pool, WV, WV.ap, e_wv[:, :])
        XB = [A.f32(KC * TT, "xa"), A.f32(KC * TT, "xb")]
        U = A.bf16(KC * TT, "u"); SQ = A.bf16(KC * TT, "sq")
        RS = A.f32(TT, "rs")
        TMP = [A.f32(TT, "t0"), A.f32(TT, "t1")]
        RC = A.f32(TT, "ropec"); RSN = A.f32(TT, "ropes")
        GBt = [A.bf16(TT, "gb0"), A.bf16(TT, "gb1")]
        CVt = [A.f32(TT, "cv0"), A.f32(TT, "cv1")]
        CQ = A.f32(2 * TT, "cq"); NQ = A.bf16(2 * TT, "nq"); SQQ = A.bf16(2 * TT, "sqq")
        CKV = A.f32(TT, "ckv"); NKV = A.bf16(TT, "nkv"); SQK = A.bf16(TT, "sqk")
        RSQ = A.f32(TT, "rsq")
        ROT = A.f32(TT, "rot")
        T1 = [A.f32(TT, "q1a"), A.f32(TT, "q1b")]
        T2 = [A.f32(TT, "q2a"), A.f32(TT, "q2b")]
        QO = [A.bf16(TT, "qo0"), A.bf16(TT, "qo1")]
        KO = [A.bf16(TT, "ko0"), A.bf16(TT, "ko1")]
        VO = [A.bf16(512, "vo0"), A.bf16(512, "vo1")]
        ZT = A.f32(512, "zt")
        P.do(dve, lambda e: e.memset(ZT.ap[:, 0:4], 0.0), w=[ZT])
        for c in range(4):
            for col in (0, NCTX + 1):
                P.store(sp, ZT, ZT.ap[:, 0:1], CV[c * 128:(c + 1) * 128, col:col + 1], slow=True)
        for it, (t0, n, s) in enumerate(tiles_of(TT, True)):
            X = XB[it % 2]
            x3, u3 = norm_mod(X, U, SQ, RS, TMP, n, s, 1, src, t0)
            P.load(sp, RC, RC.ap[0:96, 0:n], ropeC[:, t0:t0 + n])
            P.load(sp, RSN, RSN.ap[0:96, 0:n], ropeS[:, t0:t0 + n])

            def proj(col0, ncols, pb, W3=wi3, Wb=WI):
                for kc in range(KC):
                    P.do(pe, lambda e, kc=kc: e.matmul(pb.ap[0:ncols, 0:n], W3[:, kc, col0:col0 + ncols], u3[:, kc, :],
                                                       start=(kc == 0), stop=(kc == KC - 1)),
                         r=[Wb, U], w=[pb], inc=(kc == KC - 1))

            cq3 = v3(CQ.ap[:, 0:2 * n], n); nq3 = v3(NQ.ap[:, 0:2 * n], n); sqq3 = v3(SQQ.ap[:, 0:2 * n], n)
            for i in range(2):
                pq = nb(); proj(1536 + i * 128, 128, pq)
                P.do(act, lambda e, pq=pq, i=i: e.activation(out=cq3[:, i, :], in_=pq.ap[:, 0:n], func=AF.Copy), r=[pq], w=[CQ])
            P.do(act, lambda e: e.activation(out=SQQ.ap[:, 0:2 * n], in_=CQ.ap[:, 0:2 * n], func=AF.Square), r=[CQ], w=[SQQ])
            pss = nb()
            for i in range(2):
                P.do(pe, lambda e, i=i: e.matmul(pss.ap[:, 0:n], ONESB.ap, sqq3[:, i, :], start=(i == 0), stop=(i == 1)),
                     r=[SQQ, ONESB], w=[pss], inc=(i == 1))
            rstd_from_ps(pss, n, RSQ, 256)
            for i in range(2):
                T = TMP[i % 2]
                P.do(dve, lambda e, i=i, T=T: e.tensor_tensor(out=T.ap[:, 0:n], in0=cq3[:, i, :], in1=RSQ.ap[:, 0:n], op=ALU.mult),
                     r=[CQ, RSQ], w=[T])
                P.do(act, lambda e, i=i, T=T: e.activation(out=nq3[:, i, :], in_=T.ap[:, 0:n], func=AF.Identity, scale=QG.ap[:, i:i + 1]),
                     r=[T, QG], w=[NQ])
            pk = nb(); proj(1792, 128, pk)
            P.do(act, lambda e, pk=pk: e.activation(out=CKV.ap[:, 0:n], in_=pk.ap[:, 0:n], func=AF.Copy), r=[pk], w=[CKV])
            P.do(act, lambda e: e.activation(out=SQK.ap[:, 0:n], in_=CKV.ap[:, 0:n], func=AF.Square), r=[CKV], w=[SQK])
            pss = nb()
            P.do(pe, lambda e, pss=pss: e.matmul(pss.ap[:, 0:n], ONESB.ap, SQK.ap[:, 0:n], start=True, stop=True), r=[SQK, ONESB], w=[pss])
            rstd_from_ps(pss, n, RSQ, 128)
            T = TMP[0]
            P.do(dve, lambda e, T=T: e.tensor_tensor(out=T.ap[:, 0:n], in0=CKV.ap[:, 0:n], in1=RSQ.ap[:, 0:n], op=ALU.mult), r=[CKV, RSQ], w=[T])
            P.do(act, lambda e, T=T: e.activation(out=NKV.ap[:, 0:n], in_=T.ap[:, 0:n], func=AF.Identity, scale=KVG.ap[:, 0:1]), r=[T, KVG], w=[NKV])
            pr = nb(); proj(0, 96, pr, wkr3, WKR)
            prp = nb(); proj(0, 96, prp, wkrp3, WKRP)
            t1 = T1[0]; t2 = T2[0]
            P.do(dve, lambda e, pr=pr, t1=t1: e.tensor_tensor(out=t1.ap[0:96, 0:n], in0=pr.ap[0:96, 0:n], in1=RC.ap[0:96, 0:n], op=ALU.mult),
                 r=[pr, RC], w=[t1])
            P.do(dve, lambda e, prp=prp, t2=t2: e.tensor_tensor(out=t2.ap[0:96, 0:n], in0=prp.ap[0:96, 0:n], in1=RSN.ap[0:96, 0:n], op=ALU.mult),
                 r=[prp, RSN], w=[t2])
            P.do(pool, lambda e, t1=t1, t2=t2: e.tensor_tensor(out=ROT.ap[0:96, 0:n], in0=t1.ap[0:96, 0:n], in1=t2.ap[0:96, 0:n], op=ALU.add),
                 r=[t1, t2], w=[ROT])
            for c in range(4):
                pg = nb(); proj(c * 128, 128, pg)
                gbt = GBt[c % 2]
                P.do(act, lambda e, pg=pg, gbt=gbt: e.activation(out=gbt.ap[:, 0:n], in_=pg.ap[:, 0:n], func=AF.Copy), r=[pg], w=[gbt])
                P.store(sp, gbt, gbt.ap[:, 0:n], GB[c * 128:(c + 1) * 128, t0:t0 + n])
                pc = nb(); proj(512 + c * 128, 128, pc)
                pv = nb(); proj(1024 + c * 128, 128, pv)
                T = TMP[c % 2]
                cvt = CVt[c % 2]
                P.do(act, lambda e, pc=pc, T=T: e.activation(out=T.ap[:, 0:n], in_=pc.ap[:, 0:n], func=AF.Copy), r=[pc], w=[T])
                P.do(dve, lambda e, pv=pv, T=T, cvt=cvt: e.tensor_tensor(out=cvt.ap[:, 0:n], in0=pv.ap[:, 0:n], in1=T.ap[:, 0:n], op=ALU.mult),
                     r=[pv, T], w=[cvt])
                cc = cv_col(t0)
                P.store(sp, cvt, cvt.ap[:, 0:n], CV[c * 128:(c + 1) * 128, cc:cc + n])
                if t0 + n == NT:
                    P.store(sp, cvt, cvt.ap[:, n - 1:n], CVH[c * 128:(c + 1) * 128, 0:1], slow=True)
            for h in range(8):
                pa = nb(); pbk = nb()
                for i in range(2):
                    P.do(pe, lambda e, i=i, h=h, pa=pa: e.matmul(pa.ap[0:96, 0:n], wqa3[:, i, h * 96:(h + 1) * 96], nq3[:, i, :],
                                                                 start=(i == 0), stop=(i == 1)), r=[WQA, NQ], w=[pa], inc=(i == 1))
                for i in range(2):
                    P.do(pe, lambda e, i=i, h=h, pbk=pbk: e.matmul(pbk.ap[0:96, 0:n], wqb3[:, i, h * 96:(h + 1) * 96], nq3[:, i, :],
                                                                   start=(i == 0), stop=(i == 1)), r=[WQB, NQ], w=[pbk], inc=(i == 1))
                t1 = T1[h % 2]; t2 = T2[h % 2]; qo = QO[h % 2]
                P.do(dve, lambda e, pa=pa, t1=t1: e.tensor_tensor(out=t1.ap[0:96, 0:n], in0=pa.ap[0:96, 0:n], in1=RC.ap[0:96, 0:n], op=ALU.mult),
                     r=[pa, RC], w=[t1])
                P.do(dve, lambda e, pbk=pbk, t2=t2: e.tensor_tensor(out=t2.ap[0:96, 0:n], in0=pbk.ap[0:96, 0:n], in1=RSN.ap[0:96, 0:n], op=ALU.mult),
                     r=[pbk, RSN], w=[t2])
                P.do(pool, lambda e, t1=t1, t2=t2, qo=qo: e.tensor_tensor(out=qo.ap[0:96, 0:n], in0=t1.ap[0:96, 0:n], in1=t2.ap[0:96, 0:n], op=ALU.add),
                     r=[t1, t2], w=[qo])
                P.store(sp, qo, qo.ap[0:96, 0:n], QT[h, :, t0:t0 + n])
            for h in range(8):
                pkh = nb()
                P.do(pe, lambda e, pkh=pkh, h=h: e.matmul(pkh.ap[0:96, 0:n], WKN.ap[:, h * 96:(h + 1) * 96], NKV.ap[:, 0:n], start=True, stop=True),
                     r=[WKN, NKV], w=[pkh])
                ko = KO[h % 2]
                P.do(dve, lambda e, pkh=pkh, ko=ko: e.tensor_tensor(out=ko.ap[0:96, 0:n], in0=pkh.ap[0:96, 0:n], in1=ROT.ap[0:96, 0:n], op=ALU.add),
                     r=[pkh, ROT], w=[ko])
                if s == 1:
                    P.store(sp, ko, ko.ap[0:96, 0:n], KCX[h * 96:(h + 1) * 96, t0:t0 + n])
                else:
                    P.store(sp, ko, ko.ap[0:96, 0:n], KL[h // 2][(h % 2) * 96:(h % 2 + 1) * 96, t0 - NCTX:t0 - NCTX + n])
            for tb in range(n // 128):
                pvv = nb()
                P.do(pe, lambda e, pvv=pvv, tb=tb: e.matmul(pvv.ap[:, 0:512], NKV.ap[:, tb * 128:(tb + 1) * 128], WV.ap, start=True, stop=True),
                     r=[NKV, WV], w=[pvv])
                vo = VO[tb % 2]
                P.do(act, lambda e, pvv=pvv, vo=vo: e.activation(out=vo.ap, in_=pvv.ap, func=AF.Copy), r=[pvv], w=[vo])
                if s == 1:
                    P.store(sp, vo, vo.ap, VCX[t0 + tb * 128:t0 + (tb + 1) * 128, :])
                else:
                    tl_ = t0 - NCTX + tb * 128
                    P.store(sp, vo, vo.ap, VL[tl_ // 2048][tl_ % 2048:tl_ % 2048 + 128, :])

    def mix0_exchange():
        A.reset()
        for i in range(4):
            allgather(KL[i], KG[i])
        for i in range(2):
            allgather(VL[i], VG[i])
        allgather(CVH, CVG)
        A.reset()
        HB = A.f32(64, "hb"); HO = A.f32(8, "ho")
        hb3 = v3(HB.ap, 8)
        P.load(sp, HB, hb3, CVG.rearrange("(r p) k -> p r k", p=128))
        P.do(dve, lambda e: e.tensor_scalar(out=HO.ap[:, 0:4], in0=hb3[:, 0:4, 0], scalar1=SEL.ap[:, 0:1], scalar2=None, op0=ALU.mult), r=[HB, SEL], w=[HO])
        P.do(dve, lambda e: e.scalar_tensor_tensor(out=HO.ap[:, 0:4], in0=hb3[:, 4:8, 0], scalar=SEL.ap[:, 1:2], in1=HO.ap[:, 0:4],
                                                   op0=ALU.mult, op1=ALU.add), r=[HB, SEL, HO], w=[HO])
        for c in range(4):
            P.store(sp, HO, HO.ap[:, c:c + 1], CV[c * 128:(c + 1) * 128, NT + 2:NT + 3], slow=True)

    def mix0_attn():
        A.reset()
        NKT = NKEY // 128
        KH = [A.bf16(NKEY, "kh0"), A.bf16(NKEY, "kh1")]
        QH = [A.bf16(NT, "qh0"), A.bf16(NT, "qh1")]
        VA = [A.bf16(NKT * 128, "va0"), A.bf16(NKT * 128, "va1")]
        PT = [[A.bf16(512, f"pt{i}a"), A.bf16(512, f"pt{i}b")] for i in range(3)]
        RCP = A.f32(1024, "rcp")
        BO = [A.bf16(1024, "bo0"), A.bf16(1024, "bo1")]
        SP_ = [[Buf(psall[:, (2 * i + j) * 512:(2 * i + j + 1) * 512], f"s{i}{j}") for j in range(2)] for i in range(2)]
        OP_ = [[Buf(psall[:, (4 + 2 * i + j) * 512:(4 + 2 * i + j + 1) * 512], f"o{i}{j}") for j in range(2)] for i in range(2)]
        for i in range(2):
            va3 = v3(VA[i].ap, 128)
            P.do(dve, lambda e, va3=va3: e.memset(va3[:, :, 64:128], 1.0), w=[VA[i]])
        it = 0
        qtiles = [(0, NCTX, 2)] + [(NCTX + i * 1024, 1024, NKT) for i in range(SEQ_L // 1024)]
        for h in range(8):
            K_ = KH[h % 2]; Q_ = QH[h % 2]; V_ = VA[h % 2]
            va3 = v3(V_.ap, 128)
            P.load(sp, K_, K_.ap[0:96, 0:NCTX], KCX[h * 96:(h + 1) * 96, :])
            for r_ in range(2):
                P.load(sp, K_, K_.ap[0:96, NCTX + r_ * SEQ_L:NCTX + (r_ + 1) * SEQ_L],
                       KG[h // 2][r_ * 192 + (h % 2) * 96:r_ * 192 + (h % 2 + 1) * 96, :])
            P.load(sp, Q_, Q_.ap[0:96, :], QT[h, :, :])
            P.load(sp, V_, va3[:, 0:2, 0:64], VCX[:, h * 64:(h + 1) * 64].rearrange("(kt p) d -> p kt d", p=128))
            for r_ in range(2):
                for j_ in range(2):
                    k0 = 2 + r_ * 32 + j_ * 16
                    P.load(sp, V_, va3[:, k0:k0 + 16, 0:64],
                           VG[j_][r_ * 2048:(r_ + 1) * 2048, h * 64:(h + 1) * 64].rearrange("(kt p) d -> p kt d", p=128))
            for (q0, nq, nkt) in qtiles:
                O_ = OP_[it % 2]
                halves = [(0, min(512, nq))] + ([(512, 512)] if nq > 512 else [])

                def emit_qk(kt):
                    for hi, (c0, cn) in enumerate(halves):
                        S_ = SP_[kt % 2][hi]
                        P.do(pe, lambda e, S_=S_, kt=kt, c0=c0, cn=cn: e.matmul(
                            S_.ap[:, 0:cn], K_.ap[0:96, kt * 128:(kt + 1) * 128], Q_.ap[0:96, q0 + c0:q0 + c0 + cn],
                            start=True, stop=True), r=[K_, Q_], w=[S_])

                emit_qk(0)
                for kt in range(nkt):
                    if kt + 1 < nkt:
                        emit_qk(kt + 1)
                    for hi, (c0, cn) in enumerate(halves):
                        S_ = SP_[kt % 2][hi]; pt = PT[kt % 3][hi]
                        P.do(act, lambda e, S_=S_, pt=pt, cn=cn: e.activation(out=pt.ap[:, 0:cn], in_=S_.ap[:, 0:cn], func=AF.Exp, scale=MLA_SCALE),
                             r=[S_], w=[pt])
                    for hi, (c0, cn) in enumerate(halves):
                        pt = PT[kt % 3][hi]; Oh = O_[hi]
                        P.do(pe, lambda e, Oh=Oh, kt=kt, cn=cn, pt=pt: e.matmul(
                            Oh.ap[:, 0:cn], va3[:, kt, :], pt.ap[:, 0:cn],
                            start=(kt == 0), stop=(kt == nkt - 1)), r=[V_, pt], w=[Oh])
                bo = BO[it % 2]
                for hi, (c0, cn) in enumerate(halves):
                    Oh = O_[hi]
                    P.do(dve, lambda e, Oh=Oh, c0=c0, cn=cn: e.reciprocal(out=RCP.ap[64:128, c0:c0 + cn], in_=Oh.ap[64:128, 0:cn]), r=[Oh], w=[RCP])
                    P.do(dve, lambda e, Oh=Oh, bo=bo, c0=c0, cn=cn: e.tensor_tensor(out=bo.ap[0:64, c0:c0 + cn], in0=Oh.ap[0:64, 0:cn], in1=RCP.ap[64:128, c0:c0 + cn], op=ALU.mult),
                         r=[Oh, RCP], w=[bo])
                P.store(sp, bo, bo.ap[0:64, 0:nq], BT[h * 64:(h + 1) * 64, q0:q0 + nq])
                it += 1

    def mix0_out(src, dst, TT=512):
        A.reset()
        WO = A.bf16(KC * D, "wo"); wo3 = v3(WO.ap, D)
        load_w_bf16(WO, wo3, e_wout, KC)
        XB = [A.f32(KC * TT, "xa"), A.f32(KC * TT, "xb")]
        GBt = [A.bf16(4 * TT, "gb0"), A.bf16(4 * TT, "gb1")]
        CVw = [A.f32(4 * (TT + 2), "cvw0"), A.f32(4 * (TT + 2), "cvw1")]
        BTt = [A.bf16(4 * TT, "bt0"), A.bf16(4 * TT, "bt1")]
        ACC = [A.f32(TT, "acc0"), A.f32(TT, "acc1")]
        AT = A.bf16(4 * TT, "at")
        cw3 = v3(CONVW.ap, 3)
        for it, (t0, n, s) in enumerate(tiles_of(TT, True)):
            X = XB[it % 2]; gb = GBt[it % 2]; cvw = CVw[it % 2]; bt = BTt[it % 2]
            x3 = v3(X.ap[:, 0:KC * n], n)
            gb3 = v3(gb.ap[:, 0:4 * n], n); cv3 = v3(cvw.ap[:, 0:4 * (n + 2)], n + 2); bt3 = v3(bt.ap[:, 0:4 * n], n)
            at3 = v3(AT.ap[:, 0:4 * n], n)
            P.load(sp, X, x3, src.rearrange("(k p) t -> p k t", p=128)[:, :, t0:t0 + n])
            P.load(sp, gb, gb3, GB.rearrange("(c p) t -> p c t", p=128)[:, :, t0:t0 + n])
            cc = cv_col(t0)
            P.load(sp, cvw, cv3, CV.rearrange("(c p) t -> p c t", p=128)[:, :, cc - 1:cc + n + 1])
            P.load(sp, bt, bt3, BT.rearrange("(c p) t -> p c t", p=128)[:, :, t0:t0 + n])
            for c in range(4):
                acc = ACC[c % 2]
                P.do(dve, lambda e, c=c, acc=acc: e.tensor_scalar(out=acc.ap[:, 0:n], in0=cv3[:, c, 0:n], scalar1=cw3[:, c, 0:1], scalar2=None, op0=ALU.mult),
                     r=[cvw, CONVW], w=[acc])
                P.do(dve, lambda e, c=c, acc=acc: e.scalar_tensor_tensor(out=acc.ap[:, 0:n], in0=cv3[:, c, 1:n + 1], scalar=cw3[:, c, 1:2], in1=acc.ap[:, 0:n],
                                                                          op0=ALU.mult, op1=ALU.add), r=[cvw, CONVW, acc], w=[acc])
                P.do(dve, lambda e, c=c, acc=acc: e.scalar_tensor_tensor(out=acc.ap[:, 0:n], in0=cv3[:, c, 2:n + 2], scalar=cw3[:, c, 2:3], in1=acc.ap[:, 0:n],
                                                                          op0=ALU.mult, op1=ALU.add), r=[cvw, CONVW, acc], w=[acc])
                P.do(dve, lambda e, c=c, acc=acc: e.tensor_tensor(out=at3[:, c, :], in0=acc.ap[:, 0:n], in1=gb3[:, c, :], op=ALU.mult),
                     r=[acc, gb], w=[AT])
            g5 = hgvec(s, 1)
            for m in range(KC):
                po = nb()
                for c in range(8):
                    rhs = at3[:, c, :] if c < 4 else bt3[:, c - 4, :]
                    P.do(pe, lambda e, po=po, c=c, m=m, rhs=rhs: e.matmul(po.ap[:, 0:n], wo3[:, c, m * 128:(m + 1) * 128], rhs,
                                                                         start=(c == 0), stop=(c == 7)),
                         r=[WO, AT, bt], w=[po], inc=(c == 7))
                P.do(dve, lambda e, po=po, m=m: e.scalar_tensor_tensor(out=x3[:, m, :], in0=po.ap[:, 0:n], scalar=g5[:, m:m + 1],
                                                                       in1=x3[:, m, :], op0=ALU.mult, op1=ALU.add),
                     r=[po, HG, X], w=[X])
            P.store(sp, X, x3, dst.rearrange("(k p) t -> p k t", p=128)[:, :, t0:t0 + n])

    def mix1_dir(src, dst, direction, TT=256):
        bwd = direction == 1
        A.reset()
        ncols = 3
        WIN = A.bf16(KC * ncols * D, "owin"); win3 = v3(WIN.ap, ncols * D)
        for blk, srcblk in enumerate([0, 1, 2 + direction]):
            for kc in range(KC):
                P.load(pool, WIN, win3[:, kc, blk * D:(blk + 1) * D], o_win[kc * 128:(kc + 1) * 128, srcblk * D:(srcblk + 1) * D])
        NCH = TT // 64
        XB = [A.f32(KC * TT, "xa"), A.f32(KC * TT, "xb")]
        U2 = [A.bf16(KC * TT, "u0"), A.bf16(KC * TT, "u1")]
        SQn = A.bf16(KC * TT, "sqn")
        RS = A.f32(TT, "rs")
        TMP = [A.f32(TT, "t0"), A.f32(TT, "t1")]
        TMP2 = [A.f32(TT, "t2"), A.f32(TT, "t3")]
        QF2 = [A.f32(8 * TT, "qf0"), A.f32(8 * TT, "qf1")]; FF2 = [A.f32(8 * TT, "ff0"), A.f32(8 * TT, "ff1")]
        LF = A.f32(8 * TT, "lf"); BB = A.f32(8 * TT, "bb"); EE = A.f32(8 * TT, "ee")
        QT_ = A.bf16(8 * TT, "qt"); KTL = A.bf16(8 * TT, "ktl"); KE = A.bf16(8 * TT, "ke")
        VV2 = [A.bf16(NCH * 8 * 128, "vv0"), A.bf16(NCH * 8 * 128, "vv1")]
        OO = A.f32(8 * TT, "oo")
        DEC = A.f32(8 * NCH, "dec")
        M64 = A.f32(8 * TT, "m64")
        ST = A.f32(8 * 128, "st"); STBS = [A.bf16(8 * 128, "stb0"), A.bf16(8 * 128, "stb1")]
        KET = [A.bf16(1024, f"ket{i}") for i in range(2)]
        ATM = [A.bf16(512, f"atm{i}") for i in range(2)]
        stb_i = [0]; k_it = [0]
        st3 = v3(ST.ap, 128)
        P.do(dve, lambda e: e.memset(M64.ap, 1.0), w=[M64])
        P.do(dve, lambda e: e.memset(v3(M64.ap, 64)[:, :, 0:1], 0.0), w=[M64])
        mask_ap = MASK.ap[:, 64:128] if bwd else MASK.ap[:, 0:64]
        tl = tiles_of(TT, True)
        ctx_tiles = [t for t in tl if t[2] == 1]
        lat_tiles = [t for t in tl if t[2] == 0]
        order = (ctx_tiles + lat_tiles) if not bwd else lat_tiles[::-1]
        if not bwd:
            P.do(dve, lambda e: e.memset(ST.ap, 0.0), w=[ST])
            P.do(dve, lambda e: e.memset(STBS[0].ap, 0.0), w=[STBS[0]])
        else:
            P.load(sp, ST, ST.ap, SGA[0:128, :])
            P.load(sp, OO, OO.ap[:, 0:1024], SGA[128:256, :])
            P.do(dve, lambda e: e.tensor_scalar(out=ST.ap, in0=ST.ap, scalar1=SEL.ap[:, 0:1], scalar2=None, op0=ALU.mult), r=[ST, SEL], w=[ST])
            P.do(dve, lambda e: e.scalar_tensor_tensor(out=ST.ap, in0=OO.ap[:, 0:1024], scalar=SEL.ap[:, 1:2], in1=ST.ap, op0=ALU.mult, op1=ALU.add),
                 r=[OO, SEL, ST], w=[ST])
            P.do(pool, lambda e: e.tensor_copy(out=STBS[0].ap, in_=ST.ap), r=[ST], w=[STBS[0]])
        lbv = v3(LB.ap, 8)[:, direction, :]
        omlv = v3(OML.ap, 8)[:, direction, :]
        tctx = {}

        def stageA(it, gen=None):
            (t0, n, s) = order[it]
            X = XB[it % 2]; U = U2[it % 2]; QF = QF2[it % 2]; FF = FF2[it % 2]; VV = VV2[it % 2]
            x3, u3 = norm_mod(X, U, SQn, RS, TMP, n, s, 1, src, t0, lnexp=True)
            qf3 = v3(QF.ap, n); ff3 = v3(FF.ap, n)
            vv4 = VV.ap.rearrange("p (c h d) -> p c h d", h=8, d=128)
            tctx[it] = (x3, u3)
            for h in range(8):
                pq = nb()
                for kc in range(KC):
                    P.do(pe, lambda e, kc=kc, h=h, pq=pq: e.matmul(pq.ap[:, 0:n], win3[:, kc, h * 128:(h + 1) * 128], u3[:, kc, :],
                                                                   start=(kc == 0), stop=(kc == KC - 1)), r=[WIN, U], w=[pq], inc=(kc == KC - 1))
                P.do(act, lambda e, pq=pq, h=h: e.activation(out=qf3[:, h, :], in_=pq.ap[:, 0:n], func=AF.Copy), r=[pq], w=[QF])
                pz = nb()
                for kc in range(KC):
                    P.do(pe, lambda e, kc=kc, h=h, pz=pz: e.matmul(pz.ap[:, 0:n], win3[:, kc, 2 * D + h * 128:2 * D + (h + 1) * 128], u3[:, kc, :],
                                                                   start=(kc == 0), stop=(kc == KC - 1)), r=[WIN, U], w=[pz], inc=(kc == KC - 1))
                P.do(act, lambda e, pz=pz, h=h: e.activation(out=ff3[:, h, :], in_=pz.ap[:, 0:n], func=AF.Exp, scale=-1.0), r=[pz], w=[FF])
                for c in range(n // 64):
                    pvv = nb()
                    for kc in range(KC):
                        P.do(pe, lambda e, kc=kc, h=h, c=c, pvv=pvv: e.matmul(pvv.ap[0:64, 0:128], u3[:, kc, c * 64:(c + 1) * 64],
                                                                              win3[:, kc, D + h * 128:D + (h + 1) * 128],
                                                                              start=(kc == 0), stop=(kc == KC - 1)),
                             r=[WIN, U], w=[pvv], inc=(kc == KC - 1))
                    P.do(act, lambda e, pvv=pvv, c=c, h=h: e.activation(out=vv4[0:64, c, h, :], in_=pvv.ap[0:64, 0:128], func=AF.Copy), r=[pvv], w=[VV])
                if gen is not None:
                    for _ in range(3):
                        next(gen, None)

        def stageB(it):
            (t0, n, s) = order[it]
            QF = QF2[it % 2]; FF = FF2[it % 2]
            ff3 = v3(FF.ap, n)
            NN = 8 * n
            P.do(dve, lambda e: e.tensor_scalar(out=FF.ap[:, 0:NN], in0=FF.ap[:, 0:NN], scalar1=1.0, scalar2=None, op0=ALU.add), r=[FF], w=[FF])
            yield
            P.do(dve, lambda e: e.reciprocal(out=FF.ap[:, 0:NN], in_=FF.ap[:, 0:NN]), r=[FF], w=[FF])
            yield
            P.do(dve, lambda e: e.tensor_tensor(out=ff3, in0=ff3, in1=omlv.unsqueeze(2).broadcast_to([128, 8, n]), op=ALU.mult), r=[FF, OML], w=[FF])
            yield
            P.do(dve, lambda e: e.tensor_tensor(out=ff3, in0=ff3, in1=lbv.unsqueeze(2).broadcast_to([128, 8, n]), op=ALU.add), r=[FF, LB], w=[FF])
            yield
            P.do(act, lambda e: e.activation(out=LF.ap[:, 0:NN], in_=FF.ap[:, 0:NN], func=AF.Ln), r=[FF], w=[LF])
            yield
            P.do(dve, lambda e: e.tensor_scalar(out=FF.ap[:, 0:NN], in0=FF.ap[:, 0:NN], scalar1=-1.0, scalar2=1.0, op0=ALU.mult, op1=ALU.add), r=[FF], w=[FF])
            yield
            P.do(dve, lambda e: e.tensor_tensor_scan(out=BB.ap[:, 0:NN], data0=M64.ap[:, 0:NN], data1=LF.ap[:, 0:NN], initial=0.0,
                                                     op0=ALU.mult, op1=ALU.add), r=[M64, LF], w=[BB])
            yield
            bb4 = v3(BB.ap[:, 0:NN], 64)
            tot = bb4[:, :, 63:64]
            nchk = NN // 64
            totb = tot.broadcast_to([128, nchk, 64])
            P.do(act, lambda e: e.activation(out=DEC.ap[:, 0:nchk], in_=bb4[:, :, 63], func=AF.Exp), r=[BB], w=[DEC])
            yield
            if bwd:
                P.do(dve, lambda e: e.tensor_tensor(out=LF.ap[:, 0:NN], in0=LF.ap[:, 0:NN], in1=BB.ap[:, 0:NN], op=ALU.subtract), r=[LF, BB], w=[LF])
                yield
                P.do(dve, lambda e: e.tensor_tensor(out=v3(LF.ap[:, 0:NN], 64), in0=v3(LF.ap[:, 0:NN], 64), in1=totb, op=ALU.add), r=[LF, BB], w=[LF])
                yield
                Bcur = LF
            else:
                Bcur = BB
            P.do(act, lambda e: e.activation(out=EE.ap[:, 0:NN], in_=Bcur.ap[:, 0:NN], func=AF.Exp), r=[Bcur], w=[EE])
            yield
            P.do(dve, lambda e: e.scalar_tensor_tensor(out=QT_.ap[:, 0:NN], in0=QF.ap[:, 0:NN], scalar=HGRN_SCALE, in1=EE.ap[:, 0:NN],
                                                       op0=ALU.mult, op1=ALU.mult), r=[QF, EE], w=[QT_])
            yield
            P.do(act, lambda e: e.activation(out=EE.ap[:, 0:NN], in_=Bcur.ap[:, 0:NN], func=AF.Exp, scale=-1.0), r=[Bcur], w=[EE])
            yield
            P.do(pool, lambda e: e.tensor_tensor(out=KTL.ap[:, 0:NN], in0=FF.ap[:, 0:NN], in1=EE.ap[:, 0:NN], op=ALU.mult), r=[FF, EE], w=[KTL])
            yield
            P.do(pool, lambda e: e.tensor_tensor(out=v3(QF.ap[:, 0:NN], 64), in0=totb, in1=v3(Bcur.ap[:, 0:NN], 64), op=ALU.subtract), r=[Bcur, BB], w=[QF])
            yield
            P.do(act, lambda e: e.activation(out=EE.ap[:, 0:NN], in_=QF.ap[:, 0:NN], func=AF.Exp), r=[QF], w=[EE])
            yield
            P.do(dve, lambda e: e.tensor_tensor(out=KE.ap[:, 0:NN], in0=FF.ap[:, 0:NN], in1=EE.ap[:, 0:NN], op=ALU.mult), r=[FF, EE], w=[KE])
            yield

        def stageC(it):
            (t0, n, s) = order[it]
            X = XB[it % 2]; U = U2[it % 2]; VV = VV2[it % 2]; OFt = QF2[it % 2]
            x3, u3 = tctx.pop(it)
            NN = 8 * n
            nchk = NN // 64
            qt3 = v3(QT_.ap, n); kt3 = v3(KTL.ap, n); ke3 = v3(KE.ap, n); oo3 = v3(OO.ap, n)
            vv4 = VV.ap.rearrange("p (c h d) -> p c h d", h=8, d=128)
            dec3 = v3(DEC.ap[:, 0:nchk], n // 64)
            chunks = list(range(n // 64))
            if bwd:
                chunks = chunks[::-1]
            kets, atms = {}, {}

            def stage1(ci):
                c = chunks[ci]
                cs = slice(c * 64, (c + 1) * 64)
                ket = KET[ci % 2]; atm = ATM[ci % 2]
                kets[c], atms[c] = ket, atm
                ptr = nb()
                ptb = ptr.ap.bitcast(BF16)
                for h in range(8):
                    P.do(pe, lambda e, h=h, cs=cs, ptb=ptb: e.transpose(ptb[0:64, h * 128:(h + 1) * 128], ke3[:, h, cs], IDB.ap),
                         r=[KE, IDB], w=[ptr], inc=(h == 7))
                P.do(act, lambda e, ptb=ptb, ket=ket: e.activation(out=ket.ap[0:64, 0:1024], in_=ptb[0:64, 0:1024], func=AF.Copy), r=[ptr], w=[ket])
                if s == 0:
                    pat = nb()
                    for h in range(8):
                        P.do(pe, lambda e, pat=pat, h=h, cs=cs: e.matmul(pat.ap[0:64, h * 64:(h + 1) * 64], kt3[:, h, cs], qt3[:, h, cs], start=True, stop=True),
                             r=[KTL, QT_], w=[pat], inc=(h == 7))
                    P.do(dve, lambda e, pat=pat, atm=atm: e.tensor_tensor(out=v3(atm.ap[0:64, 0:512], 64), in0=v3(pat.ap[0:64, 0:512], 64),
                                                                          in1=mask_ap.unsqueeze(1).broadcast_to([64, 8, 64]), op=ALU.mult),
                         r=[pat, MASK], w=[atm])

            stage1(0)
            if len(chunks) > 1:
                stage1(1)

            def emit_ds(c):
                ket = kets[c]
                pd = [nb(), nb()]
                for h in range(8):
                    P.do(pe, lambda e, pd=pd, ket=ket, c=c, h=h: e.matmul(pd[h // 4].ap[:, (h % 4) * 128:(h % 4 + 1) * 128], ket.ap[0:64, h * 128:(h + 1) * 128],
                                                                          vv4[0:64, c, h, :], start=True, stop=True),
                         r=[ket, VV], w=[pd[h // 4]], inc=(h % 4 == 3))
                return pd

            pds = {chunks[0]: emit_ds(chunks[0])}
            for ci, c in enumerate(chunks):
                cs = slice(c * 64, (c + 1) * 64)
                if ci + 1 < len(chunks):
                    pds[chunks[ci + 1]] = emit_ds(chunks[ci + 1])
                stb_cur = STBS[stb_i[0] % 2]; stb_nxt = STBS[(stb_i[0] + 1) % 2]
                stb_i[0] += 1
                if s == 0:
                    atm = atms[c]
                    po = nb()
                    stc3 = v3(stb_cur.ap, 128)
                    for h in range(8):
                        P.do(pe, lambda e, po=po, atm=atm, c=c, h=h: e.matmul(po.ap[:, h * 64:(h + 1) * 64], vv4[0:64, c, h, :], atm.ap[0:64, h * 64:(h + 1) * 64],
                                                                              start=True, stop=False), r=[VV, atm], w=[po], inc=False)
                        P.do(pe, lambda e, po=po, h=h, cs=cs, stc3=stc3: e.matmul(po.ap[:, h * 64:(h + 1) * 64], stc3[:, h, :], qt3[:, h, cs], start=False, stop=True),
                             r=[stb_cur, QT_], w=[po], inc=(h == 7))
                    P.do(act, lambda e, po=po, cs=cs: e.activation(out=oo3[:, :, cs], in_=v3(po.ap[:, 0:512], 64), func=AF.Copy), r=[po], w=[OO])
                pd = pds[c]
                P.do(dve, lambda e, c=c: e.tensor_tensor(out=st3, in0=st3, in1=dec3[:, :, c:c + 1].broadcast_to([128, 8, 128]), op=ALU.mult),
                     r=[ST, DEC], w=[ST])
                for half in range(2):
                    P.do(dve, lambda e, half=half, pd=pd: e.tensor_tensor(out=ST.ap[:, half * 512:(half + 1) * 512], in0=ST.ap[:, half * 512:(half + 1) * 512],
                                                                          in1=pd[half].ap[:, 0:512], op=ALU.add), r=[ST, pd[half]], w=[ST])
                P.do(pool, lambda e, stb_nxt=stb_nxt: e.tensor_copy(out=stb_nxt.ap, in_=ST.ap), r=[ST], w=[stb_nxt])
                if ci + 2 < len(chunks):
                    stage1(ci + 2)
            if s == 1:
                return
            if not bwd:
                P.store(sp, OO, oo3, OF.rearrange("(h p) t -> p h t", p=128)[:, :, t0 - NCTX:t0 - NCTX + n])
                return
            P.load(sp, OFt, v3(OFt.ap, n), OF.rearrange("(h p) t -> p h t", p=128)[:, :, t0 - NCTX:t0 - NCTX + n])
            P.do(dve, lambda e: e.tensor_tensor(out=OO.ap[:, 0:NN], in0=OO.ap[:, 0:NN], in1=OFt.ap[:, 0:NN], op=ALU.add), r=[OO, OFt], w=[OO])
            P.store(sp, OO, oo3, OF.rearrange("(h p) t -> p h t", p=128)[:, :, t0 - NCTX:t0 - NCTX + n])

        stageA(0)
        for it in range(len(order)):
            gen = stageB(it)
            if it + 1 < len(order):
                stageA(it + 1, gen)
            for _ in gen:
                pass
            stageC(it)
        if not bwd:
            P.store(sp, ST, ST.ap, SX[:, :])

    def mix1_out(src, dst, TT=512):
        A.reset()
        WG = A.bf16(KC * D, "owg"); wg3 = v3(WG.ap, D)
        for kc in range(KC):
            P.load(pool, WG, wg3[:, kc, :], o_win[kc * 128:(kc + 1) * 128, 4 * D:5 * D])
        WO = A.bf16(KC * D, "owo"); wo3 = v3(WO.ap, D)
        load_w_bf16(WO, wo3, o_wout, KC)
        XB = [A.f32(KC * TT, "xa"), A.f32(KC * TT, "xb")]
        U2 = [A.bf16(KC * TT, "u0"), A.bf16(KC * TT, "u1")]; SQ = A.bf16(KC * TT, "sq")
        RS = A.f32(TT, "rs")
        TMP = [A.f32(TT, "t0"), A.f32(TT, "t1")]
        TMP2 = [A.f32(TT, "t2"), A.f32(TT, "t3")]
        OO2 = [A.f32(8 * TT, "oo0"), A.f32(8 * TT, "oo1")]
        KE2 = [A.bf16(8 * TT, "osq0"), A.bf16(8 * TT, "osq1")]; RR = A.bf16(8 * TT, "rr")
        SG2 = [A.f32(8 * TT, "sgall0"), A.f32(8 * TT, "sgall1")]
        tl = tiles_of(TT, False)
        tctx = {}

        def stageP(it, gen=None):
            (t0, n, s) = tl[it]
            X = XB[it % 2]; OO = OO2[it % 2]; U = U2[it % 2]; KE = KE2[it % 2]; SGA_ = SG2[it % 2]
            NN = 8 * n
            oo3 = v3(OO.ap, n); ke3 = v3(KE.ap, n)
            P.load(sp, OO, oo3, OF.rearrange("(h p) t -> p h t", p=128)[:, :, t0 - NCTX:t0 - NCTX + n])
            x3, u3 = norm_mod(X, U, SQ, RS, TMP, n, s, 1, src, t0, lnexp=True)
            P.do(act, lambda e: e.activation(out=KE.ap[:, 0:NN], in_=OO.ap[:, 0:NN], func=AF.Square), r=[OO], w=[KE])
            sg3 = v3(SGA_.ap, n)
            tctx[it] = (x3, u3)
            for h in range(8):
                pg = nb()
                for kc in range(KC):
                    P.do(pe, lambda e, kc=kc, h=h, pg=pg: e.matmul(pg.ap[:, 0:n], wg3[:, kc, h * 128:(h + 1) * 128], u3[:, kc, :],
                                                                   start=(kc == 0), stop=(kc == KC - 1)), r=[WG, U], w=[pg], inc=(kc == KC - 1))
                P.do(act, lambda e, pg=pg, h=h: e.activation(out=sg3[:, h, :], in_=pg.ap[:, 0:n], func=AF.Exp, scale=-1.0), r=[pg], w=[SGA_])
                P.do(dve, lambda e, h=h: e.tensor_scalar(out=sg3[:, h, :], in0=sg3[:, h, :], scalar1=1.0, scalar2=None, op0=ALU.add), r=[SGA_], w=[SGA_])
                P.do(dve, lambda e, h=h: e.reciprocal(out=sg3[:, h, :], in_=sg3[:, h, :]), r=[SGA_], w=[SGA_])
                P.do(dve, lambda e, h=h, pg=pg: e.tensor_tensor(out=sg3[:, h, :], in0=pg.ap[:, 0:n], in1=sg3[:, h, :], op=ALU.mult), r=[SGA_, pg], w=[SGA_])
                if gen is not None:
                    next(gen, None)

        def stageR(it):
            (t0, n, s) = tl[it]
            X = XB[it % 2]; OO = OO2[it % 2]; KE = KE2[it % 2]; SGA_ = SG2[it % 2]
            oo3 = v3(OO.ap, n); ke3 = v3(KE.ap, n); sg3 = v3(SGA_.ap, n); rr3 = v3(RR.ap, n)
            for h in range(8):
                pss = nb()
                P.do(pe, lambda e, pss=pss, h=h: e.matmul(pss.ap[:, 0:n], ONESB.ap, ke3[:, h, :], start=True, stop=True), r=[KE, ONESB], w=[pss])
                T = TMP2[h % 2]
                P.do(act, lambda e, pss=pss, T=T: e.activation(out=T.ap[:, 0:n], in_=pss.ap[:, 0:n], func=AF.Ln, bias=EPSB.ap[:, 0:1], scale=1.0 / 128),
                     r=[pss, EPSB], w=[T])
                P.do(act, lambda e, T=T: e.activation(out=T.ap[:, 0:n], in_=T.ap[:, 0:n], func=AF.Exp, scale=-0.5), r=[T], w=[T])
                P.do(dve, lambda e, T=T, h=h: e.scalar_tensor_tensor(out=oo3[:, h, :], in0=oo3[:, h, :], scalar=GN.ap[:, 0:1], in1=T.ap[:, 0:n],
                                                                     op0=ALU.mult, op1=ALU.mult), r=[OO, GN, T], w=[OO])
                P.do(pool, lambda e, h=h: e.tensor_tensor(out=rr3[:, h, :], in0=oo3[:, h, :], in1=sg3[:, h, :], op=ALU.mult), r=[SGA_, OO], w=[RR])
                yield

        def stageW(it):
            (t0, n, s) = tl[it]
            X = XB[it % 2]
            x3, u3 = tctx.pop(it)
            rr3 = v3(RR.ap, n)
            g5 = hgvec(s, 1)
            for m in range(KC):
                po = nb()
                for h in range(8):
                    P.do(pe, lambda e, po=po, h=h, m=m: e.matmul(po.ap[:, 0:n], wo3[:, h, m * 128:(m + 1) * 128], rr3[:, h, :],
                                                                 start=(h == 0), stop=(h == 7)), r=[WO, RR], w=[po], inc=(h == 7))
                P.do(dve, lambda e, po=po, m=m: e.scalar_tensor_tensor(out=x3[:, m, :], in0=po.ap[:, 0:n], scalar=g5[:, m:m + 1],
                                                                       in1=x3[:, m, :], op0=ALU.mult, op1=ALU.add),
                     r=[po, HG, X], w=[X])
            P.store(sp, X, x3, dst.rearrange("(k p) t -> p k t", p=128)[:, :, t0:t0 + n])

        stageP(0)
        for it in range(len(tl)):
            gen = stageR(it)
            if it + 1 < len(tl):
                stageP(it + 1, gen)
            for _ in gen:
                pass
            stageW(it)


    def mix1_exchange():
        A.reset()
        allgather(SX, SGA)

    steps = []
    steps.append(lambda: ada_phase(0))
    steps.append(lambda: ffn_phase(0, 0, xin, XT))
    steps.append(lambda: mix0_proj(XT))
    steps.append(lambda: mix0_exchange())
    steps.append(lambda: mix0_attn())
    steps.append(lambda: mix0_out(XT, XT))
    steps.append(lambda: ffn_phase(0, 2, XT, XT))
    steps.append(lambda: ada_phase(1))
    steps.append(lambda: ffn_phase(1, 0, XT, XT))
    steps.append(lambda: mix1_dir(XT, XT, 0))
    steps.append(lambda: mix1_exchange())
    steps.append(lambda: mix1_dir(XT, XT, 1))
    steps.append(lambda: mix1_out(XT, XT))
    steps.append(lambda: ffn_phase(1, 2, XT, XT, with_ctx=False))
    steps.append(lambda: final_phase(XT))
    for i, st in enumerate(steps):
        if i >= stop_after:
            break
        st()
    return P.finish()


def _rope_tables(pos):
    row = (pos // 64).astype(np.float32)
    col = (pos % 64).astype(np.float32)
    inv = (10000.0 ** (-np.arange(8, dtype=np.float32) / 8)).astype(np.float32)
    ang = np.stack([row[:, None] * inv, col[:, None] * inv], axis=1)
    cos, sin = np.cos(ang).astype(np.float32), np.sin(ang).astype(np.float32)
    C = np.ones((96, NT), np.float32)
    S = np.zeros((96, NT), np.float32)
    for ax in range(2):
        for half in range(2):
            r0 = 64 + ax * 16 + half * 8
            C[r0:r0 + 8, NCTX:] = cos[:, ax, :].T
            S[r0:r0 + 8, NCTX:] = (-sin[:, ax, :].T) if half == 0 else sin[:, ax, :].T
    return C, S


_PERM = np.concatenate([np.arange(8, 16), np.arange(0, 8), np.arange(24, 32), np.arange(16, 24)])

_NC_CACHE = {}


def _prep_inputs(inp):
    f = lambda a: np.ascontiguousarray(np.asarray(a, dtype=np.float32))
    x, c, ctx, c_ctx = f(inp["x"]), f(inp["c"]), f(inp["ctx"]), f(inp["c_ctx"])
    ewin = f(inp["even_w_in"])[0]
    wuq = f(inp["mla_w_uq"])[0].reshape(256, 8, 96)
    wukv = f(inp["mla_w_ukv"])[0].reshape(128, 8, 128)
    wkr = np.zeros((D, 96), np.float32); wkr[:, 64:] = ewin[:, 1920:1952]
    wkrp = np.zeros((D, 96), np.float32); wkrp[:, 64:] = ewin[:, 1920:1952][:, _PERM]
    wqb = np.zeros((256, 8, 96), np.float32); wqb[:, :, 64:] = wuq[:, :, 64:][:, :, _PERM]
    wkn = np.zeros((128, 8, 96), np.float32); wkn[:, :, :64] = wukv[:, :, :64]
    wv = np.ascontiguousarray(wukv[:, :, 64:]).reshape(128, 512)
    tri = np.triu(np.ones((64, 64), np.float32))
    cmask = np.concatenate([tri, tri.T], axis=1)
    conv = f(inp["even_conv_w"])[0]
    owin = f(inp["odd_w_in"])[0].reshape(D, 5, D)
    olb = f(inp["hgrn_lb_logits"])
    shared = {
        "ada_w": f(inp["ada_w"]), "ada_b": f(inp["ada_b"]), "norm_g": f(inp["norm_g"]),
        "ffn_w1": f(inp["ffn_w1"]), "ffn_w3": f(inp["ffn_w3"]), "ffn_w2": f(inp["ffn_w2"]),
        "e_win": np.ascontiguousarray(ewin[:, :1920]), "e_wkr": wkr, "e_wkrp": wkrp,
        "e_qg": f(inp["mla_q_norm_g"])[0],
        "e_wqa": np.ascontiguousarray(wuq).reshape(256, 768), "e_wqb": wqb.reshape(256, 768),
        "e_kvg": f(inp["mla_kv_norm_g"])[0], "e_wkn": wkn.reshape(128, 768), "e_wv": wv,
        "e_wout": f(inp["even_w_out"])[0],
        "o_gn": f(inp["hgrn_g_norm_g"])[0],
        "o_wout": f(inp["odd_w_out"])[0], "fin_g": f(inp["final_norm_g"]),
        "cmask": cmask, "ident": np.eye(128, dtype=np.float32),
    }
    per_half = []
    for s in range(2):
        pos = np.arange(SEQ_L) if s == 0 else (SEQ - 1 - np.arange(SEQ_L))
        C, S = _rope_tables(pos)
        d1, d2 = (0, 1) if s == 0 else (1, 0)
        owin_p = np.ascontiguousarray(owin[:, [0, 1, 2 + d1, 2 + d2, 4], :]).reshape(D, 5 * D)
        olb_p = np.ascontiguousarray(olb[:, [d1, d2], :])
        conv_p = np.ascontiguousarray(conv if s == 0 else conv[::-1])
        sel = np.zeros((128, 2), np.float32); sel[:, 1 - s] = 1.0
        per_half.append({"ropeC": C, "ropeS": S, "o_win": owin_p, "o_lb": olb_p, "e_conv": conv_p, "selv": sel})
    maps = []
    for core in range(NCORES):
        b, s = core // 2, core % 2
        m = dict(shared)
        m.update(per_half[s])
        if s == 0:
            loc = np.concatenate([ctx[b], x[b, :SEQ_L]], axis=0)
        else:
            loc = np.concatenate([ctx[b][::-1], x[b, SEQ_L:][::-1]], axis=0)
        m["xin"] = np.ascontiguousarray(loc.T)
        cv = np.stack([c[b], c_ctx], axis=1)
        m["cvec"] = np.ascontiguousarray(cv.reshape(KC, 128, 2).transpose(1, 0, 2))
        maps.append(m)
    return maps


def kernel(**inputs):
    maps = _prep_inputs(inputs)
    if "nc" not in _NC_CACHE:
        _NC_CACHE["nc"] = build()
    res = run_bass_kernel_spmd(_NC_CACHE["nc"], maps, core_ids=list(range(NCORES)))
    out = np.empty((NCORES // 2, SEQ, D), np.float32)
    for core in range(NCORES):
        b, s = core // 2, core % 2
        y = res.results[core]["yout"].T
        if s == 0:
            out[b, :SEQ_L] = y
        else:
            out[b, SEQ_L:] = y[::-1]
    return out
```

```python
import numpy as np
from contextlib import ExitStack
import concourse.bass as bass
import concourse.mybir as mybir
from concourse.bass_utils import run_bass_kernel_spmd

F32 = mybir.dt.float32
BF16 = mybir.dt.bfloat16
AF = mybir.ActivationFunctionType
ALU = mybir.AluOpType

D = 1024
NCTX = 256
SEQ = 8192
SEQ_L = SEQ // 2
NT = NCTX + SEQ_L
NKEY = NCTX + SEQ
PAIRS = [[0, 1], [2, 3], [4, 5], [6, 7]]
DFF = 2816
NJ = DFF // 128
KC = 8
EPS = 1e-6
NCORES = 8
MLA_SCALE = 96 ** -0.5
HGRN_SCALE = 128 ** -0.5


import types


def _freeze(fn):
    if fn.__closure__ is None:
        return fn
    cells = []
    for c in fn.__closure__:
        try:
            cells.append(types.CellType(c.cell_contents))
        except ValueError:
            cells.append(c)
    g = types.FunctionType(fn.__code__, fn.__globals__, fn.__name__, fn.__defaults__, tuple(cells))
    g.__kwdefaults__ = fn.__kwdefaults__
    return g


class Tok:
    __slots__ = ("sem", "val")

    def __init__(self, sem, val):
        self.sem, self.val = sem, val


class DmaSem:
    def __init__(self, P, name):
        self.sem = P.new_sem(name)
        self.count = 0

    def tok(self):
        return Tok(self.sem, self.count)


class Buf:
    def __init__(self, ap, name=""):
        self.ap, self.name = ap, name
        self.w = {}
        self.r = {}
        self.ds = None


class Eng:
    def __init__(self, P, name):
        self.P, self.name = P, name
        self.ops = []
        self.sem = P.new_sem("p_" + name)
        self.count = 0
        self.waited = {}

    def _wait(self, toks):
        need = {}
        for t in toks:
            k = id(t.sem)
            if t.val > need.get(k, (None, 0))[1]:
                need[k] = (t.sem, t.val)
        for k, (sem, val) in need.items():
            if self.waited.get(k, 0) >= val:
                continue
            self.waited[k] = val
            self.ops.append(("wait", sem, val))

    def emit(self, fn, toks, inc=True):
        self._wait(toks)
        if inc:
            self.count += 1
            self.ops.append(("op", fn, self.sem, 1))
            return Tok(self.sem, self.count)
        self.ops.append(("op", fn, None, 0))
        return None

    def replay(self, e):
        for o in self.ops:
            if o[0] == "wait":
                e.wait_ge(o[1], o[2])
            else:
                ins = o[1](e)
                if o[2] is not None:
                    ins.then_inc(o[2], o[3])


class Prog:
    def __init__(self):
        self.nc = bass.Bass("TRN2", target_bir_lowering=False)
        self.es = ExitStack()
        self.nsem = 0
        self.pe = Eng(self, "pe")
        self.act = Eng(self, "act")
        self.dve = Eng(self, "dve")
        self.pool = Eng(self, "pool")
        self.sp = Eng(self, "sp")
        self.engs = [self.pe, self.act, self.dve, self.pool, self.sp]
        self.dsems = []
        self.free_ds = []
        self.pending_pe = []

    def new_sem(self, name):
        self.nsem += 1
        return self.es.enter_context(self.nc.semaphore(name + str(self.nsem)))

    def get_ds(self):
        if self.free_ds:
            return self.free_ds.pop()
        d = DmaSem(self, "d")
        self.dsems.append(d)
        return d

    def sb(self, name, shape, dt):
        return self.es.enter_context(self.nc.sbuf_tensor(name, list(shape), dt))

    def din(self, name, shape, dt=F32):
        return self.nc.dram_tensor(name, list(shape), dt, kind="ExternalInput").ap()

    def dout(self, name, shape, dt=F32):
        return self.nc.dram_tensor(name, list(shape), dt, kind="ExternalOutput").ap()

    def dint(self, name, shape, dt=F32):
        return self.nc.dram_tensor(name, list(shape), dt, kind="Internal").ap()

    def do(self, eng, fn, r=(), w=(), inc=True):
        fn = _freeze(fn)
        toks = []
        for b in r:
            toks.extend(b.w.values())
        for b in w:
            toks.extend(b.w.values())
            toks.extend(b.r.values())
        if eng is self.pe and not inc:
            eng._wait(toks)
            eng.ops.append(("op", fn, None, 0))
            self.pending_pe.append((r, w))
            return None
        tok = eng.emit(fn, toks, True)
        groups = [(r, w)]
        if eng is self.pe:
            groups += self.pending_pe
            self.pending_pe = []
        for (rr, ww) in groups:
            for b in rr:
                b.r[id(tok.sem)] = tok
            for b in ww:
                b.w = {id(tok.sem): tok}
                b.r = {}
        return tok

    def load(self, q, buf, dst_ap, src_ap, slow=False):
        kw = {"allow_slow_non_contiguous": True} if slow else {}
        if buf.ds is None:
            buf.ds = self.get_ds()
        ds = buf.ds
        toks = [t for t in buf.w.values() if t.sem is not ds.sem] + list(buf.r.values())
        q._wait(toks)
        ds.count += 16
        q.ops.append(("op", lambda e: e.dma_start(out=dst_ap, in_=src_ap, **kw), ds.sem, 16))
        tok = ds.tok()
        buf.w = {id(ds.sem): tok}
        buf.r = {}
        return tok

    def store(self, q, buf, src_ap, dst_ap, slow=False):
        kw = {"allow_slow_non_contiguous": True} if slow else {}
        if buf.ds is None:
            buf.ds = self.get_ds()
        ds = buf.ds
        q._wait(list(buf.w.values()))
        ds.count += 16
        q.ops.append(("op", lambda e: e.dma_start(out=dst_ap, in_=src_ap, **kw), ds.sem, 16))
        tok = ds.tok()
        buf.r[id(ds.sem)] = tok
        return tok

    def barrier(self):
        assert not self.pending_pe
        toks = [Tok(e.sem, e.count) for e in self.engs[:4] if e.count > 0]
        toks += [d.tok() for d in self.dsems if d.count > 0]
        for e in self.engs:
            e._wait(toks)

    def finish(self):
        self.barrier()
        with self.nc.Block() as block:
            @block.tensor
            def _(e):
                self.pe.replay(e)

            @block.scalar
            def _(e):
                self.act.replay(e)

            @block.vector
            def _(e):
                self.dve.replay(e)

            @block.gpsimd
            def _(e):
                self.pool.replay(e)

            @block.sync
            def _(e):
                self.sp.replay(e)
        self.es.close()
        return self.nc


class Arena:
    def __init__(self, P, ncols):
        self.P = P
        self.t = P.sb("arena", [128, ncols], F32)
        self.n = ncols
        self.off = 0
        self.bufs = []

    def reset(self):
        self.P.barrier()
        for b in self.bufs:
            if b.ds is not None:
                self.P.free_ds.append(b.ds)
                b.ds = None
        self.bufs = []
        self.off = 0

    def f32(self, ncols, name=""):
        ncols = (ncols + 7) // 8 * 8
        assert self.off + ncols <= self.n, f"arena overflow {name} {self.off}+{ncols}>{self.n}"
        ap = self.t[:, self.off:self.off + ncols]
        self.off += ncols
        b = Buf(ap, name)
        self.bufs.append(b)
        return b

    def bf16(self, ncols, name=""):
        b = self.f32((ncols + 1) // 2, name)
        b.ap = b.ap.bitcast(BF16)
        return b


def v3(ap, inner):
    return ap.rearrange("p (a b) -> p a b", b=inner)


def tiles_of(TT, with_ctx=True):
    tl = [(0, NCTX, 1)] if with_ctx else []
    for i in range(SEQ_L // TT):
        tl.append((NCTX + i * TT, TT, 0))
    return tl


def build(stop_after=99, debug=False):
    P = Prog()
    nc = P.nc
    xin = P.din("xin", [D, NT])
    cvec = P.din("cvec", [128, KC, 2])
    ada_w = P.din("ada_w", [2, D, 9 * D])
    ada_b = P.din("ada_b", [2, 9 * D])
    norm_g = P.din("norm_g", [2, 3, D])
    ffn_w1 = P.din("ffn_w1", [2, 2, D, DFF])
    ffn_w3 = P.din("ffn_w3", [2, 2, D, DFF])
    ffn_w2 = P.din("ffn_w2", [2, 2, DFF, D])
    e_win = P.din("e_win", [D, 1920])
    e_wkr = P.din("e_wkr", [D, 96])
    e_wkrp = P.din("e_wkrp", [D, 96])
    e_conv = P.din("e_conv", [3, 512])
    e_qg = P.din("e_qg", [256])
    e_wqa = P.din("e_wqa", [256, 8 * 96])
    e_wqb = P.din("e_wqb", [256, 8 * 96])
    e_kvg = P.din("e_kvg", [128])
    e_wkn = P.din("e_wkn", [128, 8 * 96])
    e_wv = P.din("e_wv", [128, 512])
    e_wout = P.din("e_wout", [D, D])
    ropeC = P.din("ropeC", [96, NT])
    ropeS = P.din("ropeS", [96, NT])
    o_win = P.din("o_win", [D, 5 * D])
    o_lb = P.din("o_lb", [2, 2, D])
    o_gn = P.din("o_gn", [128])
    o_wout = P.din("o_wout", [D, D])
    fin_g = P.din("fin_g", [D])
    cmask = P.din("cmask", [64, 128])
    ident = P.din("ident", [128, 128])
    yout = P.dout("yout", [D, SEQ_L])
    selv = P.din("selv", [128, 2])
    mk = P.dout if debug else P.dint
    XT = mk("XT", [D, NT])
    GB = P.dint("GB", [512, NT], BF16)
    CV = P.dint("CV", [512, NT + 3])
    KCX = P.dint("KCX", [768, NCTX], BF16)
    KL = [P.dint(f"KL{i}", [192, SEQ_L], BF16) for i in range(4)]
    KG = [P.dint(f"KG{i}", [384, SEQ_L], BF16) for i in range(4)]
    VCX = P.dint("VCX", [NCTX, 512], BF16)
    VL = [P.dint(f"VL{i}", [SEQ_L // 2, 512], BF16) for i in range(2)]
    VG = [P.dint(f"VG{i}", [SEQ_L, 512], BF16) for i in range(2)]
    CVH = P.dint("CVH", [512, 8])
    CVG = P.dint("CVG", [1024, 8])
    SX = P.dint("SX", [128, 1024])
    SGA = P.dint("SGA", [256, 1024])
    QT = P.dint("QT", [8, 96, NT], BF16)
    BT = mk("BT", [512, NT], BF16) if not debug else P.dout("BT", [512, NT], BF16)
    OF = P.dint("OF", [D, SEQ_L])

    def cv_col(t):
        return 1 + t if t < NCTX else 2 + t

    cst = P.sb("cst", [128, 1024], F32)
    c_off = [0]

    def cbuf(n, name=""):
        ap = cst[:, c_off[0]:c_off[0] + n]
        c_off[0] += n
        assert c_off[0] <= 1024
        return Buf(ap, name)

    CVEC = cbuf(16)
    SC = cbuf(16)
    MOD = cbuf(144)
    ADAB = cbuf(72)
    NG = cbuf(48)
    G = cbuf(48)
    HG = cbuf(48)
    FING = cbuf(8)
    EPSB = cbuf(1)
    CONVW = cbuf(12)
    QG = cbuf(2)
    KVG = cbuf(1)
    LBL = cbuf(32)
    LB = cbuf(16)
    OML = cbuf(16)
    GN = cbuf(1)
    ZERO = cbuf(4)
    SEL = cbuf(2)
    ONESB = Buf(P.sb("onesb", [128, 128], BF16)[:], "ones")
    IDB = Buf(P.sb("idb", [128, 128], BF16)[:], "idb")
    MASK = Buf(P.sb("maskt", [64, 128], F32)[:], "mask")
    psall = P.es.enter_context(nc.psum_tensor("psall", [128, 4096], F32))
    banks = [Buf(psall[:, i * 512:(i + 1) * 512], f"bank{i}") for i in range(8)]
    bank_rr = [0]

    def nb():
        b = banks[bank_rr[0] % 8]
        bank_rr[0] += 1
        return b

    A = Arena(P, 50 * 1024 + 512)
    sp, pe, act, dve, pool = P.sp, P.pe, P.act, P.dve, P.pool

    P.load(sp, CVEC, v3(CVEC.ap, 2), cvec[:, :, :])
    for l_ in range(2):
        for j_ in range(3):
            P.load(sp, NG, NG.ap[:, (l_ * 3 + j_) * 8:(l_ * 3 + j_ + 1) * 8], norm_g[l_, j_].rearrange("(k p) -> p k", p=128), slow=True)
    P.load(sp, FING, FING.ap, fin_g.rearrange("(k p) -> p k", p=128), slow=True)
    for c_ in range(4):
        P.load(sp, CONVW, CONVW.ap[:, c_ * 3:(c_ + 1) * 3], e_conv[:, c_ * 128:(c_ + 1) * 128].rearrange("w p -> p w"), slow=True)
    P.load(sp, QG, QG.ap, e_qg.rearrange("(k p) -> p k", p=128), slow=True)
    P.load(sp, KVG, KVG.ap, e_kvg.rearrange("(k p) -> p k", p=128), slow=True)
    for l_ in range(2):
        for d_ in range(2):
            P.load(sp, LBL, LBL.ap[:, (l_ * 2 + d_) * 8:(l_ * 2 + d_ + 1) * 8], o_lb[l_, d_].rearrange("(h p) -> p h", p=128), slow=True)
    P.load(sp, GN, GN.ap, o_gn.rearrange("(k p) -> p k", p=128), slow=True)
    P.load(sp, MASK, MASK.ap, cmask[:, :])
    P.load(sp, SEL, SEL.ap, selv[:, :])
    P.load(pool, IDB, IDB.ap, ident[:, :])
    P.do(dve, lambda e: e.memset(EPSB.ap, EPS), w=[EPSB])
    P.do(dve, lambda e: e.memset(ZERO.ap, 0.0), w=[ZERO])
    P.do(dve, lambda e: e.memset(ONESB.ap, 1.0), w=[ONESB])
    P.do(dve, lambda e: e.tensor_tensor(out=LB.ap, in0=LBL.ap[:, 16:32], in1=LBL.ap[:, 0:16], op=ALU.subtract), r=[LBL], w=[LB])
    P.do(act, lambda e: e.activation(out=LB.ap, in_=LB.ap, func=AF.Sigmoid), r=[LB], w=[LB])
    P.do(dve, lambda e: e.tensor_scalar(out=OML.ap, in0=LB.ap, scalar1=-1.0, scalar2=1.0, op0=ALU.mult, op1=ALU.add), r=[LB], w=[OML])
    P.do(act, lambda e: e.activation(out=SC.ap, in_=CVEC.ap, func=AF.Silu), r=[CVEC], w=[SC])

    def allgather(src, dst):
        ds = P.get_ds()
        ds.count += 1
        pool.ops.append(("op", lambda e: e.collective_compute("AllGather", ALU.bypass, replica_groups=PAIRS,
                                                              ins=[src.opt()], outs=[dst.opt()]), ds.sem, 1))

    def ada_phase(l):
        A.reset()
        P.load(sp, ADAB, ADAB.ap, ada_b[l].rearrange("(j p) -> p j", p=128), slow=True)
        wb = [A.bf16(KC * 1024, f"adaw{i}") for i in range(3)]
        pb = nb()
        SCB = A.bf16(16, "scb")
        P.do(dve, lambda e: e.tensor_copy(out=SCB.ap, in_=SC.ap), r=[SC], w=[SCB])
        sc3 = v3(SCB.ap, 2)
        for j in range(9):
            W = wb[j % 3]
            W3 = v3(W.ap, 1024)
            for kc in range(KC):
                P.load(pool, W, W3[:, kc, :], ada_w[l, kc * 128:(kc + 1) * 128, j * 1024:(j + 1) * 1024])
            for oc in range(8):
                col = (j * 8 + oc) * 2
                for kc in range(KC):
                    P.do(pe, lambda e, W3=W3, kc=kc, oc=oc, col=col: e.matmul(
                        pb.ap[:, col:col + 2], W3[:, kc, oc * 128:(oc + 1) * 128], sc3[:, kc, :],
                        start=(kc == 0), stop=(kc == KC - 1)), r=[W, SCB], w=[pb], inc=(kc == KC - 1))
        ps3 = v3(pb.ap[:, 0:144], 2)
        mod3 = v3(MOD.ap, 72)
        for s in range(2):
            P.do(dve, lambda e, s=s: e.tensor_tensor(out=mod3[:, s, :], in0=ps3[:, :, s], in1=ADAB.ap, op=ALU.add),
                 r=[pb, ADAB], w=[MOD])
        for s in range(2):
            m4 = mod3[:, s, :].rearrange("p (jj t k) -> p jj t k", t=3, k=8)
            g3 = v3(G.ap, 24)[:, s, :].rearrange("p (jj k) -> p jj k", k=8)
            h3 = v3(HG.ap, 24)[:, s, :].rearrange("p (jj k) -> p jj k", k=8)
            ng3 = v3(NG.ap, 24)[:, l, :].rearrange("p (jj k) -> p jj k", k=8)
            P.do(dve, lambda e, m4=m4, g3=g3, ng3=ng3: e.scalar_tensor_tensor(
                out=g3, in0=m4[:, :, 1, :], scalar=1.0, in1=ng3, op0=ALU.add, op1=ALU.mult), r=[MOD, NG], w=[G])
            P.do(dve, lambda e, m4=m4, h3=h3: e.tensor_scalar(
                out=h3, in0=m4[:, :, 2, :], scalar1=0.5, scalar2=None, op0=ALU.mult), r=[MOD], w=[HG])
            P.do(dve, lambda e, m4=m4, h3=h3: e.tensor_copy(out=h3[:, 1, :], in_=m4[:, 1, 2, :]), r=[MOD], w=[HG])

    def modv(s, j):
        return v3(MOD.ap, 72)[:, s, j * 8:(j + 1) * 8]

    def gvec(s, jj):
        return v3(G.ap, 24)[:, s, jj * 8:(jj + 1) * 8]

    def hgvec(s, jj):
        return v3(HG.ap, 24)[:, s, jj * 8:(jj + 1) * 8]

    def rstd_from_ps(pb, n, RS, dim, lnexp=False):
        if lnexp:
            P.do(act, lambda e: e.activation(out=RS.ap[:, 0:n], in_=pb.ap[:, 0:n], func=AF.Ln, bias=EPSB.ap[:, 0:1],
                                             scale=1.0 / dim), r=[pb, EPSB], w=[RS])
            P.do(act, lambda e: e.activation(out=RS.ap[:, 0:n], in_=RS.ap[:, 0:n], func=AF.Exp, scale=-0.5), r=[RS], w=[RS])
            return
        P.do(act, lambda e: e.activation(out=RS.ap[:, 0:n], in_=pb.ap[:, 0:n], func=AF.Sqrt, bias=EPSB.ap[:, 0:1],
                                         scale=1.0 / dim), r=[pb, EPSB], w=[RS])
        P.do(dve, lambda e: e.reciprocal(out=RS.ap[:, 0:n], in_=RS.ap[:, 0:n]), r=[RS], w=[RS])

    def norm_mod(X, U, SQ, RS, TMP, n, s, jj, src, t0, lnexp=False):
        x3 = v3(X.ap[:, 0:KC * n], n)
        u3 = v3(U.ap[:, 0:KC * n], n)
        sq3 = v3(SQ.ap[:, 0:KC * n], n)
        P.load(sp, X, x3, src.rearrange("(k p) t -> p k t", p=128)[:, :, t0:t0 + n])
        P.do(act, lambda e: e.activation(out=SQ.ap[:, 0:KC * n], in_=X.ap[:, 0:KC * n], func=AF.Square), r=[X], w=[SQ])
        pb = nb()
        for kc in range(KC):
            P.do(pe, lambda e, kc=kc: e.matmul(pb.ap[:, 0:n], ONESB.ap, sq3[:, kc, :], start=(kc == 0), stop=(kc == KC - 1)),
                 r=[SQ, ONESB], w=[pb], inc=(kc == KC - 1))
        rstd_from_ps(pb, n, RS, D, lnexp)
        gv = gvec(s, jj)
        sh = modv(s, 3 * jj)
        for kc in range(KC):
            T = TMP[kc % 2]
            P.do(dve, lambda e, kc=kc, T=T: e.tensor_tensor(out=T.ap[:, 0:n], in0=x3[:, kc, :], in1=RS.ap[:, 0:n], op=ALU.mult),
                 r=[X, RS], w=[T])
            P.do(act, lambda e, kc=kc, T=T: e.activation(out=u3[:, kc, :], in_=T.ap[:, 0:n], func=AF.Identity,
                                                         bias=sh[:, kc:kc + 1], scale=gv[:, kc:kc + 1]),
                 r=[T, G, MOD], w=[U])
        return x3, u3

    def load_w_bf16(buf, view3, dram2d, nk, rows_per=128):
        for k in range(nk):
            P.load(pool, buf, view3[:, k, :], dram2d[k * rows_per:(k + 1) * rows_per, :])

    def ffn_phase(l, jj, src, dst, with_ctx=True, TT=512):
        wi = 0 if jj == 0 else 1
        A.reset()
        HJ = NJ // 2 * 128
        W1 = [A.bf16(KC * HJ, "w1a"), A.bf16(KC * HJ, "w1b")]
        W3 = [A.bf16(KC * HJ, "w3a"), A.bf16(KC * HJ, "w3b")]
        W2 = A.bf16(NJ * D, "w2")
        w13 = [v3(W1[i].ap, HJ) for i in range(2)]; w33 = [v3(W3[i].ap, HJ) for i in range(2)]; w23 = v3(W2.ap, D)
        for i in range(2):
            for kc in range(KC):
                P.load(pool, W1[i], w13[i][:, kc, :], ffn_w1[l, wi, kc * 128:(kc + 1) * 128, i * HJ:(i + 1) * HJ])
            for kc in range(KC):
                P.load(pool, W3[i], w33[i][:, kc, :], ffn_w3[l, wi, kc * 128:(kc + 1) * 128, i * HJ:(i + 1) * HJ])
        load_w_bf16(W2, w23, ffn_w2[l, wi], NJ)
        XB = [A.f32(KC * TT, "xa"), A.f32(KC * TT, "xb")]
        U = A.bf16(KC * TT, "u")
        H = A.bf16(NJ * TT, "h")
        RS = A.f32(TT, "rs")
        TMP = [A.f32(TT, "t0"), A.f32(TT, "t1")]
        SQS = [A.bf16(TT, "sq0"), A.bf16(TT, "sq1")]
        SA = TMP
        tl = tiles_of(TT, with_ctx)

        def norm_parts(it):
            (t0, n, s) = tl[it]
            X = XB[it % 2]
            x3 = v3(X.ap[:, 0:KC * n], n)
            u3 = v3(U.ap[:, 0:KC * n], n)

            def part1():
                P.load(sp, X, x3, src.rearrange("(k p) t -> p k t", p=128)[:, :, t0:t0 + n])

            def part2():
                pb = nb()
                for kc in range(KC):
                    sq = SQS[kc % 2]
                    P.do(act, lambda e, kc=kc, sq=sq: e.activation(out=sq.ap[:, 0:n], in_=x3[:, kc, :], func=AF.Square), r=[X], w=[sq])
                    P.do(pe, lambda e, kc=kc, sq=sq: e.matmul(pb.ap[:, 0:n], ONESB.ap, sq.ap[:, 0:n], start=(kc == 0), stop=(kc == KC - 1)),
                         r=[sq, ONESB], w=[pb])
                rstd_from_ps(pb, n, RS, D)
                gv = gvec(s, jj)
                sh = modv(s, 3 * jj)
                for kc in range(KC):
                    T = TMP[kc % 2]
                    P.do(dve, lambda e, kc=kc, T=T: e.tensor_tensor(out=T.ap[:, 0:n], in0=x3[:, kc, :], in1=RS.ap[:, 0:n], op=ALU.mult),
                         r=[X, RS], w=[T])
                    P.do(act, lambda e, kc=kc, T=T: e.activation(out=u3[:, kc, :], in_=T.ap[:, 0:n], func=AF.Identity,
                                                                 bias=sh[:, kc:kc + 1], scale=gv[:, kc:kc + 1]),
                         r=[T, G, MOD], w=[U])
            return x3, u3, part1, part2

        parts = {0: norm_parts(0)}
        parts[0][2]()
        parts[0][3]()
        for it, (t0, n, s) in enumerate(tl):
            X = XB[it % 2]
            h3 = v3(H.ap[:, 0:NJ * n], n)
            x3, u3, _, _ = parts.pop(it)
            if it + 1 < len(tl):
                parts[it + 1] = norm_parts(it + 1)
                parts[it + 1][2]()
            for j in range(NJ):
                pa = nb(); pbk = nb()
                for kc in range(KC):
                    P.do(pe, lambda e, pa=pa, j=j, kc=kc: e.matmul(pa.ap[:, 0:n], w13[j // 11][:, kc, (j % 11) * 128:(j % 11 + 1) * 128], u3[:, kc, :],
                                                                   start=(kc == 0), stop=(kc == KC - 1)),
                         r=[W1[j // 11], U], w=[pa], inc=(kc == KC - 1))
                for kc in range(KC):
                    P.do(pe, lambda e, pbk=pbk, j=j, kc=kc: e.matmul(pbk.ap[:, 0:n], w33[j // 11][:, kc, (j % 11) * 128:(j % 11 + 1) * 128], u3[:, kc, :],
                                                                     start=(kc == 0), stop=(kc == KC - 1)),
                         r=[W3[j // 11], U], w=[pbk], inc=(kc == KC - 1))
                S_ = SA[j % 2]
                P.do(act, lambda e, pa=pa, S_=S_: e.activation(out=S_.ap[:, 0:n], in_=pa.ap[:, 0:n], func=AF.Silu), r=[pa], w=[S_])
                P.do(dve, lambda e, pbk=pbk, S_=S_, j=j: e.tensor_tensor(out=h3[:, j, :], in0=pbk.ap[:, 0:n], in1=S_.ap[:, 0:n], op=ALU.mult),
                     r=[pbk, S_], w=[H])
            hg = hgvec(s, jj)
            for m in range(KC):
                if m == 4 and it + 1 < len(tl):
                    parts[it + 1][3]()
                po = nb()
                for j in range(NJ):
                    P.do(pe, lambda e, po=po, j=j, m=m: e.matmul(po.ap[:, 0:n], w23[:, j, m * 128:(m + 1) * 128], h3[:, j, :],
                                                                 start=(j == 0), stop=(j == NJ - 1)),
                         r=[W2, H], w=[po], inc=(j == NJ - 1))
                P.do(dve, lambda e, po=po, m=m: e.scalar_tensor_tensor(out=x3[:, m, :], in0=po.ap[:, 0:n], scalar=hg[:, m:m + 1],
                                                                       in1=x3[:, m, :], op0=ALU.mult, op1=ALU.add),
                     r=[po, HG, X], w=[X])
            P.store(sp, X, x3, dst.rearrange("(k p) t -> p k t", p=128)[:, :, t0:t0 + n])

    def final_phase(src, TT=512):
        A.reset()
        XB = [A.f32(KC * TT, "xa"), A.f32(KC * TT, "xb")]
        SQ = A.bf16(KC * TT, "sq")
        RS = A.f32(TT, "rs")
        for it, (t0, n, s) in enumerate(tiles_of(TT, False)):
            X = XB[it % 2]
            x3 = v3(X.ap[:, 0:KC * n], n)
            sq3 = v3(SQ.ap[:, 0:KC * n], n)
            P.load(sp, X, x3, src.rearrange("(k p) t -> p k t", p=128)[:, :, t0:t0 + n])
            P.do(act, lambda e, X=X: e.activation(out=SQ.ap[:, 0:KC * n], in_=X.ap[:, 0:KC * n], func=AF.Square), r=[X], w=[SQ])
            pb = nb()
            for kc in range(KC):
                P.do(pe, lambda e, pb=pb, kc=kc, sq3=sq3: e.matmul(pb.ap[:, 0:n], ONESB.ap, sq3[:, kc, :], start=(kc == 0), stop=(kc == KC - 1)),
                     r=[SQ, ONESB], w=[pb], inc=(kc == KC - 1))
            rstd_from_ps(pb, n, RS, D)
            for kc in range(KC):
                P.do(dve, lambda e, kc=kc, x3=x3: e.scalar_tensor_tensor(out=x3[:, kc, :], in0=x3[:, kc, :], scalar=FING.ap[:, kc:kc + 1],
                                                                        in1=RS.ap[:, 0:n], op0=ALU.mult, op1=ALU.mult),
                     r=[X, RS, FING], w=[X])
            P.store(sp, X, x3, yout.rearrange("(k p) t -> p k t", p=128)[:, :, t0 - NCTX:t0 - NCTX + n])

    def mix0_proj(src, TT=512):
        A.reset()
        WI = A.bf16(KC * 1920, "win"); wi3 = v3(WI.ap, 1920)
        load_w_bf16(WI, wi3, e_win, KC)
        WKR = A.bf16(KC * 96, "wkr"); wkr3 = v3(WKR.ap, 96)
        load_w_bf16(WKR, wkr3, e_wkr, KC)
        WKRP = A.bf16(KC * 96, "wkrp"); wkrp3 = v3(WKRP.ap, 96)
        load_w_bf16(WKRP, wkrp3, e_wkrp, KC)
        WQA = A.bf16(2 * 768, "wqa"); wqa3 = v3(WQA.ap, 768)
        load_w_bf16(WQA, wqa3, e_wqa, 2)
        WQB = A.bf16(2 * 768, "wqb"); wqb3 = v3(WQB.ap, 768)
        load_w_bf16(WQB, wqb3, e_wqb, 2)
        WKN = A.bf16(768, "wkn")
        P.load(pool, WKN, WKN.ap, e_wkn[:, :])
        WV = A.bf16(512, "wv")
        P.load(pool, WV, WV.ap, e_wv[:, :])
        XB = [A.f32(KC * TT, "xa"), A.f32(KC * TT, "xb")]
        U = A.bf16(KC * TT, "u"); SQ = A.bf16(KC * TT, "sq")
        RS = A.f32(TT, "rs")
        TMP = [A.f32(TT, "t0"), A.f32(TT, "t1")]
        RC = A.f32(TT, "ropec"); RSN = A.f32(TT, "ropes")
        GBt = [A.bf16(TT, "gb0"), A.bf16(TT, "gb1")]
        CVt = [A.f32(TT, "cv0"), A.f32(TT, "cv1")]
        CQ = A.f32(2 * TT, "cq"); NQ = A.bf16(2 * TT, "nq"); SQQ = A.bf16(2 * TT, "sqq")
        CKV = A.f32(TT, "ckv"); NKV = A.bf16(TT, "nkv"); SQK = A.bf16(TT, "sqk")
        RSQ = A.f32(TT, "rsq")
        ROT = A.f32(TT, "rot")
        T1 = [A.f32(TT, "q1a"), A.f32(TT, "q1b")]
        T2 = [A.f32(TT, "q2a"), A.f32(TT, "q2b")]
        QO = [A.bf16(TT, "qo0"), A.bf16(TT, "qo1")]
        KO = [A.bf16(TT, "ko0"), A.bf16(TT, "ko1")]
        VO = [A.bf16(512, "vo0"), A.bf16(512, "vo1")]
        ZT = A.f32(512, "zt")
        P.do(dve, lambda e: e.memset(ZT.ap[:, 0:4], 0.0), w=[ZT])
        for c in range(4):
            for col in (0, NCTX + 1):
                P.store(sp, ZT, ZT.ap[:, 0:1], CV[c * 128:(c + 1) * 128, col:col + 1], slow=True)
        for it, (t0, n, s) in enumerate(tiles_of(TT, True)):
            X = XB[it % 2]
            x3, u3 = norm_mod(X, U, SQ, RS, TMP, n, s, 1, src, t0)
            P.load(sp, RC, RC.ap[0:96, 0:n], ropeC[:, t0:t0 + n])
            P.load(sp, RSN, RSN.ap[0:96, 0:n], ropeS[:, t0:t0 + n])

            def proj(col0, ncols, pb, W3=wi3, Wb=WI):
                for kc in range(KC):
                    P.do(pe, lambda e, kc=kc: e.matmul(pb.ap[0:ncols, 0:n], W3[:, kc, col0:col0 + ncols], u3[:, kc, :],
                                                       start=(kc == 0), stop=(kc == KC - 1)),
                         r=[Wb, U], w=[pb], inc=(kc == KC - 1))

            cq3 = v3(CQ.ap[:, 0:2 * n], n); nq3 = v3(NQ.ap[:, 0:2 * n], n); sqq3 = v3(SQQ.ap[:, 0:2 * n], n)
            for i in range(2):
                pq = nb(); proj(1536 + i * 128, 128, pq)
                P.do(act, lambda e, pq=pq, i=i: e.activation(out=cq3[:, i, :], in_=pq.ap[:, 0:n], func=AF.Copy), r=[pq], w=[CQ])
            P.do(act, lambda e: e.activation(out=SQQ.ap[:, 0:2 * n], in_=CQ.ap[:, 0:2 * n], func=AF.Square), r=[CQ], w=[SQQ])
            pss = nb()
            for i in range(2):
                P.do(pe, lambda e, i=i: e.matmul(pss.ap[:, 0:n], ONESB.ap, sqq3[:, i, :], start=(i == 0), stop=(i == 1)),
                     r=[SQQ, ONESB], w=[pss], inc=(i == 1))
            rstd_from_ps(pss, n, RSQ, 256)
            for i in range(2):
                T = TMP[i % 2]
                P.do(dve, lambda e, i=i, T=T: e.tensor_tensor(out=T.ap[:, 0:n], in0=cq3[:, i, :], in1=RSQ.ap[:, 0:n], op=ALU.mult),
                     r=[CQ, RSQ], w=[T])
                P.do(act, lambda e, i=i, T=T: e.activation(out=nq3[:, i, :], in_=T.ap[:, 0:n], func=AF.Identity, scale=QG.ap[:, i:i + 1]),
                     r=[T, QG], w=[NQ])
            pk = nb(); proj(1792, 128, pk)
            P.do(act, lambda e, pk=pk: e.activation(out=CKV.ap[:, 0:n], in_=pk.ap[:, 0:n], func=AF.Copy), r=[pk], w=[CKV])
            P.do(act, lambda e: e.activation(out=SQK.ap[:, 0:n], in_=CKV.ap[:, 0:n], func=AF.Square), r=[CKV], w=[SQK])
            pss = nb()
            P.do(pe, lambda e, pss=pss: e.matmul(pss.ap[:, 0:n], ONESB.ap, SQK.ap[:, 0:n], start=True, stop=True), r=[SQK, ONESB], w=[pss])
            rstd_from_ps(pss, n, RSQ, 128)
            T = TMP[0]
            P.do(dve, lambda e, T=T: e.tensor_tensor(out=T.ap[:, 0:n], in0=CKV.ap[:, 0:n], in1=RSQ.ap[:, 0:n], op=ALU.mult), r=[CKV, RSQ], w=[T])
            P.do(act, lambda e, T=T: e.activation(out=NKV.ap[:, 0:n], in_=T.ap[:, 0:n], func=AF.Identity, scale=KVG.ap[:, 0:1]), r=[T, KVG], w=[NKV])
            pr = nb(); proj(0, 96, pr, wkr3, WKR)
            prp = nb(); proj(0, 96, prp, wkrp3, WKRP)
            t1 = T1[0]; t2 = T2[0]
            P.do(dve, lambda e, pr=pr, t1=t1: e.tensor_tensor(out=t1.ap[0:96, 0:n], in0=pr.ap[0:96, 0:n], in1=RC.ap[0:96, 0:n], op=ALU.mult),
                 r=[pr, RC], w=[t1])
            P.do(dve, lambda e, prp=prp, t2=t2: e.tensor_tensor(out=t2.ap[0:96, 0:n], in0=prp.ap[0:96, 0:n], in1=RSN.ap[0:96, 0:n], op=ALU.mult),
                 r=[prp, RSN], w=[t2])
            P.do(pool, lambda e, t1=t1, t2=t2: e.tensor_tensor(out=ROT.ap[0:96, 0:n], in0=t1.ap[0:96, 0:n], in1=t2.ap[0:96, 0:n], op=ALU.add),
                 r=[t1, t2], w=[ROT])
            for c in range(4):
                pg = nb(); proj(c * 128, 128, pg)
                gbt = GBt[c % 2]
                P.do(act, lambda e, pg=pg, gbt=gbt: e.activation(out=gbt.ap[:, 0:n], in_=pg.ap[:, 0:n], func=AF.Copy), r=[pg], w=[gbt])
                P.store(sp, gbt, gbt.ap[:, 0:n], GB[c * 128:(c + 1) * 128, t0:t0 + n])
                pc = nb(); proj(512 + c * 128, 128, pc)
                pv = nb(); proj(1024 + c * 128, 128, pv)
                T = TMP[c % 2]
                cvt = CVt[c % 2]
                P.do(act, lambda e, pc=pc, T=T: e.activation(out=T.ap[:, 0:n], in_=pc.ap[:, 0:n], func=AF.Copy), r=[pc], w=[T])
                P.do(dve, lambda e, pv=pv, T=T, cvt=cvt: e.tensor_tensor(out=cvt.ap[:, 0:n], in0=pv.ap[:, 0:n], in1=T.ap[:, 0:n], op=ALU.mult),
                     r=[pv, T], w=[cvt])
                cc = cv_col(t0)
                P.store(sp, cvt, cvt.ap[:, 0:n], CV[c * 128:(c + 1) * 128, cc:cc + n])
                if t0 + n == NT:
                    P.store(sp, cvt, cvt.ap[:, n - 1:n], CVH[c * 128:(c + 1) * 128, 0:1], slow=True)
            for h in range(8):
                pa = nb(); pbk = nb()
                for i in range(2):
                    P.do(pe, lambda e, i=i, h=h, pa=pa: e.matmul(pa.ap[0:96, 0:n], wqa3[:, i, h * 96:(h + 1) * 96], nq3[:, i, :],
                                                                 start=(i == 0), stop=(i == 1)), r=[WQA, NQ], w=[pa], inc=(i == 1))
                for i in range(2):
                    P.do(pe, lambda e, i=i, h=h, pbk=pbk: e.matmul(pbk.ap[0:96, 0:n], wqb3[:, i, h * 96:(h + 1) * 96], nq3[:, i, :],
                                                                   start=(i == 0), stop=(i == 1)), r=[WQB, NQ], w=[pbk], inc=(i == 1))
                t1 = T1[h % 2]; t2 = T2[h % 2]; qo = QO[h % 2]
                P.do(dve, lambda e, pa=pa, t1=t1: e.tensor_tensor(out=t1.ap[0:96, 0:n], in0=pa.ap[0:96, 0:n], in1=RC.ap[0:96, 0:n], op=ALU.mult),
                     r=[pa, RC], w=[t1])
                P.do(dve, lambda e, pbk=pbk, t2=t2: e.tensor_tensor(out=t2.ap[0:96, 0:n], in0=pbk.ap[0:96, 0:n], in1=RSN.ap[0:96, 0:n], op=ALU.mult),
                     r=[pbk, RSN], w=[t2])
                P.do(pool, lambda e, t1=t1, t2=t2, qo=qo: e.tensor_tensor(out=qo.ap[0:96, 0:n], in0=t1.ap[0:96, 0:n], in1=t2.ap[0:96, 0:n], op=ALU.add),
                     r=[t1, t2], w=[qo])
                P.store(sp, qo, qo.ap[0:96, 0:n], QT[h, :, t0:t0 + n])
            for h in range(8):
                pkh = nb()
                P.do(pe, lambda e, pkh=pkh, h=h: e.matmul(pkh.ap[0:96, 0:n], WKN.ap[:, h * 96:(h + 1) * 96], NKV.ap[:, 0:n], start=True, stop=True),
                     r=[WKN, NKV], w=[pkh])
                ko = KO[h % 2]
                P.do(dve, lambda e, pkh=pkh, ko=ko: e.tensor_tensor(out=ko.ap[0:96, 0:n], in0=pkh.ap[0:96, 0:n], in1=ROT.ap[0:96, 0:n], op=ALU.add),
                     r=[pkh, ROT], w=[ko])
                if s == 1:
                    P.store(sp, ko, ko.ap[0:96, 0:n], KCX[h * 96:(h + 1) * 96, t0:t0 + n])
                else:
                    P.store(sp, ko, ko.ap[0:96, 0:n], KL[h // 2][(h % 2) * 96:(h % 2 + 1) * 96, t0 - NCTX:t0 - NCTX + n])
            for tb in range(n // 128):
                pvv = nb()
                P.do(pe, lambda e, pvv=pvv, tb=tb: e.matmul(pvv.ap[:, 0:512], NKV.ap[:, tb * 128:(tb + 1) * 128], WV.ap, start=True, stop=True),
                     r=[NKV, WV], w=[pvv])
                vo = VO[tb % 2]
                P.do(act, lambda e, pvv=pvv, vo=vo: e.activation(out=vo.ap, in_=pvv.ap, func=AF.Copy), r=[pvv], w=[vo])
                if s == 1:
                    P.store(sp, vo, vo.ap, VCX[t0 + tb * 128:t0 + (tb + 1) * 128, :])
                else:
                    tl_ = t0 - NCTX + tb * 128
                    P.store(sp, vo, vo.ap, VL[tl_ // 2048][tl_ % 2048:tl_ % 2048 + 128, :])

    def mix0_exchange():
        A.reset()
        for i in range(4):
            allgather(KL[i], KG[i])
        for i in range(2):
            allgather(VL[i], VG[i])
        allgather(CVH, CVG)
        A.reset()
        HB = A.f32(64, "hb"); HO = A.f32(8, "ho")
        hb3 = v3(HB.ap, 8)
        P.load(sp, HB, hb3, CVG.rearrange("(r p) k -> p r k", p=128))
        P.do(dve, lambda e: e.tensor_scalar(out=HO.ap[:, 0:4], in0=hb3[:, 0:4, 0], scalar1=SEL.ap[:, 0:1], scalar2=None, op0=ALU.mult), r=[HB, SEL], w=[HO])
        P.do(dve, lambda e: e.scalar_tensor_tensor(out=HO.ap[:, 0:4], in0=hb3[:, 4:8, 0], scalar=SEL.ap[:, 1:2], in1=HO.ap[:, 0:4],
                                                   op0=ALU.mult, op1=ALU.add), r=[HB, SEL, HO], w=[HO])
        for c in range(4):
            P.store(sp, HO, HO.ap[:, c:c + 1], CV[c * 128:(c + 1) * 128, NT + 2:NT + 3], slow=True)

    def mix0_attn():
        A.reset()
        NKT = NKEY // 128
        KH = [A.bf16(NKEY, "kh0"), A.bf16(NKEY, "kh1")]
        QH = [A.bf16(NT, "qh0"), A.bf16(NT, "qh1")]
        VA = [A.bf16(NKT * 128, "va0"), A.bf16(NKT * 128, "va1")]
        PT = [[A.bf16(512, f"pt{i}a"), A.bf16(512, f"pt{i}b")] for i in range(3)]
        RCP = A.f32(1024, "rcp")
        BO = [A.bf16(1024, "bo0"), A.bf16(1024, "bo1")]
        SP_ = [[Buf(psall[:, (2 * i + j) * 512:(2 * i + j + 1) * 512], f"s{i}{j}") for j in range(2)] for i in range(2)]
        OP_ = [[Buf(psall[:, (4 + 2 * i + j) * 512:(4 + 2 * i + j + 1) * 512], f"o{i}{j}") for j in range(2)] for i in range(2)]
        for i in range(2):
            va3 = v3(VA[i].ap, 128)
            P.do(dve, lambda e, va3=va3: e.memset(va3[:, :, 64:128], 1.0), w=[VA[i]])
        it = 0
        qtiles = [(0, NCTX, 2)] + [(NCTX + i * 1024, 1024, NKT) for i in range(SEQ_L // 1024)]
        for h in range(8):
            K_ = KH[h % 2]; Q_ = QH[h % 2]; V_ = VA[h % 2]
            va3 = v3(V_.ap, 128)
            P.load(sp, K_, K_.ap[0:96, 0:NCTX], KCX[h * 96:(h + 1) * 96, :])
            for r_ in range(2):
                P.load(sp, K_, K_.ap[0:96, NCTX + r_ * SEQ_L:NCTX + (r_ + 1) * SEQ_L],
                       KG[h // 2][r_ * 192 + (h % 2) * 96:r_ * 192 + (h % 2 + 1) * 96, :])
            P.load(sp, Q_, Q_.ap[0:96, :], QT[h, :, :])
            P.load(sp, V_, va3[:, 0:2, 0:64], VCX[:, h * 64:(h + 1) * 64].rearrange("(kt p) d -> p kt d", p=128))
            for r_ in range(2):
                for j_ in range(2):
                    k0 = 2 + r_ * 32 + j_ * 16
                    P.load(sp, V_, va3[:, k0:k0 + 16, 0:64],
                           VG[j_][r_ * 2048:(r_ + 1) * 2048, h * 64:(h + 1) * 64].rearrange("(kt p) d -> p kt d", p=128))
            for (q0, nq, nkt) in qtiles:
                O_ = OP_[it % 2]
                halves = [(0, min(512, nq))] + ([(512, 512)] if nq > 512 else [])

                def emit_qk(kt):
                    for hi, (c0, cn) in enumerate(halves):
                        S_ = SP_[kt % 2][hi]
                        P.do(pe, lambda e, S_=S_, kt=kt, c0=c0, cn=cn: e.matmul(
                            S_.ap[:, 0:cn], K_.ap[0:96, kt * 128:(kt + 1) * 128], Q_.ap[0:96, q0 + c0:q0 + c0 + cn],
                            start=True, stop=True), r=[K_, Q_], w=[S_])

                emit_qk(0)
                for kt in range(nkt):
                    if kt + 1 < nkt:
                        emit_qk(kt + 1)
                    for hi, (c0, cn) in enumerate(halves):
                        S_ = SP_[kt % 2][hi]; pt = PT[kt % 3][hi]
                        P.do(act, lambda e, S_=S_, pt=pt, cn=cn: e.activation(out=pt.ap[:, 0:cn], in_=S_.ap[:, 0:cn], func=AF.Exp, scale=MLA_SCALE),
                             r=[S_], w=[pt])
                    for hi, (c0, cn) in enumerate(halves):
                        pt = PT[kt % 3][hi]; Oh = O_[hi]
                        P.do(pe, lambda e, Oh=Oh, kt=kt, cn=cn, pt=pt: e.matmul(
                            Oh.ap[:, 0:cn], va3[:, kt, :], pt.ap[:, 0:cn],
                            start=(kt == 0), stop=(kt == nkt - 1)), r=[V_, pt], w=[Oh])
                bo = BO[it % 2]
                for hi, (c0, cn) in enumerate(halves):
                    Oh = O_[hi]
                    P.do(dve, lambda e, Oh=Oh, c0=c0, cn=cn: e.reciprocal(out=RCP.ap[64:128, c0:c0 + cn], in_=Oh.ap[64:128, 0:cn]), r=[Oh], w=[RCP])
                    P.do(dve, lambda e, Oh=Oh, bo=bo, c0=c0, cn=cn: e.tensor_tensor(out=bo.ap[0:64, c0:c0 + cn], in0=Oh.ap[0:64, 0:cn], in1=RCP.ap[64:128, c0:c0 + cn], op=ALU.mult),
                         r=[Oh, RCP], w=[bo])
                P.store(sp, bo, bo.ap[0:64, 0:nq], BT[h * 64:(h + 1) * 64, q0:q0 + nq])
                it += 1

    def mix0_out(src, dst, TT=512):
        A.reset()
        WO = A.bf16(KC * D, "wo"); wo3 = v3(WO.ap, D)
        load_w_bf16(WO, wo3, e_wout, KC)
        XB = [A.f32(KC * TT, "xa"), A.f32(KC * TT, "xb")]
        GBt = [A.bf16(4 * TT, "gb0"), A.bf16(4 * TT, "gb1")]
        CVw = [A.f32(4 * (TT + 2), "cvw0"), A.f32(4 * (TT + 2), "cvw1")]
        BTt = [A.bf16(4 * TT, "bt0"), A.bf16(4 * TT, "bt1")]
        ACC = [A.f32(TT, "acc0"), A.f32(TT, "acc1")]
        AT = A.bf16(4 * TT, "at")
        cw3 = v3(CONVW.ap, 3)
        for it, (t0, n, s) in enumerate(tiles_of(TT, True)):
            X = XB[it % 2]; gb = GBt[it % 2]; cvw = CVw[it % 2]; bt = BTt[it % 2]
            x3 = v3(X.ap[:, 0:KC * n], n)
            gb3 = v3(gb.ap[:, 0:4 * n], n); cv3 = v3(cvw.ap[:, 0:4 * (n + 2)], n + 2); bt3 = v3(bt.ap[:, 0:4 * n], n)
            at3 = v3(AT.ap[:, 0:4 * n], n)
            P.load(sp, X, x3, src.rearrange("(k p) t -> p k t", p=128)[:, :, t0:t0 + n])
            P.load(sp, gb, gb3, GB.rearrange("(c p) t -> p c t", p=128)[:, :, t0:t0 + n])
            cc = cv_col(t0)
            P.load(sp, cvw, cv3, CV.rearrange("(c p) t -> p c t", p=128)[:, :, cc - 1:cc + n + 1])
            P.load(sp, bt, bt3, BT.rearrange("(c p) t -> p c t", p=128)[:, :, t0:t0 + n])
            for c in range(4):
                acc = ACC[c % 2]
                P.do(dve, lambda e, c=c, acc=acc: e.tensor_scalar(out=acc.ap[:, 0:n], in0=cv3[:, c, 0:n], scalar1=cw3[:, c, 0:1], scalar2=None, op0=ALU.mult),
                     r=[cvw, CONVW], w=[acc])
                P.do(dve, lambda e, c=c, acc=acc: e.scalar_tensor_tensor(out=acc.ap[:, 0:n], in0=cv3[:, c, 1:n + 1], scalar=cw3[:, c, 1:2], in1=acc.ap[:, 0:n],
                                                                          op0=ALU.mult, op1=ALU.add), r=[cvw, CONVW, acc], w=[acc])
                P.do(dve, lambda e, c=c, acc=acc: e.scalar_tensor_tensor(out=acc.ap[:, 0:n], in0=cv3[:, c, 2:n + 2], scalar=cw3[:, c, 2:3], in1=acc.ap[:, 0:n],
                                                                          op0=ALU.mult, op1=ALU.add), r=[cvw, CONVW, acc], w=[acc])
                P.do(dve, lambda e, c=c, acc=acc: e.tensor_tensor(out=at3[:, c, :], in0=acc.ap[:, 0:n], in1=gb3[:, c, :], op=ALU.mult),
                     r=[acc, gb], w=[AT])
            g5 = hgvec(s, 1)
            for m in range(KC):
                po = nb()
                for c in range(8):
                    rhs = at3[:, c, :] if c < 4 else bt3[:, c - 4, :]
                    P.do(pe, lambda e, po=po, c=c, m=m, rhs=rhs: e.matmul(po.ap[:, 0:n], wo3[:, c, m * 128:(m + 1) * 128], rhs,
                                                                         start=(c == 0), stop=(c == 7)),
                         r=[WO, AT, bt], w=[po], inc=(c == 7))
                P.do(dve, lambda e, po=po, m=m: e.scalar_tensor_tensor(out=x3[:, m, :], in0=po.ap[:, 0:n], scalar=g5[:, m:m + 1],
                                                                       in1=x3[:, m, :], op0=ALU.mult, op1=ALU.add),
                     r=[po, HG, X], w=[X])
            P.store(sp, X, x3, dst.rearrange("(k p) t -> p k t", p=128)[:, :, t0:t0 + n])

    def mix1_dir(src, dst, direction, TT=256):
        bwd = direction == 1
        A.reset()
        ncols = 3
        WIN = A.bf16(KC * ncols * D, "owin"); win3 = v3(WIN.ap, ncols * D)
        for blk, srcblk in enumerate([0, 1, 2 + direction]):
            for kc in range(KC):
                P.load(pool, WIN, win3[:, kc, blk * D:(blk + 1) * D], o_win[kc * 128:(kc + 1) * 128, srcblk * D:(srcblk + 1) * D])
        NCH = TT // 64
        XB = [A.f32(KC * TT, "xa"), A.f32(KC * TT, "xb")]
        U2 = [A.bf16(KC * TT, "u0"), A.bf16(KC * TT, "u1")]
        SQn = A.bf16(KC * TT, "sqn")
        RS = A.f32(TT, "rs")
        TMP = [A.f32(TT, "t0"), A.f32(TT, "t1")]
        TMP2 = [A.f32(TT, "t2"), A.f32(TT, "t3")]
        QF2 = [A.f32(8 * TT, "qf0"), A.f32(8 * TT, "qf1")]; FF2 = [A.f32(8 * TT, "ff0"), A.f32(8 * TT, "ff1")]
        LF = A.f32(8 * TT, "lf"); BB = A.f32(8 * TT, "bb"); EE = A.f32(8 * TT, "ee")
        QT_ = A.bf16(8 * TT, "qt"); KTL = A.bf16(8 * TT, "ktl"); KE = A.bf16(8 * TT, "ke")
        VV2 = [A.bf16(NCH * 8 * 128, "vv0"), A.bf16(NCH * 8 * 128, "vv1")]
        OO = A.f32(8 * TT, "oo")
        DEC = A.f32(8 * NCH, "dec")
        M64 = A.f32(8 * TT, "m64")
        ST = A.f32(8 * 128, "st"); STBS = [A.bf16(8 * 128, "stb0"), A.bf16(8 * 128, "stb1")]
        KET = [A.bf16(1024, f"ket{i}") for i in range(2)]
        ATM = [A.bf16(512, f"atm{i}") for i in range(2)]
        stb_i = [0]; k_it = [0]
        st3 = v3(ST.ap, 128)
        P.do(dve, lambda e: e.memset(M64.ap, 1.0), w=[M64])
        P.do(dve, lambda e: e.memset(v3(M64.ap, 64)[:, :, 0:1], 0.0), w=[M64])
        mask_ap = MASK.ap[:, 64:128] if bwd else MASK.ap[:, 0:64]
        tl = tiles_of(TT, True)
        ctx_tiles = [t for t in tl if t[2] == 1]
        lat_tiles = [t for t in tl if t[2] == 0]
        order = (ctx_tiles + lat_tiles) if not bwd else lat_tiles[::-1]
        if not bwd:
            P.do(dve, lambda e: e.memset(ST.ap, 0.0), w=[ST])
            P.do(dve, lambda e: e.memset(STBS[0].ap, 0.0), w=[STBS[0]])
        else:
            P.load(sp, ST, ST.ap, SGA[0:128, :])
            P.load(sp, OO, OO.ap[:, 0:1024], SGA[128:256, :])
            P.do(dve, lambda e: e.tensor_scalar(out=ST.ap, in0=ST.ap, scalar1=SEL.ap[:, 0:1], scalar2=None, op0=ALU.mult), r=[ST, SEL], w=[ST])
            P.do(dve, lambda e: e.scalar_tensor_tensor(out=ST.ap, in0=OO.ap[:, 0:1024], scalar=SEL.ap[:, 1:2], in1=ST.ap, op0=ALU.mult, op1=ALU.add),
                 r=[OO, SEL, ST], w=[ST])
            P.do(pool, lambda e: e.tensor_copy(out=STBS[0].ap, in_=ST.ap), r=[ST], w=[STBS[0]])
        lbv = v3(LB.ap, 8)[:, direction, :]
        omlv = v3(OML.ap, 8)[:, direction, :]
        tctx = {}

        def stageA(it, gen=None):
            (t0, n, s) = order[it]
            X = XB[it % 2]; U = U2[it % 2]; QF = QF2[it % 2]; FF = FF2[it % 2]; VV = VV2[it % 2]
            x3, u3 = norm_mod(X, U, SQn, RS, TMP, n, s, 1, src, t0, lnexp=True)
            qf3 = v3(QF.ap, n); ff3 = v3(FF.ap, n)
            vv4 = VV.ap.rearrange("p (c h d) -> p c h d", h=8, d=128)
            tctx[it] = (x3, u3)
            for h in range(8):
                pq = nb()
                for kc in range(KC):
                    P.do(pe, lambda e, kc=kc, h=h, pq=pq: e.matmul(pq.ap[:, 0:n], win3[:, kc, h * 128:(h + 1) * 128], u3[:, kc, :],
                                                                   start=(kc == 0), stop=(kc == KC - 1)), r=[WIN, U], w=[pq], inc=(kc == KC - 1))
                P.do(act, lambda e, pq=pq, h=h: e.activation(out=qf3[:, h, :], in_=pq.ap[:, 0:n], func=AF.Copy), r=[pq], w=[QF])
                pz = nb()
                for kc in range(KC):
                    P.do(pe, lambda e, kc=kc, h=h, pz=pz: e.matmul(pz.ap[:, 0:n], win3[:, kc, 2 * D + h * 128:2 * D + (h + 1) * 128], u3[:, kc, :],
                                                                   start=(kc == 0), stop=(kc == KC - 1)), r=[WIN, U], w=[pz], inc=(kc == KC - 1))
                P.do(act, lambda e, pz=pz, h=h: e.activation(out=ff3[:, h, :], in_=pz.ap[:, 0:n], func=AF.Exp, scale=-1.0), r=[pz], w=[FF])
                for c in range(n // 64):
                    pvv = nb()
                    for kc in range(KC):
                        P.do(pe, lambda e, kc=kc, h=h, c=c, pvv=pvv: e.matmul(pvv.ap[0:64, 0:128], u3[:, kc, c * 64:(c + 1) * 64],
                                                                              win3[:, kc, D + h * 128:D + (h + 1) * 128],
                                                                              start=(kc == 0), stop=(kc == KC - 1)),
                             r=[WIN, U], w=[pvv], inc=(kc == KC - 1))
                    P.do(act, lambda e, pvv=pvv, c=c, h=h: e.activation(out=vv4[0:64, c, h, :], in_=pvv.ap[0:64, 0:128], func=AF.Copy), r=[pvv], w=[VV])
                if gen is not None:
                    for _ in range(3):
                        next(gen, None)

        def stageB(it):
            (t0, n, s) = order[it]
            QF = QF2[it % 2]; FF = FF2[it % 2]
            ff3 = v3(FF.ap, n)
            NN = 8 * n
            P.do(dve, lambda e: e.tensor_scalar(out=FF.ap[:, 0:NN], in0=FF.ap[:, 0:NN], scalar1=1.0, scalar2=None, op0=ALU.add), r=[FF], w=[FF])
            yield
            P.do(dve, lambda e: e.reciprocal(out=FF.ap[:, 0:NN], in_=FF.ap[:, 0:NN]), r=[FF], w=[FF])
            yield
            P.do(dve, lambda e: e.tensor_tensor(out=ff3, in0=ff3, in1=omlv.unsqueeze(2).broadcast_to([128, 8, n]), op=ALU.mult), r=[FF, OML], w=[FF])
            yield
            P.do(dve, lambda e: e.tensor_tensor(out=ff3, in0=ff3, in1=lbv.unsqueeze(2).broadcast_to([128, 8, n]), op=ALU.add), r=[FF, LB], w=[FF])
            yield
            P.do(act, lambda e: e.activation(out=LF.ap[:, 0:NN], in_=FF.ap[:, 0:NN], func=AF.Ln), r=[FF], w=[LF])
            yield
            P.do(dve, lambda e: e.tensor_scalar(out=FF.ap[:, 0:NN], in0=FF.ap[:, 0:NN], scalar1=-1.0, scalar2=1.0, op0=ALU.mult, op1=ALU.add), r=[FF], w=[FF])
            yield
            P.do(dve, lambda e: e.tensor_tensor_scan(out=BB.ap[:, 0:NN], data0=M64.ap[:, 0:NN], data1=LF.ap[:, 0:NN], initial=0.0,
                                                     op0=ALU.mult, op1=ALU.add), r=[M64, LF], w=[BB])
            yield
            bb4 = v3(BB.ap[:, 0:NN], 64)
            tot = bb4[:, :, 63:64]
            nchk = NN // 64
            totb = tot.broadcast_to([128, nchk, 64])
            P.do(act, lambda e: e.activation(out=DEC.ap[:, 0:nchk], in_=bb4[:, :, 63], func=AF.Exp), r=[BB], w=[DEC])
            yield
            if bwd:
                P.do(dve, lambda e: e.tensor_tensor(out=LF.ap[:, 0:NN], in0=LF.ap[:, 0:NN], in1=BB.ap[:, 0:NN], op=ALU.subtract), r=[LF, BB], w=[LF])
                yield
                P.do(dve, lambda e: e.tensor_tensor(out=v3(LF.ap[:, 0:NN], 64), in0=v3(LF.ap[:, 0:NN], 64), in1=totb, op=ALU.add), r=[LF, BB], w=[LF])
                yield
                Bcur = LF
            else:
                Bcur = BB
            P.do(act, lambda e: e.activation(out=EE.ap[:, 0:NN], in_=Bcur.ap[:, 0:NN], func=AF.Exp), r=[Bcur], w=[EE])
            yield
            P.do(dve, lambda e: e.scalar_tensor_tensor(out=QT_.ap[:, 0:NN], in0=QF.ap[:, 0:NN], scalar=HGRN_SCALE, in1=EE.ap[:, 0:NN],
                                                       op0=ALU.mult, op1=ALU.mult), r=[QF, EE], w=[QT_])
            yield
            P.do(act, lambda e: e.activation(out=EE.ap[:, 0:NN], in_=Bcur.ap[:, 0:NN], func=AF.Exp, scale=-1.0), r=[Bcur], w=[EE])
            yield
            P.do(pool, lambda e: e.tensor_tensor(out=KTL.ap[:, 0:NN], in0=FF.ap[:, 0:NN], in1=EE.ap[:, 0:NN], op=ALU.mult), r=[FF, EE], w=[KTL])
            yield
            P.do(pool, lambda e: e.tensor_tensor(out=v3(QF.ap[:, 0:NN], 64), in0=totb, in1=v3(Bcur.ap[:, 0:NN], 64), op=ALU.subtract), r=[Bcur, BB], w=[QF])
            yield
            P.do(act, lambda e: e.activation(out=EE.ap[:, 0:NN], in_=QF.ap[:, 0:NN], func=AF.Exp), r=[QF], w=[EE])
            yield
            P.do(dve, lambda e: e.tensor_tensor(out=KE.ap[:, 0:NN], in0=FF.ap[:, 0:NN], in1=EE.ap[:, 0:NN], op=ALU.mult), r=[FF, EE], w=[KE])
            yield

        def stageC(it):
            (t0, n, s) = order[it]
            X = XB[it % 2]; U = U2[it % 2]; VV = VV2[it % 2]; OFt = QF2[it % 2]
            x3, u3 = tctx.pop(it)
            NN = 8 * n
            nchk = NN // 64
            qt3 = v3(QT_.ap, n); kt3 = v3(KTL.ap, n); ke3 = v3(KE.ap, n); oo3 = v3(OO.ap, n)
            vv4 = VV.ap.rearrange("p (c h d) -> p c h d", h=8, d=128)
            dec3 = v3(DEC.ap[:, 0:nchk], n // 64)
            chunks = list(range(n // 64))
            if bwd:
                chunks = chunks[::-1]
            kets, atms = {}, {}

            def stage1(ci):
                c = chunks[ci]
                cs = slice(c * 64, (c + 1) * 64)
                ket = KET[ci % 2]; atm = ATM[ci % 2]
                kets[c], atms[c] = ket, atm
                ptr = nb()
                ptb = ptr.ap.bitcast(BF16)
                for h in range(8):
                    P.do(pe, lambda e, h=h, cs=cs, ptb=ptb: e.transpose(ptb[0:64, h * 128:(h + 1) * 128], ke3[:, h, cs], IDB.ap),
                         r=[KE, IDB], w=[ptr], inc=(h == 7))
                P.do(act, lambda e, ptb=ptb, ket=ket: e.activation(out=ket.ap[0:64, 0:1024], in_=ptb[0:64, 0:1024], func=AF.Copy), r=[ptr], w=[ket])
                if s == 0:
                    pat = nb()
                    for h in range(8):
                        P.do(pe, lambda e, pat=pat, h=h, cs=cs: e.matmul(pat.ap[0:64, h * 64:(h + 1) * 64], kt3[:, h, cs], qt3[:, h, cs], start=True, stop=True),
                             r=[KTL, QT_], w=[pat], inc=(h == 7))
                    P.do(dve, lambda e, pat=pat, atm=atm: e.tensor_tensor(out=v3(atm.ap[0:64, 0:512], 64), in0=v3(pat.ap[0:64, 0:512], 64),
                                                                          in1=mask_ap.unsqueeze(1).broadcast_to([64, 8, 64]), op=ALU.mult),
                         r=[pat, MASK], w=[atm])

            stage1(0)
            if len(chunks) > 1:
                stage1(1)

            def emit_ds(c):
                ket = kets[c]
                pd = [nb(), nb()]
                for h in range(8):
                    P.do(pe, lambda e, pd=pd, ket=ket, c=c, h=h: e.matmul(pd[h // 4].ap[:, (h % 4) * 128:(h % 4 + 1) * 128], ket.ap[0:64, h * 128:(h + 1) * 128],
                                                                          vv4[0:64, c, h, :], start=True, stop=True),
                         r=[ket, VV], w=[pd[h // 4]], inc=(h % 4 == 3))
                return pd

            pds = {chunks[0]: emit_ds(chunks[0])}
            for ci, c in enumerate(chunks):
                cs = slice(c * 64, (c + 1) * 64)
                if ci + 1 < len(chunks):
                    pds[chunks[ci + 1]] = emit_ds(chunks[ci + 1])
                stb_cur = STBS[stb_i[0] % 2]; stb_nxt = STBS[(stb_i[0] + 1) % 2]
                stb_i[0] += 1
                if s == 0:
                    atm = atms[c]
                    po = nb()
                    stc3 = v3(stb_cur.ap, 128)
                    for h in range(8):
                        P.do(pe, lambda e, po=po, atm=atm, c=c, h=h: e.matmul(po.ap[:, h * 64:(h + 1) * 64], vv4[0:64, c, h, :], atm.ap[0:64, h * 64:(h + 1) * 64],
                                                                              start=True, stop=False), r=[VV, atm], w=[po], inc=False)
                        P.do(pe, lambda e, po=po, h=h, cs=cs, stc3=stc3: e.matmul(po.ap[:, h * 64:(h + 1) * 64], stc3[:, h, :], qt3[:, h, cs], start=False, stop=True),
                             r=[stb_cur, QT_], w=[po], inc=(h == 7))
                    P.do(act, lambda e, po=po, cs=cs: e.activation(out=oo3[:, :, cs], in_=v3(po.ap[:, 0:512], 64), func=AF.Copy), r=[po], w=[OO])
                pd = pds[c]
                P.do(dve, lambda e, c=c: e.tensor_tensor(out=st3, in0=st3, in1=dec3[:, :, c:c + 1].broadcast_to([128, 8, 128]), op=ALU.mult),
                     r=[ST, DEC], w=[ST])
                for half in range(2):
                    P.do(dve, lambda e, half=half, pd=pd: e.tensor_tensor(out=ST.ap[:, half * 512:(half + 1) * 512], in0=ST.ap[:, half * 512:(half + 1) * 512],
                                                                          in1=pd[half].ap[:, 0:512], op=ALU.add), r=[ST, pd[half]], w=[ST])
                P.do(pool, lambda e, stb_nxt=stb_nxt: e.tensor_copy(out=stb_nxt.ap, in_=ST.ap), r=[ST], w=[stb_nxt])
                if ci + 2 < len(chunks):
                    stage1(ci + 2)
            if s == 1:
                return
            if not bwd:
                P.store(sp, OO, oo3, OF.rearrange("(h p) t -> p h t", p=128)[:, :, t0 - NCTX:t0 - NCTX + n])
                return
            P.load(sp, OFt, v3(OFt.ap, n), OF.rearrange("(h p) t -> p h t", p=128)[:, :, t0 - NCTX:t0 - NCTX + n])
            P.do(dve, lambda e: e.tensor_tensor(out=OO.ap[:, 0:NN], in0=OO.ap[:, 0:NN], in1=OFt.ap[:, 0:NN], op=ALU.add), r=[OO, OFt], w=[OO])
            P.store(sp, OO, oo3, OF.rearrange("(h p) t -> p h t", p=128)[:, :, t0 - NCTX:t0 - NCTX + n])

        stageA(0)
        for it in range(len(order)):
            gen = stageB(it)
            if it + 1 < len(order):
                stageA(it + 1, gen)
            for _ in gen:
                pass
            stageC(it)
        if not bwd:
            P.store(sp, ST, ST.ap, SX[:, :])

    def mix1_out(src, dst, TT=512):
        A.reset()
        WG = A.bf16(KC * D, "owg"); wg3 = v3(WG.ap, D)
        for kc in range(KC):
            P.load(pool, WG, wg3[:, kc, :], o_win[kc * 128:(kc + 1) * 128, 4 * D:5 * D])
        WO = A.bf16(KC * D, "owo"); wo3 = v3(WO.ap, D)
        load_w_bf16(WO, wo3, o_wout, KC)
        XB = [A.f32(KC * TT, "xa"), A.f32(KC * TT, "xb")]
        U2 = [A.bf16(KC * TT, "u0"), A.bf16(KC * TT, "u1")]; SQ = A.bf16(KC * TT, "sq")
        RS = A.f32(TT, "rs")
        TMP = [A.f32(TT, "t0"), A.f32(TT, "t1")]
        TMP2 = [A.f32(TT, "t2"), A.f32(TT, "t3")]
        OO2 = [A.f32(8 * TT, "oo0"), A.f32(8 * TT, "oo1")]
        KE2 = [A.bf16(8 * TT, "osq0"), A.bf16(8 * TT, "osq1")]; RR = A.bf16(8 * TT, "rr")
        SG2 = [A.f32(8 * TT, "sgall0"), A.f32(8 * TT, "sgall1")]
        tl = tiles_of(TT, False)
        tctx = {}

        def stageP(it, gen=None):
            (t0, n, s) = tl[it]
            X = XB[it % 2]; OO = OO2[it % 2]; U = U2[it % 2]; KE = KE2[it % 2]; SGA_ = SG2[it % 2]
            NN = 8 * n
            oo3 = v3(OO.ap, n); ke3 = v3(KE.ap, n)
            P.load(sp, OO, oo3, OF.rearrange("(h p) t -> p h t", p=128)[:, :, t0 - NCTX:t0 - NCTX + n])
            x3, u3 = norm_mod(X, U, SQ, RS, TMP, n, s, 1, src, t0, lnexp=True)
            P.do(act, lambda e: e.activation(out=KE.ap[:, 0:NN], in_=OO.ap[:, 0:NN], func=AF.Square), r=[OO], w=[KE])
            sg3 = v3(SGA_.ap, n)
            tctx[it] = (x3, u3)
            for h in range(8):
                pg = nb()
                for kc in range(KC):
                    P.do(pe, lambda e, kc=kc, h=h, pg=pg: e.matmul(pg.ap[:, 0:n], wg3[:, kc, h * 128:(h + 1) * 128], u3[:, kc, :],
                                                                   start=(kc == 0), stop=(kc == KC - 1)), r=[WG, U], w=[pg], inc=(kc == KC - 1))
                P.do(act, lambda e, pg=pg, h=h: e.activation(out=sg3[:, h, :], in_=pg.ap[:, 0:n], func=AF.Exp, scale=-1.0), r=[pg], w=[SGA_])
                P.do(dve, lambda e, h=h: e.tensor_scalar(out=sg3[:, h, :], in0=sg3[:, h, :], scalar1=1.0, scalar2=None, op0=ALU.add), r=[SGA_], w=[SGA_])
                P.do(dve, lambda e, h=h: e.reciprocal(out=sg3[:, h, :], in_=sg3[:, h, :]), r=[SGA_], w=[SGA_])
                P.do(dve, lambda e, h=h, pg=pg: e.tensor_tensor(out=sg3[:, h, :], in0=pg.ap[:, 0:n], in1=sg3[:, h, :], op=ALU.mult), r=[SGA_, pg], w=[SGA_])
                if gen is not None:
                    next(gen, None)

        def stageR(it):
            (t0, n, s) = tl[it]
            X = XB[it % 2]; OO = OO2[it % 2]; KE = KE2[it % 2]; SGA_ = SG2[it % 2]
            oo3 = v3(OO.ap, n); ke3 = v3(KE.ap, n); sg3 = v3(SGA_.ap, n); rr3 = v3(RR.ap, n)
            for h in range(8):
                pss = nb()
                P.do(pe, lambda e, pss=pss, h=h: e.matmul(pss.ap[:, 0:n], ONESB.ap, ke3[:, h, :], start=True, stop=True), r=[KE, ONESB], w=[pss])
                T = TMP2[h % 2]
                P.do(act, lambda e, pss=pss, T=T: e.activation(out=T.ap[:, 0:n], in_=pss.ap[:, 0:n], func=AF.Ln, bias=EPSB.ap[:, 0:1], scale=1.0 / 128),
                     r=[pss, EPSB], w=[T])
                P.do(act, lambda e, T=T: e.activation(out=T.ap[:, 0:n], in_=T.ap[:, 0:n], func=AF.Exp, scale=-0.5), r=[T], w=[T])
                P.do(dve, lambda e, T=T, h=h: e.scalar_tensor_tensor(out=oo3[:, h, :], in0=oo3[:, h, :], scalar=GN.ap[:, 0:1], in1=T.ap[:, 0:n],
                                                                     op0=ALU.mult, op1=ALU.mult), r=[OO, GN, T], w=[OO])
                P.do(pool, lambda e, h=h: e.tensor_tensor(out=rr3[:, h, :], in0=oo3[:, h, :], in1=sg3[:, h, :], op=ALU.mult), r=[SGA_, OO], w=[RR])
                yield

        def stageW(it):
            (t0, n, s) = tl[it]
            X = XB[it % 2]
            x3, u3 = tctx.pop(it)
            rr3 = v3(RR.ap, n)
            g5 = hgvec(s, 1)
            for m in range(KC):
                po = nb()
                for h in range(8):
                    P.do(pe, lambda e, po=po, h=h, m=m: e.matmul(po.ap[:, 0:n], wo3[:, h, m * 128:(m + 1) * 128], rr3[:, h, :],
                                                                 start=(h == 0), stop=(h == 7)), r=[WO, RR], w=[po], inc=(h == 7))
                P.do(dve, lambda e, po=po, m=m: e.scalar_tensor_tensor(out=x3[:, m, :], in0=po.ap[:, 0:n], scalar=g5[:, m:m + 1],
                                                                       in1=x3[:, m, :], op0=ALU.mult, op1=ALU.add),
                     r=[po, HG, X], w=[X])
            P.store(sp, X, x3, dst.rearrange("(k p) t -> p k t", p=128)[:, :, t0:t0 + n])

        stageP(0)
        for it in range(len(tl)):
            gen = stageR(it)
            if it + 1 < len(tl):
                stageP(it + 1, gen)
            for _ in gen:
                pass
            stageW(it)


    def mix1_exchange():
        A.reset()
        allgather(SX, SGA)

    steps = []
    steps.append(lambda: ada_phase(0))
    steps.append(lambda: ffn_phase(0, 0, xin, XT))
    steps.append(lambda: mix0_proj(XT))
    steps.append(lambda: mix0_exchange())
    steps.append(lambda: mix0_attn())
    steps.append(lambda: mix0_out(XT, XT))
    steps.append(lambda: ffn_phase(0, 2, XT, XT))
    steps.append(lambda: ada_phase(1))
    steps.append(lambda: ffn_phase(1, 0, XT, XT))
    steps.append(lambda: mix1_dir(XT, XT, 0))
    steps.append(lambda: mix1_exchange())
    steps.append(lambda: mix1_dir(XT, XT, 1))
    steps.append(lambda: mix1_out(XT, XT))
    steps.append(lambda: ffn_phase(1, 2, XT, XT, with_ctx=False))
    steps.append(lambda: final_phase(XT))
    for i, st in enumerate(steps):
        if i >= stop_after:
            break
        st()
    return P.finish()


def _rope_tables(pos):
    row = (pos // 64).astype(np.float32)
    col = (pos % 64).astype(np.float32)
    inv = (10000.0 ** (-np.arange(8, dtype=np.float32) / 8)).astype(np.float32)
    ang = np.stack([row[:, None] * inv, col[:, None] * inv], axis=1)
    cos, sin = np.cos(ang).astype(np.float32), np.sin(ang).astype(np.float32)
    C = np.ones((96, NT), np.float32)
    S = np.zeros((96, NT), np.float32)
    for ax in range(2):
        for half in range(2):
            r0 = 64 + ax * 16 + half * 8
            C[r0:r0 + 8, NCTX:] = cos[:, ax, :].T
            S[r0:r0 + 8, NCTX:] = (-sin[:, ax, :].T) if half == 0 else sin[:, ax, :].T
    return C, S


_PERM = np.concatenate([np.arange(8, 16), np.arange(0, 8), np.arange(24, 32), np.arange(16, 24)])

_NC_CACHE = {}


def _prep_inputs(inp):
    f = lambda a: np.ascontiguousarray(np.asarray(a, dtype=np.float32))
    x, c, ctx, c_ctx = f(inp["x"]), f(inp["c"]), f(inp["ctx"]), f(inp["c_ctx"])
    ewin = f(inp["even_w_in"])[0]
    wuq = f(inp["mla_w_uq"])[0].reshape(256, 8, 96)
    wukv = f(inp["mla_w_ukv"])[0].reshape(128, 8, 128)
    wkr = np.zeros((D, 96), np.float32); wkr[:, 64:] = ewin[:, 1920:1952]
    wkrp = np.zeros((D, 96), np.float32); wkrp[:, 64:] = ewin[:, 1920:1952][:, _PERM]
    wqb = np.zeros((256, 8, 96), np.float32); wqb[:, :, 64:] = wuq[:, :, 64:][:, :, _PERM]
    wkn = np.zeros((128, 8, 96), np.float32); wkn[:, :, :64] = wukv[:, :, :64]
    wv = np.ascontiguousarray(wukv[:, :, 64:]).reshape(128, 512)
    tri = np.triu(np.ones((64, 64), np.float32))
    cmask = np.concatenate([tri, tri.T], axis=1)
    conv = f(inp["even_conv_w"])[0]
    owin = f(inp["odd_w_in"])[0].reshape(D, 5, D)
    olb = f(inp["hgrn_lb_logits"])
    shared = {
        "ada_w": f(inp["ada_w"]), "ada_b": f(inp["ada_b"]), "norm_g": f(inp["norm_g"]),
        "ffn_w1": f(inp["ffn_w1"]), "ffn_w3": f(inp["ffn_w3"]), "ffn_w2": f(inp["ffn_w2"]),
        "e_win": np.ascontiguousarray(ewin[:, :1920]), "e_wkr": wkr, "e_wkrp": wkrp,
        "e_qg": f(inp["mla_q_norm_g"])[0],
        "e_wqa": np.ascontiguousarray(wuq).reshape(256, 768), "e_wqb": wqb.reshape(256, 768),
        "e_kvg": f(inp["mla_kv_norm_g"])[0], "e_wkn": wkn.reshape(128, 768), "e_wv": wv,
        "e_wout": f(inp["even_w_out"])[0],
        "o_gn": f(inp["hgrn_g_norm_g"])[0],
        "o_wout": f(inp["odd_w_out"])[0], "fin_g": f(inp["final_norm_g"]),
        "cmask": cmask, "ident": np.eye(128, dtype=np.float32),
    }
    per_half = []
    for s in range(2):
        pos = np.arange(SEQ_L) if s == 0 else (SEQ - 1 - np.arange(SEQ_L))
        C, S = _rope_tables(pos)
        d1, d2 = (0, 1) if s == 0 else (1, 0)
        owin_p = np.ascontiguousarray(owin[:, [0, 1, 2 + d1, 2 + d2, 4], :]).reshape(D, 5 * D)
        olb_p = np.ascontiguousarray(olb[:, [d1, d2], :])
        conv_p = np.ascontiguousarray(conv if s == 0 else conv[::-1])
        sel = np.zeros((128, 2), np.float32); sel[:, 1 - s] = 1.0
        per_half.append({"ropeC": C, "ropeS": S, "o_win": owin_p, "o_lb": olb_p, "e_conv": conv_p, "selv": sel})
    maps = []
    for core in range(NCORES):
        b, s = core // 2, core % 2
        m = dict(shared)
        m.update(per_half[s])
        if s == 0:
            loc = np.concatenate([ctx[b], x[b, :SEQ_L]], axis=0)
        else:
            loc = np.concatenate([ctx[b][::-1], x[b, SEQ_L:][::-1]], axis=0)
        m["xin"] = np.ascontiguousarray(loc.T)
        cv = np.stack([c[b], c_ctx], axis=1)
        m["cvec"] = np.ascontiguousarray(cv.reshape(KC, 128, 2).transpose(1, 0, 2))
        maps.append(m)
    return maps


def kernel(**inputs):
    maps = _prep_inputs(inputs)
    if "nc" not in _NC_CACHE:
        _NC_CACHE["nc"] = build()
    res = run_bass_kernel_spmd(_NC_CACHE["nc"], maps, core_ids=list(range(NCORES)))
    out = np.empty((NCORES // 2, SEQ, D), np.float32)
    for core in range(NCORES):
        b, s = core // 2, core % 2
        y = res.results[core]["yout"].T
        if s == 0:
            out[b, :SEQ_L] = y
        else:
            out[b, SEQ_L:] = y[::-1]
    return out
```

```python
import numpy as np
from contextlib import ExitStack
import concourse.bass as bass
import concourse.mybir as mybir
from concourse.bass_utils import run_bass_kernel_spmd

F32 = mybir.dt.float32
BF16 = mybir.dt.bfloat16
AF = mybir.ActivationFunctionType
ALU = mybir.AluOpType

D = 1024
NCTX = 256
SEQ = 8192
SEQ_L = SEQ // 2
NT = NCTX + SEQ_L
NKEY = NCTX + SEQ
PAIRS = [[0, 1], [2, 3], [4, 5], [6, 7]]
DFF = 2816
NJ = DFF // 128
KC = 8
EPS = 1e-6
NCORES = 8
MLA_SCALE = 96 ** -0.5
HGRN_SCALE = 128 ** -0.5


import types


def _freeze(fn):
    if fn.__closure__ is None:
        return fn
    cells = []
    for c in fn.__closure__:
        try:
            cells.append(types.CellType(c.cell_contents))
        except ValueError:
            cells.append(c)
    g = types.FunctionType(fn.__code__, fn.__globals__, fn.__name__, fn.__defaults__, tuple(cells))
    g.__kwdefaults__ = fn.__kwdefaults__
    return g


class Tok:
    __slots__ = ("sem", "val")

    def __init__(self, sem, val):
        self.sem, self.val = sem, val


class DmaSem:
    def __init__(self, P, name):
        self.sem = P.new_sem(name)
        self.count = 0

    def tok(self):
        return Tok(self.sem, self.count)


class Buf:
    def __init__(self, ap, name=""):
        self.ap, self.name = ap, name
        self.w = {}
        self.r = {}
        self.ds = None


class Eng:
    def __init__(self, P, name):
        self.P, self.name = P, name
        self.ops = []
        self.sem = P.new_sem("p_" + name)
        self.count = 0
        self.waited = {}

    def _wait(self, toks):
        need = {}
        for t in toks:
            k = id(t.sem)
            if t.val > need.get(k, (None, 0))[1]:
                need[k] = (t.sem, t.val)
        for k, (sem, val) in need.items():
            if self.waited.get(k, 0) >= val:
                continue
            self.waited[k] = val
            self.ops.append(("wait", sem, val))

    def emit(self, fn, toks, inc=True):
        self._wait(toks)
        if inc:
            self.count += 1
            self.ops.append(("op", fn, self.sem, 1))
            return Tok(self.sem, self.count)
        self.ops.append(("op", fn, None, 0))
        return None

    def replay(self, e):
        for o in self.ops:
            if o[0] == "wait":
                e.wait_ge(o[1], o[2])
            else:
                ins = o[1](e)
                if o[2] is not None:
                    ins.then_inc(o[2], o[3])


class Prog:
    def __init__(self):
        self.nc = bass.Bass("TRN2", target_bir_lowering=False)
        self.es = ExitStack()
        self.nsem = 0
        self.pe = Eng(self, "pe")
        self.act = Eng(self, "act")
        self.dve = Eng(self, "dve")
        self.pool = Eng(self, "pool")
        self.sp = Eng(self, "sp")
        self.engs = [self.pe, self.act, self.dve, self.pool, self.sp]
        self.dsems = []
        self.free_ds = []
        self.pending_pe = []

    def new_sem(self, name):
        self.nsem += 1
        return self.es.enter_context(self.nc.semaphore(name + str(self.nsem)))

    def get_ds(self):
        if self.free_ds:
            return self.free_ds.pop()
        d = DmaSem(self, "d")
        self.dsems.append(d)
        return d

    def sb(self, name, shape, dt):
        return self.es.enter_context(self.nc.sbuf_tensor(name, list(shape), dt))

    def din(self, name, shape, dt=F32):
        return self.nc.dram_tensor(name, list(shape), dt, kind="ExternalInput").ap()

    def dout(self, name, shape, dt=F32):
        return self.nc.dram_tensor(name, list(shape), dt, kind="ExternalOutput").ap()

    def dint(self, name, shape, dt=F32):
        return self.nc.dram_tensor(name, list(shape), dt, kind="Internal").ap()

    def do(self, eng, fn, r=(), w=(), inc=True):
        fn = _freeze(fn)
        toks = []
        for b in r:
            toks.extend(b.w.values())
        for b in w:
            toks.extend(b.w.values())
            toks.extend(b.r.values())
        if eng is self.pe and not inc:
            eng._wait(toks)
            eng.ops.append(("op", fn, None, 0))
            self.pending_pe.append((r, w))
            return None
        tok = eng.emit(fn, toks, True)
        groups = [(r, w)]
        if eng is self.pe:
            groups += self.pending_pe
            self.pending_pe = []
        for (rr, ww) in groups:
            for b in rr:
                b.r[id(tok.sem)] = tok
            for b in ww:
                b.w = {id(tok.sem): tok}
                b.r = {}
        return tok

    def load(self, q, buf, dst_ap, src_ap, slow=False):
        kw = {"allow_slow_non_contiguous": True} if slow else {}
        if buf.ds is None:
            buf.ds = self.get_ds()
        ds = buf.ds
        toks = [t for t in buf.w.values() if t.sem is not ds.sem] + list(buf.r.values())
        q._wait(toks)
        ds.count += 16
        q.ops.append(("op", lambda e: e.dma_start(out=dst_ap, in_=src_ap, **kw), ds.sem, 16))
        tok = ds.tok()
        buf.w = {id(ds.sem): tok}
        buf.r = {}
        return tok

    def store(self, q, buf, src_ap, dst_ap, slow=False):
        kw = {"allow_slow_non_contiguous": True} if slow else {}
        if buf.ds is None:
            buf.ds = self.get_ds()
        ds = buf.ds
        q._wait(list(buf.w.values()))
        ds.count += 16
        q.ops.append(("op", lambda e: e.dma_start(out=dst_ap, in_=src_ap, **kw), ds.sem, 16))
        tok = ds.tok()
        buf.r[id(ds.sem)] = tok
        return tok

    def barrier(self):
        assert not self.pending_pe
        toks = [Tok(e.sem, e.count) for e in self.engs[:4] if e.count > 0]
        toks += [d.tok() for d in self.dsems if d.count > 0]
        for e in self.engs:
            e._wait(toks)

    def finish(self):
        self.barrier()
        with self.nc.Block() as block:
            @block.tensor
            def _(e):
                self.pe.replay(e)

            @block.scalar
            def _(e):
                self.act.replay(e)

            @block.vector
            def _(e):
                self.dve.replay(e)

            @block.gpsimd
            def _(e):
                self.pool.replay(e)

            @block.sync
            def _(e):
                self.sp.replay(e)
        self.es.close()
        return self.nc


class Arena:
    def __init__(self, P, ncols):
        self.P = P
        self.t = P.sb("arena", [128, ncols], F32)
        self.n = ncols
        self.off = 0
        self.bufs = []

    def reset(self):
        self.P.barrier()
        for b in self.bufs:
            if b.ds is not None:
                self.P.free_ds.append(b.ds)
                b.ds = None
        self.bufs = []
        self.off = 0

    def f32(self, ncols, name=""):
        ncols = (ncols + 7) // 8 * 8
        assert self.off + ncols <= self.n, f"arena overflow {name} {self.off}+{ncols}>{self.n}"
        ap = self.t[:, self.off:self.off + ncols]
        self.off += ncols
        b = Buf(ap, name)
        self.bufs.append(b)
        return b

    def bf16(self, ncols, name=""):
        b = self.f32((ncols + 1) // 2, name)
        b.ap = b.ap.bitcast(BF16)
        return b


def v3(ap, inner):
    return ap.rearrange("p (a b) -> p a b", b=inner)


def tiles_of(TT, with_ctx=True):
    tl = [(0, NCTX, 1)] if with_ctx else []
    for i in range(SEQ_L // TT):
        tl.append((NCTX + i * TT, TT, 0))
    return tl


def build(stop_after=99, debug=False):
    P = Prog()
    nc = P.nc
    xin = P.din("xin", [D, NT])
    cvec = P.din("cvec", [128, KC, 2])
    ada_w = P.din("ada_w", [2, D, 9 * D])
    ada_b = P.din("ada_b", [2, 9 * D])
    norm_g = P.din("norm_g", [2, 3, D])
    ffn_w1 = P.din("ffn_w1", [2, 2, D, DFF])
    ffn_w3 = P.din("ffn_w3", [2, 2, D, DFF])
    ffn_w2 = P.din("ffn_w2", [2, 2, DFF, D])
    e_win = P.din("e_win", [D, 1920])
    e_wkr = P.din("e_wkr", [D, 96])
    e_wkrp = P.din("e_wkrp", [D, 96])
    e_conv = P.din("e_conv", [3, 512])
    e_qg = P.din("e_qg", [256])
    e_wqa = P.din("e_wqa", [256, 8 * 96])
    e_wqb = P.din("e_wqb", [256, 8 * 96])
    e_kvg = P.din("e_kvg", [128])
    e_wkn = P.din("e_wkn", [128, 8 * 96])
    e_wv = P.din("e_wv", [128, 512])
    e_wout = P.din("e_wout", [D, D])
    ropeC = P.din("ropeC", [96, NT])
    ropeS = P.din("ropeS", [96, NT])
    o_win = P.din("o_win", [D, 5 * D])
    o_lb = P.din("o_lb", [2, 2, D])
    o_gn = P.din("o_gn", [128])
    o_wout = P.din("o_wout", [D, D])
    fin_g = P.din("fin_g", [D])
    cmask = P.din("cmask", [64, 128])
    ident = P.din("ident", [128, 128])
    yout = P.dout("yout", [D, SEQ_L])
    selv = P.din("selv", [128, 2])
    mk = P.dout if debug else P.dint
    XT = mk("XT", [D, NT])
    GB = P.dint("GB", [512, NT], BF16)
    CV = P.dint("CV", [512, NT + 3])
    KCX = P.dint("KCX", [768, NCTX], BF16)
    KL = [P.dint(f"KL{i}", [192, SEQ_L], BF16) for i in range(4)]
    KG = [P.dint(f"KG{i}", [384, SEQ_L], BF16) for i in range(4)]
    VCX = P.dint("VCX", [NCTX, 512], BF16)
    VL = [P.dint(f"VL{i}", [SEQ_L // 2, 512], BF16) for i in range(2)]
    VG = [P.dint(f"VG{i}", [SEQ_L, 512], BF16) for i in range(2)]
    CVH = P.dint("CVH", [512, 8])
    CVG = P.dint("CVG", [1024, 8])
    SX = P.dint("SX", [128, 1024])
    SGA = P.dint("SGA", [256, 1024])
    QT = P.dint("QT", [8, 96, NT], BF16)
    BT = mk("BT", [512, NT], BF16) if not debug else P.dout("BT", [512, NT], BF16)
    OF = P.dint("OF", [D, SEQ_L])

    def cv_col(t):
        return 1 + t if t < NCTX else 2 + t

    cst = P.sb("cst", [128, 1024], F32)
    c_off = [0]

    def cbuf(n, name=""):
        ap = cst[:, c_off[0]:c_off[0] + n]
        c_off[0] += n
        assert c_off[0] <= 1024
        return Buf(ap, name)

    CVEC = cbuf(16)
    SC = cbuf(16)
    MOD = cbuf(144)
    ADAB = cbuf(72)
    NG = cbuf(48)
    G = cbuf(48)
    HG = cbuf(48)
    FING = cbuf(8)
    EPSB = cbuf(1)
    CONVW = cbuf(12)
    QG = cbuf(2)
    KVG = cbuf(1)
    LBL = cbuf(32)
    LB = cbuf(16)
    OML = cbuf(16)
    GN = cbuf(1)
    ZERO = cbuf(4)
    SEL = cbuf(2)
    ONESB = Buf(P.sb("onesb", [128, 128], BF16)[:], "ones")
    IDB = Buf(P.sb("idb", [128, 128], BF16)[:], "idb")
    MASK = Buf(P.sb("maskt", [64, 128], F32)[:], "mask")
    psall = P.es.enter_context(nc.psum_tensor("psall", [128, 4096], F32))
    banks = [Buf(psall[:, i * 512:(i + 1) * 512], f"bank{i}") for i in range(8)]
    bank_rr = [0]

    def nb():
        b = banks[bank_rr[0] % 8]
        bank_rr[0] += 1
        return b

    A = Arena(P, 50 * 1024 + 512)
    sp, pe, act, dve, pool = P.sp, P.pe, P.act, P.dve, P.pool

    P.load(sp, CVEC, v3(CVEC.ap, 2), cvec[:, :, :])
    for l_ in range(2):
        for j_ in range(3):
            P.load(sp, NG, NG.ap[:, (l_ * 3 + j_) * 8:(l_ * 3 + j_ + 1) * 8], norm_g[l_, j_].rearrange("(k p) -> p k", p=128), slow=True)
    P.load(sp, FING, FING.ap, fin_g.rearrange("(k p) -> p k", p=128), slow=True)
    for c_ in range(4):
        P.load(sp, CONVW, CONVW.ap[:, c_ * 3:(c_ + 1) * 3], e_conv[:, c_ * 128:(c_ + 1) * 128].rearrange("w p -> p w"), slow=True)
    P.load(sp, QG, QG.ap, e_qg.rearrange("(k p) -> p k", p=128), slow=True)
    P.load(sp, KVG, KVG.ap, e_kvg.rearrange("(k p) -> p k", p=128), slow=True)
    for l_ in range(2):
        for d_ in range(2):
            P.load(sp, LBL, LBL.ap[:, (l_ * 2 + d_) * 8:(l_ * 2 + d_ + 1) * 8], o_lb[l_, d_].rearrange("(h p) -> p h", p=128), slow=True)
    P.load(sp, GN, GN.ap, o_gn.rearrange("(k p) -> p k", p=128), slow=True)
    P.load(sp, MASK, MASK.ap, cmask[:, :])
    P.load(sp, SEL, SEL.ap, selv[:, :])
    P.load(pool, IDB, IDB.ap, ident[:, :])
    P.do(dve, lambda e: e.memset(EPSB.ap, EPS), w=[EPSB])
    P.do(dve, lambda e: e.memset(ZERO.ap, 0.0), w=[ZERO])
    P.do(dve, lambda e: e.memset(ONESB.ap, 1.0), w=[ONESB])
    P.do(dve, lambda e: e.tensor_tensor(out=LB.ap, in0=LBL.ap[:, 16:32], in1=LBL.ap[:, 0:16], op=ALU.subtract), r=[LBL], w=[LB])
    P.do(act, lambda e: e.activation(out=LB.ap, in_=LB.ap, func=AF.Sigmoid), r=[LB], w=[LB])
    P.do(dve, lambda e: e.tensor_scalar(out=OML.ap, in0=LB.ap, scalar1=-1.0, scalar2=1.0, op0=ALU.mult, op1=ALU.add), r=[LB], w=[OML])
    P.do(act, lambda e: e.activation(out=SC.ap, in_=CVEC.ap, func=AF.Silu), r=[CVEC], w=[SC])

    def allgather(src, dst):
        ds = P.get_ds()
        ds.count += 1
        pool.ops.append(("op", lambda e: e.collective_compute("AllGather", ALU.bypass, replica_groups=PAIRS,
                                                              ins=[src.opt()], outs=[dst.opt()]), ds.sem, 1))

    def ada_phase(l):
        A.reset()
        P.load(sp, ADAB, ADAB.ap, ada_b[l].rearrange("(j p) -> p j", p=128), slow=True)
        wb = [A.bf16(KC * 1024, f"adaw{i}") for i in range(3)]
        pb = nb()
        SCB = A.bf16(16, "scb")
        P.do(dve, lambda e: e.tensor_copy(out=SCB.ap, in_=SC.ap), r=[SC], w=[SCB])
        sc3 = v3(SCB.ap, 2)
        for j in range(9):
            W = wb[j % 3]
            W3 = v3(W.ap, 1024)
            for kc in range(KC):
                P.load(pool, W, W3[:, kc, :], ada_w[l, kc * 128:(kc + 1) * 128, j * 1024:(j + 1) * 1024])
            for oc in range(8):
                col = (j * 8 + oc) * 2
                for kc in range(KC):
                    P.do(pe, lambda e, W3=W3, kc=kc, oc=oc, col=col: e.matmul(
                        pb.ap[:, col:col + 2], W3[:, kc, oc * 128:(oc + 1) * 128], sc3[:, kc, :],
                        start=(kc == 0), stop=(kc == KC - 1)), r=[W, SCB], w=[pb], inc=(kc == KC - 1))
        ps3 = v3(pb.ap[:, 0:144], 2)
        mod3 = v3(MOD.ap, 72)
        for s in range(2):
            P.do(dve, lambda e, s=s: e.tensor_tensor(out=mod3[:, s, :], in0=ps3[:, :, s], in1=ADAB.ap, op=ALU.add),
                 r=[pb, ADAB], w=[MOD])
        for s in range(2):
            m4 = mod3[:, s, :].rearrange("p (jj t k) -> p jj t k", t=3, k=8)
            g3 = v3(G.ap, 24)[:, s, :].rearrange("p (jj k) -> p jj k", k=8)
            h3 = v3(HG.ap, 24)[:, s, :].rearrange("p (jj k) -> p jj k", k=8)
            ng3 = v3(NG.ap, 24)[:, l, :].rearrange("p (jj k) -> p jj k", k=8)
            P.do(dve, lambda e, m4=m4, g3=g3, ng3=ng3: e.scalar_tensor_tensor(
                out=g3, in0=m4[:, :, 1, :], scalar=1.0, in1=ng3, op0=ALU.add, op1=ALU.mult), r=[MOD, NG], w=[G])
            P.do(dve, lambda e, m4=m4, h3=h3: e.tensor_scalar(
                out=h3, in0=m4[:, :, 2, :], scalar1=0.5, scalar2=None, op0=ALU.mult), r=[MOD], w=[HG])
            P.do(dve, lambda e, m4=m4, h3=h3: e.tensor_copy(out=h3[:, 1, :], in_=m4[:, 1, 2, :]), r=[MOD], w=[HG])

    def modv(s, j):
        return v3(MOD.ap, 72)[:, s, j * 8:(j + 1) * 8]

    def gvec(s, jj):
        return v3(G.ap, 24)[:, s, jj * 8:(jj + 1) * 8]

    def hgvec(s, jj):
        return v3(HG.ap, 24)[:, s, jj * 8:(jj + 1) * 8]

    def rstd_from_ps(pb, n, RS, dim, lnexp=False):
        if lnexp:
            P.do(act, lambda e: e.activation(out=RS.ap[:, 0:n], in_=pb.ap[:, 0:n], func=AF.Ln, bias=EPSB.ap[:, 0:1],
                                             scale=1.0 / dim), r=[pb, EPSB], w=[RS])
            P.do(act, lambda e: e.activation(out=RS.ap[:, 0:n], in_=RS.ap[:, 0:n], func=AF.Exp, scale=-0.5), r=[RS], w=[RS])
            return
        P.do(act, lambda e: e.activation(out=RS.ap[:, 0:n], in_=pb.ap[:, 0:n], func=AF.Sqrt, bias=EPSB.ap[:, 0:1],
                                         scale=1.0 / dim), r=[pb, EPSB], w=[RS])
        P.do(dve, lambda e: e.reciprocal(out=RS.ap[:, 0:n], in_=RS.ap[:, 0:n]), r=[RS], w=[RS])

    def norm_mod(X, U, SQ, RS, TMP, n, s, jj, src, t0, lnexp=False):
        x3 = v3(X.ap[:, 0:KC * n], n)
        u3 = v3(U.ap[:, 0:KC * n], n)
        sq3 = v3(SQ.ap[:, 0:KC * n], n)
        P.load(sp, X, x3, src.rearrange("(k p) t -> p k t", p=128)[:, :, t0:t0 + n])
        P.do(act, lambda e: e.activation(out=SQ.ap[:, 0:KC * n], in_=X.ap[:, 0:KC * n], func=AF.Square), r=[X], w=[SQ])
        pb = nb()
        for kc in range(KC):
            P.do(pe, lambda e, kc=kc: e.matmul(pb.ap[:, 0:n], ONESB.ap, sq3[:, kc, :], start=(kc == 0), stop=(kc == KC - 1)),
                 r=[SQ, ONESB], w=[pb], inc=(kc == KC - 1))
        rstd_from_ps(pb, n, RS, D, lnexp)
        gv = gvec(s, jj)
        sh = modv(s, 3 * jj)
        for kc in range(KC):
            T = TMP[kc % 2]
            P.do(dve, lambda e, kc=kc, T=T: e.tensor_tensor(out=T.ap[:, 0:n], in0=x3[:, kc, :], in1=RS.ap[:, 0:n], op=ALU.mult),
                 r=[X, RS], w=[T])
            P.do(act, lambda e, kc=kc, T=T: e.activation(out=u3[:, kc, :], in_=T.ap[:, 0:n], func=AF.Identity,
                                                         bias=sh[:, kc:kc + 1], scale=gv[:, kc:kc + 1]),
                 r=[T, G, MOD], w=[U])
        return x3, u3

    def load_w_bf16(buf, view3, dram2d, nk, rows_per=128):
        for k in range(nk):
            P.load(pool, buf, view3[:, k, :], dram2d[k * rows_per:(k + 1) * rows_per, :])

    def ffn_phase(l, jj, src, dst, with_ctx=True, TT=512):
        wi = 0 if jj == 0 else 1
        A.reset()
        HJ = NJ // 2 * 128
        W1 = [A.bf16(KC * HJ, "w1a"), A.bf16(KC * HJ, "w1b")]
        W3 = [A.bf16(KC * HJ, "w3a"), A.bf16(KC * HJ, "w3b")]
        W2 = A.bf16(NJ * D, "w2")
        w13 = [v3(W1[i].ap, HJ) for i in range(2)]; w33 = [v3(W3[i].ap, HJ) for i in range(2)]; w23 = v3(W2.ap, D)
        for i in range(2):
            for kc in range(KC):
                P.load(pool, W1[i], w13[i][:, kc, :], ffn_w1[l, wi, kc * 128:(kc + 1) * 128, i * HJ:(i + 1) * HJ])
            for kc in range(KC):
                P.load(pool, W3[i], w33[i][:, kc, :], ffn_w3[l, wi, kc * 128:(kc + 1) * 128, i * HJ:(i + 1) * HJ])
        load_w_bf16(W2, w23, ffn_w2[l, wi], NJ)
        XB = [A.f32(KC * TT, "xa"), A.f32(KC * TT, "xb")]
        U = A.bf16(KC * TT, "u")
        H = A.bf16(NJ * TT, "h")
        RS = A.f32(TT, "rs")
        TMP = [A.f32(TT, "t0"), A.f32(TT, "t1")]
        SQS = [A.bf16(TT, "sq0"), A.bf16(TT, "sq1")]
        SA = TMP
        tl = tiles_of(TT, with_ctx)

        def norm_parts(it):
            (t0, n, s) = tl[it]
            X = XB[it % 2]
            x3 = v3(X.ap[:, 0:KC * n], n)
            u3 = v3(U.ap[:, 0:KC * n], n)

            def part1():
                P.load(sp, X, x3, src.rearrange("(k p) t -> p k t", p=128)[:, :, t0:t0 + n])

            def part2():
                pb = nb()
                for kc in range(KC):
                    sq = SQS[kc % 2]
                    P.do(act, lambda e, kc=kc, sq=sq: e.activation(out=sq.ap[:, 0:n], in_=x3[:, kc, :], func=AF.Square), r=[X], w=[sq])
                    P.do(pe, lambda e, kc=kc, sq=sq: e.matmul(pb.ap[:, 0:n], ONESB.ap, sq.ap[:, 0:n], start=(kc == 0), stop=(kc == KC - 1)),
                         r=[sq, ONESB], w=[pb])
                rstd_from_ps(pb, n, RS, D)
                gv = gvec(s, jj)
                sh = modv(s, 3 * jj)
                for kc in range(KC):
                    T = TMP[kc % 2]
                    P.do(dve, lambda e, kc=kc, T=T: e.tensor_tensor(out=T.ap[:, 0:n], in0=x3[:, kc, :], in1=RS.ap[:, 0:n], op=ALU.mult),
                         r=[X, RS], w=[T])
                    P.do(act, lambda e, kc=kc, T=T: e.activation(out=u3[:, kc, :], in_=T.ap[:, 0:n], func=AF.Identity,
                                                                 bias=sh[:, kc:kc + 1], scale=gv[:, kc:kc + 1]),
                         r=[T, G, MOD], w=[U])
            return x3, u3, part1, part2

        parts = {0: norm_parts(0)}
        parts[0][2]()
        parts[0][3]()
        for it, (t0, n, s) in enumerate(tl):
            X = XB[it % 2]
            h3 = v3(H.ap[:, 0:NJ * n], n)
            x3, u3, _, _ = parts.pop(it)
            if it + 1 < len(tl):
                parts[it + 1] = norm_parts(it + 1)
                parts[it + 1][2]()
            for j in range(NJ):
                pa = nb(); pbk = nb()
                for kc in range(KC):
                    P.do(pe, lambda e, pa=pa, j=j, kc=kc: e.matmul(pa.ap[:, 0:n], w13[j // 11][:, kc, (j % 11) * 128:(j % 11 + 1) * 128], u3[:, kc, :],
                                                                   start=(kc == 0), stop=(kc == KC - 1)),
                         r=[W1[j // 11], U], w=[pa], inc=(kc == KC - 1))
                for kc in range(KC):
                    P.do(pe, lambda e, pbk=pbk, j=j, kc=kc: e.matmul(pbk.ap[:, 0:n], w33[j // 11][:, kc, (j % 11) * 128:(j % 11 + 1) * 128], u3[:, kc, :],
                                                                     start=(kc == 0), stop=(kc == KC - 1)),
                         r=[W3[j // 11], U], w=[pbk], inc=(kc == KC - 1))
                S_ = SA[j % 2]
                P.do(act, lambda e, pa=pa, S_=S_: e.activation(out=S_.ap[:, 0:n], in_=pa.ap[:, 0:n], func=AF.Silu), r=[pa], w=[S_])
                P.do(dve, lambda e, pbk=pbk, S_=S_, j=j: e.tensor_tensor(out=h3[:, j, :], in0=pbk.ap[:, 0:n], in1=S_.ap[:, 0:n], op=ALU.mult),
                     r=[pbk, S_], w=[H])
            hg = hgvec(s, jj)
            for m in range(KC):
                if m == 4 and it + 1 < len(tl):
                    parts[it + 1][3]()
                po = nb()
                for j in range(NJ):
                    P.do(pe, lambda e, po=po, j=j, m=m: e.matmul(po.ap[:, 0:n], w23[:, j, m * 128:(m + 1) * 128], h3[:, j, :],
                                                                 start=(j == 0), stop=(j == NJ - 1)),
                         r=[W2, H], w=[po], inc=(j == NJ - 1))
                P.do(dve, lambda e, po=po, m=m: e.scalar_tensor_tensor(out=x3[:, m, :], in0=po.ap[:, 0:n], scalar=hg[:, m:m + 1],
                                                                       in1=x3[:, m, :], op0=ALU.mult, op1=ALU.add),
                     r=[po, HG, X], w=[X])
            P.store(sp, X, x3, dst.rearrange("(k p) t -> p k t", p=128)[:, :, t0:t0 + n])

    def final_phase(src, TT=512):
        A.reset()
        XB = [A.f32(KC * TT, "xa"), A.f32(KC * TT, "xb")]
        SQ = A.bf16(KC * TT, "sq")
        RS = A.f32(TT, "rs")
        for it, (t0, n, s) in enumerate(tiles_of(TT, False)):
            X = XB[it % 2]
            x3 = v3(X.ap[:, 0:KC * n], n)
            sq3 = v3(SQ.ap[:, 0:KC * n], n)
            P.load(sp, X, x3, src.rearrange("(k p) t -> p k t", p=128)[:, :, t0:t0 + n])
            P.do(act, lambda e, X=X: e.activation(out=SQ.ap[:, 0:KC * n], in_=X.ap[:, 0:KC * n], func=AF.Square), r=[X], w=[SQ])
            pb = nb()
            for kc in range(KC):
                P.do(pe, lambda e, pb=pb, kc=kc, sq3=sq3: e.matmul(pb.ap[:, 0:n], ONESB.ap, sq3[:, kc, :], start=(kc == 0), stop=(kc == KC - 1)),
                     r=[SQ, ONESB], w=[pb], inc=(kc == KC - 1))
            rstd_from_ps(pb, n, RS, D)
            for kc in range(KC):
                P.do(dve, lambda e, kc=kc, x3=x3: e.scalar_tensor_tensor(out=x3[:, kc, :], in0=x3[:, kc, :], scalar=FING.ap[:, kc:kc + 1],
                                                                        in1=RS.ap[:, 0:n], op0=ALU.mult, op1=ALU.mult),
                     r=[X, RS, FING], w=[X])
            P.store(sp, X, x3, yout.rearrange("(k p) t -> p k t", p=128)[:, :, t0 - NCTX:t0 - NCTX + n])

    def mix0_proj(src, TT=512):
        A.reset()
        WI = A.bf16(KC * 1920, "win"); wi3 = v3(WI.ap, 1920)
        load_w_bf16(WI, wi3, e_win, KC)
        WKR = A.bf16(KC * 96, "wkr"); wkr3 = v3(WKR.ap, 96)
        load_w_bf16(WKR, wkr3, e_wkr, KC)
        WKRP = A.bf16(KC * 96, "wkrp"); wkrp3 = v3(WKRP.ap, 96)
        load_w_bf16(WKRP, wkrp3, e_wkrp, KC)
        WQA = A.bf16(2 * 768, "wqa"); wqa3 = v3(WQA.ap, 768)
        load_w_bf16(WQA, wqa3, e_wqa, 2)
        WQB = A.bf16(2 * 768, "wqb"); wqb3 = v3(WQB.ap, 768)
        load_w_bf16(WQB, wqb3, e_wqb, 2)
        WKN = A.bf16(768, "wkn")
        P.load(pool, WKN, WKN.ap, e_wkn[:, :])
        WV = A.bf16(512, "wv")
        P.load(pool, WV, WV.ap, e_wv[:, :])
        XB = [A.f32(KC * TT, "xa"), A.f32(KC * TT, "xb")]
        U = A.bf16(KC * TT, "u"); SQ = A.bf16(KC * TT, "sq")
        RS = A.f32(TT, "rs")
        TMP = [A.f32(TT, "t0"), A.f32(TT, "t1")]
        RC = A.f32(TT, "ropec"); RSN = A.f32(TT, "ropes")
        GBt = [A.bf16(TT, "gb0"), A.bf16(TT, "gb1")]
        CVt = [A.f32(TT, "cv0"), A.f32(TT, "cv1")]
        CQ = A.f32(2 * TT, "cq"); NQ = A.bf16(2 * TT, "nq"); SQQ = A.bf16(2 * TT, "sqq")
        CKV = A.f32(TT, "ckv"); NKV = A.bf16(TT, "nkv"); SQK = A.bf16(TT, "sqk")
        RSQ = A.f32(TT, "rsq")
        ROT = A.f32(TT, "rot")
        T1 = [A.f32(TT, "q1a"), A.f32(TT, "q1b")]
        T2 = [A.f32(TT, "q2a"), A.f32(TT, "q2b")]
        QO = [A.bf16(TT, "qo0"), A.bf16(TT, "qo1")]
        KO = [A.bf16(TT, "ko0"), A.bf16(TT, "ko1")]
        VO = [A.bf16(512, "vo0"), A.bf16(512, "vo1")]
        ZT = A.f32(512, "zt")
        P.do(dve, lambda e: e.memset(ZT.ap[:, 0:4], 0.0), w=[ZT])
        for c in range(4):
            for col in (0, NCTX + 1):
                P.store(sp, ZT, ZT.ap[:, 0:1], CV[c * 128:(c + 1) * 128, col:col + 1], slow=True)
        for it, (t0, n, s) in enumerate(tiles_of(TT, True)):
            X = XB[it % 2]
            x3, u3 = norm_mod(X, U, SQ, RS, TMP, n, s, 1, src, t0)
            P.load(sp, RC, RC.ap[0:96, 0:n], ropeC[:, t0:t0 + n])
            P.load(sp, RSN, RSN.ap[0:96, 0:n], ropeS[:, t0:t0 + n])

            def proj(col0, ncols, pb, W3=wi3, Wb=WI):
                for kc in range(KC):
                    P.do(pe, lambda e, kc=kc: e.matmul(pb.ap[0:ncols, 0:n], W3[:, kc, col0:col0 + ncols], u3[:, kc, :],
                                                       start=(kc == 0), stop=(kc == KC - 1)),
                         r=[Wb, U], w=[pb], inc=(kc == KC - 1))

            cq3 = v3(CQ.ap[:, 0:2 * n], n); nq3 = v3(NQ.ap[:, 0:2 * n], n); sqq3 = v3(SQQ.ap[:, 0:2 * n], n)
            for i in range(2):
                pq = nb(); proj(1536 + i * 128, 128, pq)
                P.do(act, lambda e, pq=pq, i=i: e.activation(out=cq3[:, i, :], in_=pq.ap[:, 0:n], func=AF.Copy), r=[pq], w=[CQ])
            P.do(act, lambda e: e.activation(out=SQQ.ap[:, 0:2 * n], in_=CQ.ap[:, 0:2 * n], func=AF.Square), r=[CQ], w=[SQQ])
            pss = nb()
            for i in range(2):
                P.do(pe, lambda e, i=i: e.matmul(pss.ap[:, 0:n], ONESB.ap, sqq3[:, i, :], start=(i == 0), stop=(i == 1)),
                     r=[SQQ, ONESB], w=[pss], inc=(i == 1))
            rstd_from_ps(pss, n, RSQ, 256)
            for i in range(2):
                T = TMP[i % 2]
                P.do(dve, lambda e, i=i, T=T: e.tensor_tensor(out=T.ap[:, 0:n], in0=cq3[:, i, :], in1=RSQ.ap[:, 0:n], op=ALU.mult),
                     r=[CQ, RSQ], w=[T])
                P.do(act, lambda e, i=i, T=T: e.activation(out=nq3[:, i, :], in_=T.ap[:, 0:n], func=AF.Identity, scale=QG.ap[:, i:i + 1]),
                     r=[T, QG], w=[NQ])
            pk = nb(); proj(1792, 128, pk)
            P.do(act, lambda e, pk=pk: e.activation(out=CKV.ap[:, 0:n], in_=pk.ap[:, 0:n], func=AF.Copy), r=[pk], w=[CKV])
            P.do(act, lambda e: e.activation(out=SQK.ap[:, 0:n], in_=CKV.ap[:, 0:n], func=AF.Square), r=[CKV], w=[SQK])
            pss = nb()
            P.do(pe, lambda e, pss=pss: e.matmul(pss.ap[:, 0:n], ONESB.ap, SQK.ap[:, 0:n], start=True, stop=True), r=[SQK, ONESB], w=[pss])
            rstd_from_ps(pss, n, RSQ, 128)
            T = TMP[0]
            P.do(dve, lambda e, T=T: e.tensor_tensor(out=T.ap[:, 0:n], in0=CKV.ap[:, 0:n], in1=RSQ.ap[:, 0:n], op=ALU.mult), r=[CKV, RSQ], w=[T])
            P.do(act, lambda e, T=T: e.activation(out=NKV.ap[:, 0:n], in_=T.ap[:, 0:n], func=AF.Identity, scale=KVG.ap[:, 0:1]), r=[T, KVG], w=[NKV])
            pr = nb(); proj(0, 96, pr, wkr3, WKR)
            prp = nb(); proj(0, 96, prp, wkrp3, WKRP)
            t1 = T1[0]; t2 = T2[0]
            P.do(dve, lambda e, pr=pr, t1=t1: e.tensor_tensor(out=t1.ap[0:96, 0:n], in0=pr.ap[0:96, 0:n], in1=RC.ap[0:96, 0:n], op=ALU.mult),
                 r=[pr, RC], w=[t1])
            P.do(dve, lambda e, prp=prp, t2=t2: e.tensor_tensor(out=t2.ap[0:96, 0:n], in0=prp.ap[0:96, 0:n], in1=RSN.ap[0:96, 0:n], op=ALU.mult),
                 r=[prp, RSN], w=[t2])
            P.do(pool, lambda e, t1=t1, t2=t2: e.tensor_tensor(out=ROT.ap[0:96, 0:n], in0=t1.ap[0:96, 0:n], in1=t2.ap[0:96, 0:n], op=ALU.add),
                 r=[t1, t2], w=[ROT])
            for c in range(4):
                pg = nb(); proj(c * 128, 128, pg)
                gbt = GBt[c % 2]
                P.do(act, lambda e, pg=pg, gbt=gbt: e.activation(out=gbt.ap[:, 0:n], in_=pg.ap[:, 0:n], func=AF.Copy), r=[pg], w=[gbt])
                P.store(sp, gbt, gbt.ap[:, 0:n], GB[c * 128:(c + 1) * 128, t0:t0 + n])
                pc = nb(); proj(512 + c * 128, 128, pc)
                pv = nb(); proj(1024 + c * 128, 128, pv)
                T = TMP[c % 2]
                cvt = CVt[c % 2]
                P.do(act, lambda e, pc=pc, T=T: e.activation(out=T.ap[:, 0:n], in_=pc.ap[:, 0:n], func=AF.Copy), r=[pc], w=[T])
                P.do(dve, lambda e, pv=pv, T=T, cvt=cvt: e.tensor_tensor(out=cvt.ap[:, 0:n], in0=pv.ap[:, 0:n], in1=T.ap[:, 0:n], op=ALU.mult),
                     r=[pv, T], w=[cvt])
                cc = cv_col(t0)
                P.store(sp, cvt, cvt.ap[:, 0:n], CV[c * 128:(c + 1) * 128, cc:cc + n])
                if t0 + n == NT:
                    P.store(sp, cvt, cvt.ap[:, n - 1:n], CVH[c * 128:(c + 1) * 128, 0:1], slow=True)
            for h in range(8):
                pa = nb(); pbk = nb()
                for i in range(2):
                    P.do(pe, lambda e, i=i, h=h, pa=pa: e.matmul(pa.ap[0:96, 0:n], wqa3[:, i, h * 96:(h + 1) * 96], nq3[:, i, :],
                                                                 start=(i == 0), stop=(i == 1)), r=[WQA, NQ], w=[pa], inc=(i == 1))
                for i in range(2):
                    P.do(pe, lambda e, i=i, h=h, pbk=pbk: e.matmul(pbk.ap[0:96, 0:n], wqb3[:, i, h * 96:(h + 1) * 96], nq3[:, i, :],
                                                                   start=(i == 0), stop=(i == 1)), r=[WQB, NQ], w=[pbk], inc=(i == 1))
                t1 = T1[h % 2]; t2 = T2[h % 2]; qo = QO[h % 2]
                P.do(dve, lambda e, pa=pa, t1=t1: e.tensor_tensor(out=t1.ap[0:96, 0:n], in0=pa.ap[0:96, 0:n], in1=RC.ap[0:96, 0:n], op=ALU.mult),
                     r=[pa, RC], w=[t1])
                P.do(dve, lambda e, pbk=pbk, t2=t2: e.tensor_tensor(out=t2.ap[0:96, 0:n], in0=pbk.ap[0:96, 0:n], in1=RSN.ap[0:96, 0:n], op=ALU.mult),
                     r=[pbk, RSN], w=[t2])
                P.do(pool, lambda e, t1=t1, t2=t2, qo=qo: e.tensor_tensor(out=qo.ap[0:96, 0:n], in0=t1.ap[0:96, 0:n], in1=t2.ap[0:96, 0:n], op=ALU.add),
                     r=[t1, t2], w=[qo])
                P.store(sp, qo, qo.ap[0:96, 0:n], QT[h, :, t0:t0 + n])
            for h in range(8):
                pkh = nb()
                P.do(pe, lambda e, pkh=pkh, h=h: e.matmul(pkh.ap[0:96, 0:n], WKN.ap[:, h * 96:(h + 1) * 96], NKV.ap[:, 0:n], start=True, stop=True),
                     r=[WKN, NKV], w=[pkh])
                ko = KO[h % 2]
                P.do(dve, lambda e, pkh=pkh, ko=ko: e.tensor_tensor(out=ko.ap[0:96, 0:n], in0=pkh.ap[0:96, 0:n], in1=ROT.ap[0:96, 0:n], op=ALU.add),
                     r=[pkh, ROT], w=[ko])
                if s == 1:
                    P.store(sp, ko, ko.ap[0:96, 0:n], KCX[h * 96:(h + 1) * 96, t0:t0 + n])
                else:
                    P.store(sp, ko, ko.ap[0:96, 0:n], KL[h // 2][(h % 2) * 96:(h % 2 + 1) * 96, t0 - NCTX:t0 - NCTX + n])
            for tb in range(n // 128):
                pvv = nb()
                P.do(pe, lambda e, pvv=pvv, tb=tb: e.matmul(pvv.ap[:, 0:512], NKV.ap[:, tb * 128:(tb + 1) * 128], WV.ap, start=True, stop=True),
                     r=[NKV, WV], w=[pvv])
                vo = VO[tb % 2]
                P.do(act, lambda e, pvv=pvv, vo=vo: e.activation(out=vo.ap, in_=pvv.ap, func=AF.Copy), r=[pvv], w=[vo])
                if s == 1:
                    P.store(sp, vo, vo.ap, VCX[t0 + tb * 128:t0 + (tb + 1) * 128, :])
                else:
                    tl_ = t0 - NCTX + tb * 128
                    P.store(sp, vo, vo.ap, VL[tl_ // 2048][tl_ % 2048:tl_ % 2048 + 128, :])

    def mix0_exchange():
        A.reset()
        for i in range(4):
            allgather(KL[i], KG[i])
        for i in range(2):
            allgather(VL[i], VG[i])
        allgather(CVH, CVG)
        A.reset()
        HB = A.f32(64, "hb"); HO = A.f32(8, "ho")
        hb3 = v3(HB.ap, 8)
        P.load(sp, HB, hb3, CVG.rearrange("(r p) k -> p r k", p=128))
        P.do(dve, lambda e: e.tensor_scalar(out=HO.ap[:, 0:4], in0=hb3[:, 0:4, 0], scalar1=SEL.ap[:, 0:1], scalar2=None, op0=ALU.mult), r=[HB, SEL], w=[HO])
        P.do(dve, lambda e: e.scalar_tensor_tensor(out=HO.ap[:, 0:4], in0=hb3[:, 4:8, 0], scalar=SEL.ap[:, 1:2], in1=HO.ap[:, 0:4],
                                                   op0=ALU.mult, op1=ALU.add), r=[HB, SEL, HO], w=[HO])
        for c in range(4):
            P.store(sp, HO, HO.ap[:, c:c + 1], CV[c * 128:(c + 1) * 128, NT + 2:NT + 3], slow=True)

    def mix0_attn():
        A.reset()
        NKT = NKEY // 128
        KH = [A.bf16(NKEY, "kh0"), A.bf16(NKEY, "kh1")]
        QH = [A.bf16(NT, "qh0"), A.bf16(NT, "qh1")]
        VA = [A.bf16(NKT * 128, "va0"), A.bf16(NKT * 128, "va1")]
        PT = [[A.bf16(512, f"pt{i}a"), A.bf16(512, f"pt{i}b")] for i in range(3)]
        RCP = A.f32(1024, "rcp")
        BO = [A.bf16(1024, "bo0"), A.bf16(1024, "bo1")]
        SP_ = [[Buf(psall[:, (2 * i + j) * 512:(2 * i + j + 1) * 512], f"s{i}{j}") for j in range(2)] for i in range(2)]
        OP_ = [[Buf(psall[:, (4 + 2 * i + j) * 512:(4 + 2 * i + j + 1) * 512], f"o{i}{j}") for j in range(2)] for i in range(2)]
        for i in range(2):
            va3 = v3(VA[i].ap, 128)
            P.do(dve, lambda e, va3=va3: e.memset(va3[:, :, 64:128], 1.0), w=[VA[i]])
        it = 0
        qtiles = [(0, NCTX, 2)] + [(NCTX + i * 1024, 1024, NKT) for i in range(SEQ_L // 1024)]
        for h in range(8):
            K_ = KH[h % 2]; Q_ = QH[h % 2]; V_ = VA[h % 2]
            va3 = v3(V_.ap, 128)
            P.load(sp, K_, K_.ap[0:96, 0:NCTX], KCX[h * 96:(h + 1) * 96, :])
            for r_ in range(2):
                P.load(sp, K_, K_.ap[0:96, NCTX + r_ * SEQ_L:NCTX + (r_ + 1) * SEQ_L],
                       KG[h // 2][r_ * 192 + (h % 2) * 96:r_ * 192 + (h % 2 + 1) * 96, :])
            P.load(sp, Q_, Q_.ap[0:96, :], QT[h, :, :])
            P.load(sp, V_, va3[:, 0:2, 0:64], VCX[:, h * 64:(h + 1) * 64].rearrange("(kt p) d -> p kt d", p=128))
            for r_ in range(2):
                for j_ in range(2):
                    k0 = 2 + r_ * 32 + j_ * 16
                    P.load(sp, V_, va3[:, k0:k0 + 16, 0:64],
                           VG[j_][r_ * 2048:(r_ + 1) * 2048, h * 64:(h + 1) * 64].rearrange("(kt p) d -> p kt d", p=128))
            for (q0, nq, nkt) in qtiles:
                O_ = OP_[it % 2]
                halves = [(0, min(512, nq))] + ([(512, 512)] if nq > 512 else [])

                def emit_qk(kt):
                    for hi, (c0, cn) in enumerate(halves):
                        S_ = SP_[kt % 2][hi]
                        P.do(pe, lambda e, S_=S_, kt=kt, c0=c0, cn=cn: e.matmul(
                            S_.ap[:, 0:cn], K_.ap[0:96, kt * 128:(kt + 1) * 128], Q_.ap[0:96, q0 + c0:q0 + c0 + cn],
                            start=True, stop=True), r=[K_, Q_], w=[S_])

                emit_qk(0)
                for kt in range(nkt):
                    if kt + 1 < nkt:
                        emit_qk(kt + 1)
                    for hi, (c0, cn) in enumerate(halves):
                        S_ = SP_[kt % 2][hi]; pt = PT[kt % 3][hi]
                        P.do(act, lambda e, S_=S_, pt=pt, cn=cn: e.activation(out=pt.ap[:, 0:cn], in_=S_.ap[:, 0:cn], func=AF.Exp, scale=MLA_SCALE),
                             r=[S_], w=[pt])
                    for hi, (c0, cn) in enumerate(halves):
                        pt = PT[kt % 3][hi]; Oh = O_[hi]
                        P.do(pe, lambda e, Oh=Oh, kt=kt, cn=cn, pt=pt: e.matmul(
                            Oh.ap[:, 0:cn], va3[:, kt, :], pt.ap[:, 0:cn],
                            start=(kt == 0), stop=(kt == nkt - 1)), r=[V_, pt], w=[Oh])
                bo = BO[it % 2]
                for hi, (c0, cn) in enumerate(halves):
                    Oh = O_[hi]
                    P.do(dve, lambda e, Oh=Oh, c0=c0, cn=cn: e.reciprocal(out=RCP.ap[64:128, c0:c0 + cn], in_=Oh.ap[64:128, 0:cn]), r=[Oh], w=[RCP])
                    P.do(dve, lambda e, Oh=Oh, bo=bo, c0=c0, cn=cn: e.tensor_tensor(out=bo.ap[0:64, c0:c0 + cn], in0=Oh.ap[0:64, 0:cn], in1=RCP.ap[64:128, c0:c0 + cn], op=ALU.mult),
                         r=[Oh, RCP], w=[bo])
                P.store(sp, bo, bo.ap[0:64, 0:nq], BT[h * 64:(h + 1) * 64, q0:q0 + nq])
                it += 1

    def mix0_out(src, dst, TT=512):
        A.reset()
        WO = A.bf16(KC * D, "wo"); wo3 = v3(WO.ap, D)
        load_w_bf16(WO, wo3, e_wout, KC)
        XB = [A.f32(KC * TT, "xa"), A.f32(KC * TT, "xb")]
        GBt = [A.bf16(4 * TT, "gb0"), A.bf16(4 * TT, "gb1")]
        CVw = [A.f32(4 * (TT + 2), "cvw0"), A.f32(4 * (TT + 2), "cvw1")]
        BTt = [A.bf16(4 * TT, "bt0"), A.bf16(4 * TT, "bt1")]
        ACC = [A.f32(TT, "acc0"), A.f32(TT, "acc1")]
        AT = A.bf16(4 * TT, "at")
        cw3 = v3(CONVW.ap, 3)
        for it, (t0, n, s) in enumerate(tiles_of(TT, True)):
            X = XB[it % 2]; gb = GBt[it % 2]; cvw = CVw[it % 2]; bt = BTt[it % 2]
            x3 = v3(X.ap[:, 0:KC * n], n)
            gb3 = v3(gb.ap[:, 0:4 * n], n); cv3 = v3(cvw.ap[:, 0:4 * (n + 2)], n + 2); bt3 = v3(bt.ap[:, 0:4 * n], n)
            at3 = v3(AT.ap[:, 0:4 * n], n)
            P.load(sp, X, x3, src.rearrange("(k p) t -> p k t", p=128)[:, :, t0:t0 + n])
            P.load(sp, gb, gb3, GB.rearrange("(c p) t -> p c t", p=128)[:, :, t0:t0 + n])
            cc = cv_col(t0)
            P.load(sp, cvw, cv3, CV.rearrange("(c p) t -> p c t", p=128)[:, :, cc - 1:cc + n + 1])
            P.load(sp, bt, bt3, BT.rearrange("(c p) t -> p c t", p=128)[:, :, t0:t0 + n])
            for c in range(4):
                acc = ACC[c % 2]
                P.do(dve, lambda e, c=c, acc=acc: e.tensor_scalar(out=acc.ap[:, 0:n], in0=cv3[:, c, 0:n], scalar1=cw3[:, c, 0:1], scalar2=None, op0=ALU.mult),
                     r=[cvw, CONVW], w=[acc])
                P.do(dve, lambda e, c=c, acc=acc: e.scalar_tensor_tensor(out=acc.ap[:, 0:n], in0=cv3[:, c, 1:n + 1], scalar=cw3[:, c, 1:2], in1=acc.ap[:, 0:n],
                                                                          op0=ALU.mult, op1=ALU.add), r=[cvw, CONVW, acc], w=[acc])
                P.do(dve, lambda e, c=c, acc=acc: e.scalar_tensor_tensor(out=acc.ap[:, 0:n], in0=cv3[:, c, 2:n + 2], scalar=cw3[:, c, 2:3], in1=acc.ap[:, 0:n],
                                                                          op0=ALU.mult, op1=ALU.add), r=[cvw, CONVW, acc], w=[acc])
                P.do(dve, lambda e, c=c, acc=acc: e.tensor_tensor(out=at3[:, c, :], in0=acc.ap[:, 0:n], in1=gb3[:, c, :], op=ALU.mult),
                     r=[acc, gb], w=[AT])
            g5 = hgvec(s, 1)
            for m in range(KC):
                po = nb()
                for c in range(8):
                    rhs = at3[:, c, :] if c < 4 else bt3[:, c - 4, :]
                    P.do(pe, lambda e, po=po, c=c, m=m, rhs=rhs: e.matmul(po.ap[:, 0:n], wo3[:, c, m * 128:(m + 1) * 128], rhs,
                                                                         start=(c == 0), stop=(c == 7)),
                         r=[WO, AT, bt], w=[po], inc=(c == 7))
                P.do(dve, lambda e, po=po, m=m: e.scalar_tensor_tensor(out=x3[:, m, :], in0=po.ap[:, 0:n], scalar=g5[:, m:m + 1],
                                                                       in1=x3[:, m, :], op0=ALU.mult, op1=ALU.add),
                     r=[po, HG, X], w=[X])
            P.store(sp, X, x3, dst.rearrange("(k p) t -> p k t", p=128)[:, :, t0:t0 + n])

    def mix1_dir(src, dst, direction, TT=256):
        bwd = direction == 1
        A.reset()
        ncols = 3
        WIN = A.bf16(KC * ncols * D, "owin"); win3 = v3(WIN.ap, ncols * D)
        for blk, srcblk in enumerate([0, 1, 2 + direction]):
            for kc in range(KC):
                P.load(pool, WIN, win3[:, kc, blk * D:(blk + 1) * D], o_win[kc * 128:(kc + 1) * 128, srcblk * D:(srcblk + 1) * D])
        NCH = TT // 64
        XB = [A.f32(KC * TT, "xa"), A.f32(KC * TT, "xb")]
        U2 = [A.bf16(KC * TT, "u0"), A.bf16(KC * TT, "u1")]
        SQn = A.bf16(KC * TT, "sqn")
        RS = A.f32(TT, "rs")
        TMP = [A.f32(TT, "t0"), A.f32(TT, "t1")]
        TMP2 = [A.f32(TT, "t2"), A.f32(TT, "t3")]
        QF2 = [A.f32(8 * TT, "qf0"), A.f32(8 * TT, "qf1")]; FF2 = [A.f32(8 * TT, "ff0"), A.f32(8 * TT, "ff1")]
        LF = A.f32(8 * TT, "lf"); BB = A.f32(8 * TT, "bb"); EE = A.f32(8 * TT, "ee")
        QT_ = A.bf16(8 * TT, "qt"); KTL = A.bf16(8 * TT, "ktl"); KE = A.bf16(8 * TT, "ke")
        VV2 = [A.bf16(NCH * 8 * 128, "vv0"), A.bf16(NCH * 8 * 128, "vv1")]
        OO = A.f32(8 * TT, "oo")
        DEC = A.f32(8 * NCH, "dec")
        M64 = A.f32(8 * TT, "m64")
        ST = A.f32(8 * 128, "st"); STBS = [A.bf16(8 * 128, "stb0"), A.bf16(8 * 128, "stb1")]
        KET = [A.bf16(1024, f"ket{i}") for i in range(2)]
        ATM = [A.bf16(512, f"atm{i}") for i in range(2)]
        stb_i = [0]; k_it = [0]
        st3 = v3(ST.ap, 128)
        P.do(dve, lambda e: e.memset(M64.ap, 1.0), w=[M64])
        P.do(dve, lambda e: e.memset(v3(M64.ap, 64)[:, :, 0:1], 0.0), w=[M64])
        mask_ap = MASK.ap[:, 64:128] if bwd else MASK.ap[:, 0:64]
        tl = tiles_of(TT, True)
        ctx_tiles = [t for t in tl if t[2] == 1]
        lat_tiles = [t for t in tl if t[2] == 0]
        order = (ctx_tiles + lat_tiles) if not bwd else lat_tiles[::-1]
        if not bwd:
            P.do(dve, lambda e: e.memset(ST.ap, 0.0), w=[ST])
            P.do(dve, lambda e: e.memset(STBS[0].ap, 0.0), w=[STBS[0]])
        else:
            P.load(sp, ST, ST.ap, SGA[0:128, :])
            P.load(sp, OO, OO.ap[:, 0:1024], SGA[128:256, :])
            P.do(dve, lambda e: e.tensor_scalar(out=ST.ap, in0=ST.ap, scalar1=SEL.ap[:, 0:1], scalar2=None, op0=ALU.mult), r=[ST, SEL], w=[ST])
            P.do(dve, lambda e: e.scalar_tensor_tensor(out=ST.ap, in0=OO.ap[:, 0:1024], scalar=SEL.ap[:, 1:2], in1=ST.ap, op0=ALU.mult, op1=ALU.add),
                 r=[OO, SEL, ST], w=[ST])
            P.do(pool, lambda e: e.tensor_copy(out=STBS[0].ap, in_=ST.ap), r=[ST], w=[STBS[0]])
        lbv = v3(LB.ap, 8)[:, direction, :]
        omlv = v3(OML.ap, 8)[:, direction, :]
        tctx = {}

        def stageA(it, gen=None):
            (t0, n, s) = order[it]
            X = XB[it % 2]; U = U2[it % 2]; QF = QF2[it % 2]; FF = FF2[it % 2]; VV = VV2[it % 2]
            x3, u3 = norm_mod(X, U, SQn, RS, TMP, n, s, 1, src, t0, lnexp=True)
            qf3 = v3(QF.ap, n); ff3 = v3(FF.ap, n)
            vv4 = VV.ap.rearrange("p (c h d) -> p c h d", h=8, d=128)
            tctx[it] = (x3, u3)
            for h in range(8):
                pq = nb()
                for kc in range(KC):
                    P.do(pe, lambda e, kc=kc, h=h, pq=pq: e.matmul(pq.ap[:, 0:n], win3[:, kc, h * 128:(h + 1) * 128], u3[:, kc, :],
                                                                   start=(kc == 0), stop=(kc == KC - 1)), r=[WIN, U], w=[pq], inc=(kc == KC - 1))
                P.do(act, lambda e, pq=pq, h=h: e.activation(out=qf3[:, h, :], in_=pq.ap[:, 0:n], func=AF.Copy), r=[pq], w=[QF])
                pz = nb()
                for kc in range(KC):
                    P.do(pe, lambda e, kc=kc, h=h, pz=pz: e.matmul(pz.ap[:, 0:n], win3[:, kc, 2 * D + h * 128:2 * D + (h + 1) * 128], u3[:, kc, :],
                                                                   start=(kc == 0), stop=(kc == KC - 1)), r=[WIN, U], w=[pz], inc=(kc == KC - 1))
                P.do(act, lambda e, pz=pz, h=h: e.activation(out=ff3[:, h, :], in_=pz.ap[:, 0:n], func=AF.Exp, scale=-1.0), r=[pz], w=[FF])
                if h % 4 == 3:
                    hg = h // 4
                    for c in range(n // 64):
                        pvv = nb()
                        for kc in range(KC):
                            P.do(pe, lambda e, kc=kc, hg=hg, c=c, pvv=pvv: e.matmul(pvv.ap[0:64, 0:512], u3[:, kc, c * 64:(c + 1) * 64],
                                                                                    win3[:, kc, D + hg * 512:D + (hg + 1) * 512],
                                                                                    start=(kc == 0), stop=(kc == KC - 1)),
                                 r=[WIN, U], w=[pvv], inc=(kc == KC - 1))
                        P.do(act, lambda e, pvv=pvv, c=c, hg=hg: e.activation(out=vv4[0:64, c, hg * 4:(hg + 1) * 4, :],
                                                                             in_=pvv.ap[0:64, 0:512].rearrange("p (h d) -> p h d", d=128), func=AF.Copy),
                             r=[pvv], w=[VV])
                if gen is not None:
                    for _ in range(3):
                        next(gen, None)

        def stageB(it):
            (t0, n, s) = order[it]
            QF = QF2[it % 2]; FF = FF2[it % 2]
            ff3 = v3(FF.ap, n)
            NN = 8 * n
            P.do(dve, lambda e: e.tensor_scalar(out=FF.ap[:, 0:NN], in0=FF.ap[:, 0:NN], scalar1=1.0, scalar2=None, op0=ALU.add), r=[FF], w=[FF])
            yield
            P.do(dve, lambda e: e.reciprocal(out=FF.ap[:, 0:NN], in_=FF.ap[:, 0:NN]), r=[FF], w=[FF])
            yield
            P.do(dve, lambda e: e.tensor_tensor(out=ff3, in0=ff3, in1=omlv.unsqueeze(2).broadcast_to([128, 8, n]), op=ALU.mult), r=[FF, OML], w=[FF])
            yield
            P.do(dve, lambda e: e.tensor_tensor(out=ff3, in0=ff3, in1=lbv.unsqueeze(2).broadcast_to([128, 8, n]), op=ALU.add), r=[FF, LB], w=[FF])
            yield
            P.do(act, lambda e: e.activation(out=LF.ap[:, 0:NN], in_=FF.ap[:, 0:NN], func=AF.Ln), r=[FF], w=[LF])
            yield
            P.do(dve, lambda e: e.tensor_scalar(out=FF.ap[:, 0:NN], in0=FF.ap[:, 0:NN], scalar1=-1.0, scalar2=1.0, op0=ALU.mult, op1=ALU.add), r=[FF], w=[FF])
            yield
            P.do(dve, lambda e: e.tensor_tensor_scan(out=BB.ap[:, 0:NN], data0=M64.ap[:, 0:NN], data1=LF.ap[:, 0:NN], initial=0.0,
                                                     op0=ALU.mult, op1=ALU.add), r=[M64, LF], w=[BB])
            yield
            bb4 = v3(BB.ap[:, 0:NN], 64)
            tot = bb4[:, :, 63:64]
            nchk = NN // 64
            totb = tot.broadcast_to([128, nchk, 64])
            P.do(act, lambda e: e.activation(out=DEC.ap[:, 0:nchk], in_=bb4[:, :, 63], func=AF.Exp), r=[BB], w=[DEC])
            yield
            if bwd:
                P.do(dve, lambda e: e.tensor_tensor(out=LF.ap[:, 0:NN], in0=LF.ap[:, 0:NN], in1=BB.ap[:, 0:NN], op=ALU.subtract), r=[LF, BB], w=[LF])
                yield
                P.do(dve, lambda e: e.tensor_tensor(out=v3(LF.ap[:, 0:NN], 64), in0=v3(LF.ap[:, 0:NN], 64), in1=totb, op=ALU.add), r=[LF, BB], w=[LF])
                yield
                Bcur = LF
            else:
                Bcur = BB
            P.do(act, lambda e: e.activation(out=EE.ap[:, 0:NN], in_=Bcur.ap[:, 0:NN], func=AF.Exp), r=[Bcur], w=[EE])
            yield
            P.do(dve, lambda e: e.scalar_tensor_tensor(out=QT_.ap[:, 0:NN], in0=QF.ap[:, 0:NN], scalar=HGRN_SCALE, in1=EE.ap[:, 0:NN],
                                                       op0=ALU.mult, op1=ALU.mult), r=[QF, EE], w=[QT_])
            yield
            P.do(act, lambda e: e.activation(out=EE.ap[:, 0:NN], in_=Bcur.ap[:, 0:NN], func=AF.Exp, scale=-1.0), r=[Bcur], w=[EE])
            yield
            P.do(pool, lambda e: e.tensor_tensor(out=KTL.ap[:, 0:NN], in0=FF.ap[:, 0:NN], in1=EE.ap[:, 0:NN], op=ALU.mult), r=[FF, EE], w=[KTL])
            yield
            P.do(pool, lambda e: e.tensor_tensor(out=v3(QF.ap[:, 0:NN], 64), in0=totb, in1=v3(Bcur.ap[:, 0:NN], 64), op=ALU.subtract), r=[Bcur, BB], w=[QF])
            yield
            P.do(act, lambda e: e.activation(out=EE.ap[:, 0:NN], in_=QF.ap[:, 0:NN], func=AF.Exp), r=[QF], w=[EE])
            yield
            P.do(dve, lambda e: e.tensor_tensor(out=KE.ap[:, 0:NN], in0=FF.ap[:, 0:NN], in1=EE.ap[:, 0:NN], op=ALU.mult), r=[FF, EE], w=[KE])
            yield

        def stageC(it):
            (t0, n, s) = order[it]
            X = XB[it % 2]; U = U2[it % 2]; VV = VV2[it % 2]; OFt = QF2[it % 2]
            x3, u3 = tctx.pop(it)
            NN = 8 * n
            nchk = NN // 64
            qt3 = v3(QT_.ap, n); kt3 = v3(KTL.ap, n); ke3 = v3(KE.ap, n); oo3 = v3(OO.ap, n)
            vv4 = VV.ap.rearrange("p (c h d) -> p c h d", h=8, d=128)
            dec3 = v3(DEC.ap[:, 0:nchk], n // 64)
            chunks = list(range(n // 64))
            if bwd:
                chunks = chunks[::-1]
            kets, atms = {}, {}

            def stage1(ci):
                c = chunks[ci]
                cs = slice(c * 64, (c + 1) * 64)
                ket = KET[ci % 2]; atm = ATM[ci % 2]
                kets[c], atms[c] = ket, atm
                ptr = nb()
                ptb = ptr.ap.bitcast(BF16)
                for h in range(8):
                    P.do(pe, lambda e, h=h, cs=cs, ptb=ptb: e.transpose(ptb[0:64, h * 128:(h + 1) * 128], ke3[:, h, cs], IDB.ap),
                         r=[KE, IDB], w=[ptr], inc=(h == 7))
                P.do(act, lambda e, ptb=ptb, ket=ket: e.activation(out=ket.ap[0:64, 0:1024], in_=ptb[0:64, 0:1024], func=AF.Copy), r=[ptr], w=[ket])
                if s == 0:
                    pat = nb()
                    for h in range(8):
                        P.do(pe, lambda e, pat=pat, h=h, cs=cs: e.matmul(pat.ap[0:64, h * 64:(h + 1) * 64], kt3[:, h, cs], qt3[:, h, cs], start=True, stop=True),
                             r=[KTL, QT_], w=[pat], inc=(h == 7))
                    P.do(dve, lambda e, pat=pat, atm=atm: e.tensor_tensor(out=v3(atm.ap[0:64, 0:512], 64), in0=v3(pat.ap[0:64, 0:512], 64),
                                                                          in1=mask_ap.unsqueeze(1).broadcast_to([64, 8, 64]), op=ALU.mult),
                         r=[pat, MASK], w=[atm])

            stage1(0)
            if len(chunks) > 1:
                stage1(1)

            def emit_ds(c):
                ket = kets[c]
                pd = [nb(), nb()]
                for h in range(8):
                    P.do(pe, lambda e, pd=pd, ket=ket, c=c, h=h: e.matmul(pd[h // 4].ap[:, (h % 4) * 128:(h % 4 + 1) * 128], ket.ap[0:64, h * 128:(h + 1) * 128],
                                                                          vv4[0:64, c, h, :], start=True, stop=True),
                         r=[ket, VV], w=[pd[h // 4]], inc=(h % 4 == 3))
                return pd

            pds = {chunks[0]: emit_ds(chunks[0])}
            for ci, c in enumerate(chunks):
                cs = slice(c * 64, (c + 1) * 64)
                if ci + 1 < len(chunks):
                    pds[chunks[ci + 1]] = emit_ds(chunks[ci + 1])
                stb_cur = STBS[stb_i[0] % 2]; stb_nxt = STBS[(stb_i[0] + 1) % 2]
                stb_i[0] += 1
                if s == 0:
                    atm = atms[c]
                    po = nb()
                    stc3 = v3(stb_cur.ap, 128)
                    for h in range(8):
                        P.do(pe, lambda e, po=po, atm=atm, c=c, h=h: e.matmul(po.ap[:, h * 64:(h + 1) * 64], vv4[0:64, c, h, :], atm.ap[0:64, h * 64:(h + 1) * 64],
                                                                              start=True, stop=False), r=[VV, atm], w=[po], inc=False)
                        P.do(pe, lambda e, po=po, h=h, cs=cs, stc3=stc3: e.matmul(po.ap[:, h * 64:(h + 1) * 64], stc3[:, h, :], qt3[:, h, cs], start=False, stop=True),
                             r=[stb_cur, QT_], w=[po], inc=(h == 7))
                    P.do(act, lambda e, po=po, cs=cs: e.activation(out=oo3[:, :, cs], in_=v3(po.ap[:, 0:512], 64), func=AF.Copy), r=[po], w=[OO])
                pd = pds[c]
                P.do(dve, lambda e, c=c: e.tensor_tensor(out=st3, in0=st3, in1=dec3[:, :, c:c + 1].broadcast_to([128, 8, 128]), op=ALU.mult),
                     r=[ST, DEC], w=[ST])
                for half in range(2):
                    P.do(dve, lambda e, half=half, pd=pd: e.tensor_tensor(out=ST.ap[:, half * 512:(half + 1) * 512], in0=ST.ap[:, half * 512:(half + 1) * 512],
                                                                          in1=pd[half].ap[:, 0:512], op=ALU.add), r=[ST, pd[half]], w=[ST])
                P.do(pool, lambda e, stb_nxt=stb_nxt: e.tensor_copy(out=stb_nxt.ap, in_=ST.ap), r=[ST], w=[stb_nxt])
                if ci + 2 < len(chunks):
                    stage1(ci + 2)
            if s == 1:
                return
            if not bwd:
                P.store(sp, OO, oo3, OF.rearrange("(h p) t -> p h t", p=128)[:, :, t0 - NCTX:t0 - NCTX + n])
                return
            P.load(sp, OFt, v3(OFt.ap, n), OF.rearrange("(h p) t -> p h t", p=128)[:, :, t0 - NCTX:t0 - NCTX + n])
            P.do(dve, lambda e: e.tensor_tensor(out=OO.ap[:, 0:NN], in0=OO.ap[:, 0:NN], in1=OFt.ap[:, 0:NN], op=ALU.add), r=[OO, OFt], w=[OO])
            P.store(sp, OO, oo3, OF.rearrange("(h p) t -> p h t", p=128)[:, :, t0 - NCTX:t0 - NCTX + n])

        stageA(0)
        for it in range(len(order)):
            gen = stageB(it)
            if it + 1 < len(order):
                stageA(it + 1, gen)
            for _ in gen:
                pass
            stageC(it)
        if not bwd:
            P.store(sp, ST, ST.ap, SX[:, :])

    def mix1_out(src, dst, TT=512):
        A.reset()
        WG = A.bf16(KC * D, "owg"); wg3 = v3(WG.ap, D)
        for kc in range(KC):
            P.load(pool, WG, wg3[:, kc, :], o_win[kc * 128:(kc + 1) * 128, 4 * D:5 * D])
        WO = A.bf16(KC * D, "owo"); wo3 = v3(WO.ap, D)
        load_w_bf16(WO, wo3, o_wout, KC)
        XB = [A.f32(KC * TT, "xa"), A.f32(KC * TT, "xb")]
        U2 = [A.bf16(KC * TT, "u0"), A.bf16(KC * TT, "u1")]; SQ = A.bf16(KC * TT, "sq")
        RS = A.f32(TT, "rs")
        TMP = [A.f32(TT, "t0"), A.f32(TT, "t1")]
        TMP2 = [A.f32(TT, "t2"), A.f32(TT, "t3")]
        OO2 = [A.f32(8 * TT, "oo0"), A.f32(8 * TT, "oo1")]
        KE2 = [A.bf16(8 * TT, "osq0"), A.bf16(8 * TT, "osq1")]; RR = A.bf16(8 * TT, "rr")
        SG2 = [A.f32(8 * TT, "sgall0"), A.f32(8 * TT, "sgall1")]
        tl = tiles_of(TT, False)
        tctx = {}

        def stageP(it, gen=None):
            (t0, n, s) = tl[it]
            X = XB[it % 2]; OO = OO2[it % 2]; U = U2[it % 2]; KE = KE2[it % 2]; SGA_ = SG2[it % 2]
            NN = 8 * n
            oo3 = v3(OO.ap, n); ke3 = v3(KE.ap, n)
            P.load(sp, OO, oo3, OF.rearrange("(h p) t -> p h t", p=128)[:, :, t0 - NCTX:t0 - NCTX + n])
            x3, u3 = norm_mod(X, U, SQ, RS, TMP, n, s, 1, src, t0, lnexp=True)
            P.do(act, lambda e: e.activation(out=KE.ap[:, 0:NN], in_=OO.ap[:, 0:NN], func=AF.Square), r=[OO], w=[KE])
            sg3 = v3(SGA_.ap, n)
            tctx[it] = (x3, u3)
            for h in range(8):
                pg = nb()
                for kc in range(KC):
                    P.do(pe, lambda e, kc=kc, h=h, pg=pg: e.matmul(pg.ap[:, 0:n], wg3[:, kc, h * 128:(h + 1) * 128], u3[:, kc, :],
                                                                   start=(kc == 0), stop=(kc == KC - 1)), r=[WG, U], w=[pg], inc=(kc == KC - 1))
                P.do(act, lambda e, pg=pg, h=h: e.activation(out=sg3[:, h, :], in_=pg.ap[:, 0:n], func=AF.Exp, scale=-1.0), r=[pg], w=[SGA_])
                P.do(dve, lambda e, h=h: e.tensor_scalar(out=sg3[:, h, :], in0=sg3[:, h, :], scalar1=1.0, scalar2=None, op0=ALU.add), r=[SGA_], w=[SGA_])
                P.do(dve, lambda e, h=h: e.reciprocal(out=sg3[:, h, :], in_=sg3[:, h, :]), r=[SGA_], w=[SGA_])
                P.do(dve, lambda e, h=h, pg=pg: e.tensor_tensor(out=sg3[:, h, :], in0=pg.ap[:, 0:n], in1=sg3[:, h, :], op=ALU.mult), r=[SGA_, pg], w=[SGA_])
                if gen is not None:
                    next(gen, None)

        def stageR(it):
            (t0, n, s) = tl[it]
            X = XB[it % 2]; OO = OO2[it % 2]; KE = KE2[it % 2]; SGA_ = SG2[it % 2]
            oo3 = v3(OO.ap, n); ke3 = v3(KE.ap, n); sg3 = v3(SGA_.ap, n); rr3 = v3(RR.ap, n)
            for h in range(8):
                pss = nb()
                P.do(pe, lambda e, pss=pss, h=h: e.matmul(pss.ap[:, 0:n], ONESB.ap, ke3[:, h, :], start=True, stop=True), r=[KE, ONESB], w=[pss])
                T = TMP2[h % 2]
                P.do(act, lambda e, pss=pss, T=T: e.activation(out=T.ap[:, 0:n], in_=pss.ap[:, 0:n], func=AF.Ln, bias=EPSB.ap[:, 0:1], scale=1.0 / 128),
                     r=[pss, EPSB], w=[T])
                P.do(act, lambda e, T=T: e.activation(out=T.ap[:, 0:n], in_=T.ap[:, 0:n], func=AF.Exp, scale=-0.5), r=[T], w=[T])
                P.do(dve, lambda e, T=T, h=h: e.scalar_tensor_tensor(out=oo3[:, h, :], in0=oo3[:, h, :], scalar=GN.ap[:, 0:1], in1=T.ap[:, 0:n],
                                                                     op0=ALU.mult, op1=ALU.mult), r=[OO, GN, T], w=[OO])
                P.do(pool, lambda e, h=h: e.tensor_tensor(out=rr3[:, h, :], in0=oo3[:, h, :], in1=sg3[:, h, :], op=ALU.mult), r=[SGA_, OO], w=[RR])
                yield

        def stageW(it):
            (t0, n, s) = tl[it]
            X = XB[it % 2]
            x3, u3 = tctx.pop(it)
            rr3 = v3(RR.ap, n)
            g5 = hgvec(s, 1)
            for m in range(KC):
                po = nb()
                for h in range(8):
                    P.do(pe, lambda e, po=po, h=h, m=m: e.matmul(po.ap[:, 0:n], wo3[:, h, m * 128:(m + 1) * 128], rr3[:, h, :],
                                                                 start=(h == 0), stop=(h == 7)), r=[WO, RR], w=[po], inc=(h == 7))
                P.do(dve, lambda e, po=po, m=m: e.scalar_tensor_tensor(out=x3[:, m, :], in0=po.ap[:, 0:n], scalar=g5[:, m:m + 1],
                                                                       in1=x3[:, m, :], op0=ALU.mult, op1=ALU.add),
                     r=[po, HG, X], w=[X])
            P.store(sp, X, x3, dst.rearrange("(k p) t -> p k t", p=128)[:, :, t0:t0 + n])

        stageP(0)
        for it in range(len(tl)):
            gen = stageR(it)
            if it + 1 < len(tl):
                stageP(it + 1, gen)
            for _ in gen:
                pass
            stageW(it)


    def mix1_exchange():
        A.reset()
        allgather(SX, SGA)

    steps = []
    steps.append(lambda: ada_phase(0))
    steps.append(lambda: ffn_phase(0, 0, xin, XT))
    steps.append(lambda: mix0_proj(XT))
    steps.append(lambda: mix0_exchange())
    steps.append(lambda: mix0_attn())
    steps.append(lambda: mix0_out(XT, XT))
    steps.append(lambda: ffn_phase(0, 2, XT, XT))
    steps.append(lambda: ada_phase(1))
    steps.append(lambda: ffn_phase(1, 0, XT, XT))
    steps.append(lambda: mix1_dir(XT, XT, 0))
    steps.append(lambda: mix1_exchange())
    steps.append(lambda: mix1_dir(XT, XT, 1))
    steps.append(lambda: mix1_out(XT, XT))
    steps.append(lambda: ffn_phase(1, 2, XT, XT, with_ctx=False))
    steps.append(lambda: final_phase(XT))
    for i, st in enumerate(steps):
        if i >= stop_after:
            break
        st()
    return P.finish()


def _rope_tables(pos):
    row = (pos // 64).astype(np.float32)
    col = (pos % 64).astype(np.float32)
    inv = (10000.0 ** (-np.arange(8, dtype=np.float32) / 8)).astype(np.float32)
    ang = np.stack([row[:, None] * inv, col[:, None] * inv], axis=1)
    cos, sin = np.cos(ang).astype(np.float32), np.sin(ang).astype(np.float32)
    C = np.ones((96, NT), np.float32)
    S = np.zeros((96, NT), np.float32)
    for ax in range(2):
        for half in range(2):
            r0 = 64 + ax * 16 + half * 8
            C[r0:r0 + 8, NCTX:] = cos[:, ax, :].T
            S[r0:r0 + 8, NCTX:] = (-sin[:, ax, :].T) if half == 0 else sin[:, ax, :].T
    return C, S


_PERM = np.concatenate([np.arange(8, 16), np.arange(0, 8), np.arange(24, 32), np.arange(16, 24)])

_NC_CACHE = {}


def _prep_inputs(inp):
    f = lambda a: np.ascontiguousarray(np.asarray(a, dtype=np.float32))
    x, c, ctx, c_ctx = f(inp["x"]), f(inp["c"]), f(inp["ctx"]), f(inp["c_ctx"])
    ewin = f(inp["even_w_in"])[0]
    wuq = f(inp["mla_w_uq"])[0].reshape(256, 8, 96)
    wukv = f(inp["mla_w_ukv"])[0].reshape(128, 8, 128)
    wkr = np.zeros((D, 96), np.float32); wkr[:, 64:] = ewin[:, 1920:1952]
    wkrp = np.zeros((D, 96), np.float32); wkrp[:, 64:] = ewin[:, 1920:1952][:, _PERM]
    wqb = np.zeros((256, 8, 96), np.float32); wqb[:, :, 64:] = wuq[:, :, 64:][:, :, _PERM]
    wkn = np.zeros((128, 8, 96), np.float32); wkn[:, :, :64] = wukv[:, :, :64]
    wv = np.ascontiguousarray(wukv[:, :, 64:]).reshape(128, 512)
    tri = np.triu(np.ones((64, 64), np.float32))
    cmask = np.concatenate([tri, tri.T], axis=1)
    conv = f(inp["even_conv_w"])[0]
    owin = f(inp["odd_w_in"])[0].reshape(D, 5, D)
    olb = f(inp["hgrn_lb_logits"])
    shared = {
        "ada_w": f(inp["ada_w"]), "ada_b": f(inp["ada_b"]), "norm_g": f(inp["norm_g"]),
        "ffn_w1": f(inp["ffn_w1"]), "ffn_w3": f(inp["ffn_w3"]), "ffn_w2": f(inp["ffn_w2"]),
        "e_win": np.ascontiguousarray(ewin[:, :1920]), "e_wkr": wkr, "e_wkrp": wkrp,
        "e_qg": f(inp["mla_q_norm_g"])[0],
        "e_wqa": np.ascontiguousarray(wuq).reshape(256, 768), "e_wqb": wqb.reshape(256, 768),
        "e_kvg": f(inp["mla_kv_norm_g"])[0], "e_wkn": wkn.reshape(128, 768), "e_wv": wv,
        "e_wout": f(inp["even_w_out"])[0],
        "o_gn": f(inp["hgrn_g_norm_g"])[0],
        "o_wout": f(inp["odd_w_out"])[0], "fin_g": f(inp["final_norm_g"]),
        "cmask": cmask, "ident": np.eye(128, dtype=np.float32),
    }
    per_half = []
    for s in range(2):
        pos = np.arange(SEQ_L) if s == 0 else (SEQ - 1 - np.arange(SEQ_L))
        C, S = _rope_tables(pos)
        d1, d2 = (0, 1) if s == 0 else (1, 0)
        owin_p = np.ascontiguousarray(owin[:, [0, 1, 2 + d1, 2 + d2, 4], :]).reshape(D, 5 * D)
        olb_p = np.ascontiguousarray(olb[:, [d1, d2], :])
        conv_p = np.ascontiguousarray(conv if s == 0 else conv[::-1])
        sel = np.zeros((128, 2), np.float32); sel[:, 1 - s] = 1.0
        per_half.append({"ropeC": C, "ropeS": S, "o_win": owin_p, "o_lb": olb_p, "e_conv": conv_p, "selv": sel})
    maps = []
    for core in range(NCORES):
        b, s = core // 2, core % 2
        m = dict(shared)
        m.update(per_half[s])
        if s == 0:
            loc = np.concatenate([ctx[b], x[b, :SEQ_L]], axis=0)
        else:
            loc = np.concatenate([ctx[b][::-1], x[b, SEQ_L:][::-1]], axis=0)
        m["xin"] = np.ascontiguousarray(loc.T)
        cv = np.stack([c[b], c_ctx], axis=1)
        m["cvec"] = np.ascontiguousarray(cv.reshape(KC, 128, 2).transpose(1, 0, 2))
        maps.append(m)
    return maps


def kernel(**inputs):
    maps = _prep_inputs(inputs)
    if "nc" not in _NC_CACHE:
        _NC_CACHE["nc"] = build()
    res = run_bass_kernel_spmd(_NC_CACHE["nc"], maps, core_ids=list(range(NCORES)))
    out = np.empty((NCORES // 2, SEQ, D), np.float32)
    for core in range(NCORES):
        b, s = core // 2, core % 2
        y = res.results[core]["yout"].T
        if s == 0:
            out[b, :SEQ_L] = y
        else:
            out[b, SEQ_L:] = y[::-1]
    return out
```
